# Optimizing a Trainium2 kernel written in Bass

```python
import jax
import jax.numpy as jnp
from jax import lax
import numpy as np

D_MODEL = 1024
BATCH = 8
SEQ = 4096
DEPTH = 1

GRID_W = 64
EPS = 1e-6
ATTN_HEADS = 8
ATTN_KV_HEADS = 2
ATTN_HEAD_DIM = 64
ATTN_WIDTH = ATTN_HEADS * ATTN_HEAD_DIM
KV_WIDTH = ATTN_KV_HEADS * ATTN_HEAD_DIM
Q_BLOCK = 128
ROPE_THETA = 10000.0
MLSTM_HEADS = 4
MLSTM_HEAD_DIM = 128
MLSTM_WIDTH = MLSTM_HEADS * MLSTM_HEAD_DIM
MLSTM_CHUNK = 64
CONV_WIDTH = 5
N_GATES = 4 * MLSTM_HEADS
FORGET_BIAS_LO = 3.0
FORGET_BIAS_HI = 6.0
MIX_WIDTH = ATTN_WIDTH + MLSTM_WIDTH
IN_PROJ_WIDTH = ATTN_WIDTH + 2 * KV_WIDTH + 4 * MLSTM_WIDTH + N_GATES
FFN_HIDDEN = 256 * ((8 * D_MODEL + 3 * 256 - 1) // (3 * 256))

kernel_name = 'hymba_axial_gqa_bi_mlstm_adaln_block'


def rms_norm(x, g):
    xf = x.astype(jnp.float32)
    y = xf * lax.rsqrt(jnp.mean(xf * xf, axis=-1, keepdims=True) + EPS)
    return (y * g.astype(jnp.float32)).astype(x.dtype)


def modulate(h, shift, scale):
    return h * (1.0 + scale[:, None, :]) + shift[:, None, :]


def axial_rope_tables(seq):
    rows = seq // GRID_W
    row = jnp.repeat(jnp.arange(rows, dtype=jnp.float32), GRID_W)
    col = jnp.tile(jnp.arange(GRID_W, dtype=jnp.float32), rows)
    axis_dim = ATTN_HEAD_DIM // 2
    inv_freq = ROPE_THETA ** (-jnp.arange(0, axis_dim, 2, dtype=jnp.float32) / axis_dim)
    ang_row = row[:, None] * inv_freq[None, :]
    ang_col = col[:, None] * inv_freq[None, :]
    return (jnp.cos(ang_row), jnp.sin(ang_row), jnp.cos(ang_col), jnp.sin(ang_col))


def rotate_pairs(x, cos, sin):
    x1, x2 = jnp.split(x, 2, axis=-1)
    cos = cos[:, None, :]
    sin = sin[:, None, :]
    return jnp.concatenate([x1 * cos - x2 * sin, x2 * cos + x1 * sin], axis=-1)


def apply_axial_rope(x, tables):
    cos_r, sin_r, cos_c, sin_c = tables
    x_row, x_col = jnp.split(x.astype(jnp.float32), 2, axis=-1)
    out = jnp.concatenate([rotate_pairs(x_row, cos_r, sin_r), rotate_pairs(x_col, cos_c, sin_c)], axis=-1)
    return out.astype(x.dtype)


def axial_gqa_attention(q, k, v, q_g, k_g, tables):
    B, S, _ = q.shape
    G = ATTN_HEADS // ATTN_KV_HEADS
    q = q.reshape(B, S, ATTN_HEADS, ATTN_HEAD_DIM)
    k = k.reshape(B, S, ATTN_KV_HEADS, ATTN_HEAD_DIM)
    v = v.reshape(B, S, ATTN_KV_HEADS, ATTN_HEAD_DIM)
    q = apply_axial_rope(rms_norm(q, q_g), tables)
    k = apply_axial_rope(rms_norm(k, k_g), tables)
    n_blocks = S // Q_BLOCK
    qb = q.reshape(B, n_blocks, Q_BLOCK, ATTN_KV_HEADS, G, ATTN_HEAD_DIM).transpose(1, 0, 3, 4, 2, 5)
    kt = k.transpose(0, 2, 1, 3)
    vt = v.transpose(0, 2, 1, 3)
    scale = ATTN_HEAD_DIM ** -0.5

    def one_block(q_blk):
        s = jnp.einsum('bkgqd,bksd->bkgqs', q_blk, kt).astype(jnp.float32) * scale
        p = jax.nn.softmax(s, axis=-1).astype(vt.dtype)
        return jnp.einsum('bkgqs,bksd->bkgqd', p, vt)

    o = lax.map(one_block, qb)
    return o.transpose(1, 0, 4, 2, 3, 5).reshape(B, S, ATTN_WIDTH)


def mlstm_chunkwise(q, k, v, i_pre, log_f):
    B, H, S, D = q.shape
    L = MLSTM_CHUNK
    nc = S // L

    def to_chunks(a):
        return jnp.moveaxis(a.reshape((B, H, nc, L) + a.shape[3:]), 2, 0)

    xs = (to_chunks(q), to_chunks(k), to_chunks(v), to_chunks(i_pre), to_chunks(log_f))
    tril = jnp.tril(jnp.ones((L, L), dtype=bool))

    def step(carry, chunk):
        C, n, m = carry
        qc, kc, vc, ic, fc = chunk
        b = jnp.cumsum(fc, axis=-1)
        g = b[..., -1]
        d = b[..., :, None] - b[..., None, :] + ic[..., None, :]
        d = jnp.where(tril, d, -jnp.inf)
        m_inter = b + m[..., None]
        m_t = jnp.maximum(m_inter, jnp.max(d, axis=-1))
        w_inter = jnp.exp(m_inter - m_t)
        p = jnp.exp(d - m_t[..., None])
        s = jnp.einsum('bhtd,bhsd->bhts', qc, kc) * p
        num = w_inter[..., None] * jnp.einsum('bhvd,bhtd->bhtv', C, qc) + jnp.einsum('bhts,bhsv->bhtv', s, vc)
        den = w_inter * jnp.einsum('bhd,bhtd->bht', n, qc) + jnp.sum(s, axis=-1)
        h = num / jnp.maximum(jnp.abs(den), jnp.exp(-m_t))[..., None]
        a = g[..., None] - b + ic
        m_new = jnp.maximum(g + m, jnp.max(a, axis=-1))
        decay = jnp.exp(g + m - m_new)
        w = jnp.exp(a - m_new[..., None])
        C = decay[..., None, None] * C + jnp.einsum('bhs,bhsv,bhsd->bhvd', w, vc, kc)
        n = decay[..., None] * n + jnp.einsum('bhs,bhsd->bhd', w, kc)
        return (C, n, m_new), h

    init = (jnp.zeros((B, H, D, D), jnp.float32), jnp.zeros((B, H, D), jnp.float32), jnp.zeros((B, H), jnp.float32))
    _, h = lax.scan(step, init, xs)
    return jnp.moveaxis(h, 0, 2).reshape(B, H, S, D)


def centred_depthwise_conv(x, w, b):
    ch = x.shape[-1]
    y = lax.conv_general_dilated(x, w[:, None, :].astype(x.dtype), window_strides=(1,),
                                 padding=[(CONV_WIDTH // 2, CONV_WIDTH // 2)],
                                 dimension_numbers=('NWC', 'WIO', 'NWC'), feature_group_count=ch)
    return y + b.astype(x.dtype)


def bidirectional_mlstm(qk_pre, v, o_pre, gate_pre, conv_w, conv_b, gate_b, norm_g):
    B, S, _ = v.shape
    qk = jax.nn.silu(centred_depthwise_conv(qk_pre, conv_w, conv_b))
    q, k = jnp.split(qk, 2, axis=-1)

    def heads(a):
        return a.astype(jnp.float32).reshape(B, S, MLSTM_HEADS, MLSTM_HEAD_DIM).transpose(0, 2, 1, 3)

    qh = heads(q)
    kh = heads(k) * MLSTM_HEAD_DIM ** -0.5
    vh = heads(v)
    g = (gate_pre.astype(jnp.float32) + gate_b.astype(jnp.float32)).reshape(B, S, 4, MLSTM_HEADS)
    g = g.transpose(2, 0, 3, 1)
    i_fw, f_fw, i_bw, f_bw = g[0], g[1], g[2], g[3]
    h_fw = mlstm_chunkwise(qh, kh, vh, i_fw, jax.nn.log_sigmoid(f_fw))
    flip = lambda a: jnp.flip(a, axis=2)
    h_bw = flip(mlstm_chunkwise(flip(qh), flip(kh), flip(vh), flip(i_bw), flip(jax.nn.log_sigmoid(f_bw))))
    h = (h_fw + h_bw).transpose(0, 2, 1, 3)
    h = h * lax.rsqrt(jnp.mean(h * h, axis=-1, keepdims=True) + EPS)
    h = h * norm_g.astype(jnp.float32).reshape(MLSTM_HEADS, MLSTM_HEAD_DIM)
    h = h.reshape(B, S, MLSTM_WIDTH) * jax.nn.sigmoid(o_pre.astype(jnp.float32))
    return h.astype(v.dtype)


def setup_inputs(seed: int = 0) -> dict:
    key = jax.random.key(seed)
    ks = jax.random.split(key, 18)
    nrm = lambda k, shape, s: jax.random.normal(k, shape, jnp.float32) * s
    gain = lambda k, shape: 1.0 + nrm(k, shape, 0.02)
    f_bias = jnp.linspace(FORGET_BIAS_LO, FORGET_BIAS_HI, MLSTM_HEADS, dtype=jnp.float32)
    gate_offset = jnp.zeros((4, MLSTM_HEADS), jnp.float32).at[1].set(f_bias).at[3].set(f_bias)
    gate_b = (nrm(ks[10], (DEPTH, 4, MLSTM_HEADS), 0.1) + gate_offset).reshape(DEPTH, N_GATES)
    return {
        'x': nrm(ks[0], (BATCH, SEQ, D_MODEL), 1.0),
        'c': nrm(ks[1], (BATCH, D_MODEL), 1.0),
        'w_ada': nrm(ks[2], (DEPTH, D_MODEL, 6 * D_MODEL), D_MODEL ** -0.5),
        'b_ada': nrm(ks[3], (DEPTH, 6 * D_MODEL), 0.02),
        'norm_mix_g': gain(ks[4], (DEPTH, D_MODEL)),
        'w_in': nrm(ks[5], (DEPTH, D_MODEL, IN_PROJ_WIDTH), D_MODEL ** -0.5),
        'q_norm_g': gain(ks[6], (DEPTH, ATTN_HEAD_DIM)),
        'k_norm_g': gain(ks[7], (DEPTH, ATTN_HEAD_DIM)),
        'conv_w': nrm(ks[8], (DEPTH, CONV_WIDTH, 2 * MLSTM_WIDTH), CONV_WIDTH ** -0.5),
        'conv_b': nrm(ks[9], (DEPTH, 2 * MLSTM_WIDTH), 0.02),
        'gate_b': gate_b,
        'mlstm_norm_g': gain(ks[11], (DEPTH, MLSTM_WIDTH)),
        'w_out': nrm(ks[12], (DEPTH, MIX_WIDTH, D_MODEL), MIX_WIDTH ** -0.5),
        'norm_ffn_g': gain(ks[13], (DEPTH, D_MODEL)),
        'w_gate': nrm(ks[14], (DEPTH, D_MODEL, FFN_HIDDEN), D_MODEL ** -0.5),
        'w_up': nrm(ks[15], (DEPTH, D_MODEL, FFN_HIDDEN), D_MODEL ** -0.5),
        'w_down': nrm(ks[16], (DEPTH, FFN_HIDDEN, D_MODEL), FFN_HIDDEN ** -0.5),
        'final_norm_g': gain(ks[17], (D_MODEL,)),
    }


def reference(x, c, w_ada, b_ada, norm_mix_g, w_in, q_norm_g, k_norm_g, conv_w, conv_b, gate_b,
              mlstm_norm_g, w_out, norm_ffn_g, w_gate, w_up, w_down, final_norm_g):
    S = x.shape[1]
    tables = axial_rope_tables(S)
    cond = jax.nn.silu(c)
    splits = [ATTN_WIDTH,
              ATTN_WIDTH + KV_WIDTH,
              ATTN_WIDTH + 2 * KV_WIDTH,
              ATTN_WIDTH + 2 * KV_WIDTH + 2 * MLSTM_WIDTH,
              ATTN_WIDTH + 2 * KV_WIDTH + 3 * MLSTM_WIDTH,
              ATTN_WIDTH + 2 * KV_WIDTH + 4 * MLSTM_WIDTH]
    for l in range(DEPTH):
        mod = cond @ w_ada[l] + b_ada[l]
        sh1, sc1, g1, sh2, sc2, g2 = jnp.split(mod, 6, axis=-1)
        h = modulate(rms_norm(x, norm_mix_g[l]), sh1, sc1)
        proj = h @ w_in[l]
        q_a, k_a, v_a, qk_m, v_m, o_m, gates = jnp.split(proj, splits, axis=-1)
        y_attn = axial_gqa_attention(q_a, k_a, v_a, q_norm_g[l], k_norm_g[l], tables)
        y_mem = bidirectional_mlstm(qk_m, v_m, o_m, gates, conv_w[l], conv_b[l], gate_b[l], mlstm_norm_g[l])
        mix = jnp.concatenate([y_attn, y_mem], axis=-1) @ w_out[l]
        x = x + g1[:, None, :] * mix
        h = modulate(rms_norm(x, norm_ffn_g[l]), sh2, sc2)
        ffn = (jax.nn.silu(h @ w_gate[l]) * (h @ w_up[l])) @ w_down[l]
        x = x + g2[:, None, :] * ffn
    return rms_norm(x, final_norm_g)
```

```python
import math
from contextlib import ExitStack

import numpy as np
import concourse.bass as bass
import concourse.mybir as mybir
from concourse.bass_utils import run_bass_kernel_spmd

F32 = mybir.dt.float32
BF16 = mybir.dt.bfloat16
AF = mybir.ActivationFunctionType
ALU = mybir.AluOpType
AX = mybir.AxisListType

S = 4096
D = 1024
NT = 32
NB = 8
NJ = 22
EPS = 1e-6
DEBUG = False
STOP_AFTER = 99
INTERLEAVE = False
PACKPV = False


class Sem:
    def __init__(self, h):
        self.h = h
        self.val = 0


class Buf:
    __slots__ = ("w", "r", "excl")

    def __init__(self, excl=False):
        self.w = None
        self.r = {}
        self.excl = excl


class Eng:
    def __init__(self, e, sem, selfsync=True):
        self.e = e
        self.sem = sem
        self.selfsync = selfsync
        self.waited = {}

    def wait_tok(self, tok):
        if tok is None:
            return
        sem, val = tok
        if sem is self.sem and not self.selfsync:
            return
        if self.waited.get(id(sem), 0) >= val:
            return
        self.e.wait_ge(sem.h, val)
        self.waited[id(sem)] = val


class DmaQ:
    def __init__(self, eng, sems):
        self.eng = eng
        self.sems = sems
        self.i = 0


class Ctx:
    def __init__(self, nc, es):
        self.nc = nc
        self.es = es
        self.nsem = 0
        self.all_sems = []
        self.PE = Eng(nc.tensor, self.new_sem("pe"), selfsync=False)
        self.ACT = Eng(nc.scalar, self.new_sem("act"))
        self.DVE = Eng(nc.vector, self.new_sem("dve"))
        self.POOL = Eng(nc.gpsimd, self.new_sem("pool"))
        self.SP = Eng(nc.sync, self.new_sem("sp"))
        self.engs = [self.PE, self.ACT, self.DVE, self.POOL, self.SP]
        self.qsp = DmaQ(self.SP, [self.new_sem("dsp%d" % i) for i in range(40)])
        self.qpool = DmaQ(self.POOL, [self.new_sem("dpl%d" % i) for i in range(40)])
        self.bank_rr = 0

    def new_sem(self, name):
        s = Sem(self.es.enter_context(self.nc.semaphore(name)))
        self.all_sems.append(s)
        return s

    def _deps(self, E, r, w):
        for b in r:
            E.wait_tok(b.w)
        for b in w:
            E.wait_tok(b.w)
            for tok in b.r.values():
                E.wait_tok(tok)

    def _mark(self, tok, r, w):
        for b in r:
            b.r[id(tok[0])] = tok
        for b in w:
            b.w = tok
            b.r = {}

    def op(self, E, fn, r=(), w=()):
        ex = [b for b in r if b.excl]
        if ex:
            r = [b for b in r if not b.excl]
            w = list(w) + ex
        self._deps(E, r, w)
        inst = fn()
        E.sem.val += 1
        inst.then_inc(E.sem.h, 1)
        self._mark((E.sem, E.sem.val), r, w)

    def dma(self, Q, out, in_, r=(), w=()):
        s = Q.sems[Q.i % len(Q.sems)]
        Q.i += 1
        E = Q.eng
        if s.val > 0:
            E.wait_tok((s, s.val))
        self._deps(E, r, w)
        inst = E.e.dma_start(out=out, in_=in_)
        s.val += 16
        inst.then_inc(s.h, 16)
        self._mark((s, s.val), r, w)

    def barrier(self):
        for E in self.engs:
            for s in self.all_sems:
                if s.val > 0:
                    E.wait_tok((s, s.val))


def build():
    nc = bass.Bass("TRN2", target_bir_lowering=False)

    def din(name, shape, dt=F32):
        return nc.dram_tensor(name, list(shape), dt, kind="ExternalInput").ap()

    x_d = din("x", [S, D])
    cT_d = din("cT", [128, 8])
    wada_d = din("w_ada", [12, 128, 8, 512])
    bada_d = din("b_adaT", [128, 48])
    gmix_d = din("gmixT", [128, 8])
    gffn_d = din("gffnT", [128, 8])
    win_d = din("w_in", [128, 8, 2832])
    g10_d = din("g10", [128, 640])
    cos_d = din("ropecos", [128, NT, 32])
    sin_d = din("ropesin", [128, NT, 32])
    convw_d = din("convwT", [128, 8, 5])
    convb_d = din("convbT", [128, 8])
    gateb_d = din("gateb", [128, 16])
    mng_d = din("mngT", [128, 4])
    wout_d = din("w_out", [128, 8, 1024])
    wg_d = din("w_gate", [NJ * 128, 1024])
    wu_d = din("w_up", [NJ * 128, 1024])
    wd_d = din("w_down", [128, NJ, 1024])
    fng_d = din("fngb", [128, 1024])
    ident_d = din("ident", [128, 128])
    triu_d = din("triu", [128, 128])
    tril_d = din("tril", [128, 128])
    out_d = nc.dram_tensor("out", [S, D], F32, kind="ExternalOutput").ap()
    mix_s = nc.dram_tensor("mix_s", [D, S], BF16).ap()
    wg_s = nc.dram_tensor("wg_s", [NJ * 128, 1024], BF16).ap()
    wu_s = nc.dram_tensor("wu_s", [NJ * 128, 1024], BF16).ap()
    hT_s = nc.dram_tensor("hT_s", [NB, 128, 8, 512], BF16).ap()
    dbg = {}
    if DEBUG:
        dbg["hT"] = nc.dram_tensor("dbg_hT", [128, 8, S], BF16, kind="ExternalOutput").ap()
        dbg["QT"] = nc.dram_tensor("dbg_QT", [128, 4, S], BF16, kind="ExternalOutput").ap()
        dbg["KT"] = nc.dram_tensor("dbg_KT", [128, 2, S], BF16, kind="ExternalOutput").ap()
        dbg["mix"] = nc.dram_tensor("dbg_mix", [D, S], BF16, kind="ExternalOutput").ap()
        dbg["modT"] = nc.dram_tensor("dbg_modT", [128, 48], F32, kind="ExternalOutput").ap()

    with ExitStack() as es:
        cx = Ctx(nc, es)
        PE, ACT, DVE, POOL, SP = cx.PE, cx.ACT, cx.DVE, cx.POOL, cx.SP
        op, dma = cx.op, cx.dma
        QS, QP = cx.qsp, cx.qpool

        def sb(stack, name, shape, dt):
            return stack.enter_context(nc.sbuf_tensor("s_" + name, list(shape), dt))

        psall = es.enter_context(nc.psum_tensor("psall", [128, 4096], F32))
        psall_b = psall.bitcast(BF16)
        ps = [psall[:, i * 512:(i + 1) * 512] for i in range(8)]
        psb = [psall_b[:, i * 1024:(i + 1) * 1024] for i in range(8)]
        psB = [Buf(excl=True) for _ in range(8)]

        cx.banks = list(range(8))

        def nbank():
            i = cx.banks[cx.bank_rr % len(cx.banks)]
            cx.bank_rr += 1
            return i

        ident = sb(es, "ident", [128, 128], BF16)
        mask2 = sb(es, "mask2", [128, 2, 128], BF16)
        masku = mask2[:, 0, :]
        maskl = mask2[:, 1, :]
        triu = sb(es, "triu_f", [128, 128], F32)
        tril = sb(es, "tril_f", [128, 128], F32)
        ones_f = sb(es, "ones_f", [128, 128], F32)
        modT = sb(es, "modT", [128, 48], F32)
        scale1 = sb(es, "scale1", [128, 8], F32)
        scale2 = sb(es, "scale2", [128, 8], F32)
        ident_f = sb(es, "ident_f", [128, 128], F32)
        GT = sb(es, "GT", [128, NT, 16], F32)
        lns = sb(es, "lns", [128, 1], F32)
        epsT = sb(es, "epsT", [128, 1], F32)
        cTt = sb(es, "cTt", [128, 8], F32)
        cond = sb(es, "cond", [128, 8], BF16)
        gmixT = sb(es, "gmixT", [128, 8], F32)
        gffnT = sb(es, "gffnT", [128, 8], F32)
        tmp8 = sb(es, "tmp8", [128, 8], F32)
        badaT = sb(es, "badaT", [128, 48], F32)
        mrow = [sb(es, "mrow%d" % i, [1, 512], F32) for i in range(2)]
        cB = Buf()
        modB = Buf()
        GTB = Buf()
        mixB = Buf()
        wsB = Buf()

        dma(QP, ident[:], ident_d, w=[cB])
        dma(QP, mask2[:, 0, :], triu_d, w=[cB])
        dma(QP, mask2[:, 1, :], tril_d, w=[cB])
        dma(QS, triu[:], triu_d, w=[cB])
        dma(QS, tril[:], tril_d, w=[cB])
        dma(QS, ident_f[:], ident_d, w=[cB])
        op(DVE, lambda: nc.vector.memset(ones_f[:], 1.0), w=[cB])
        op(DVE, lambda: nc.vector.memset(lns[:], math.log(128.0 ** -0.5)), w=[cB])
        op(DVE, lambda: nc.vector.memset(epsT[:], EPS), w=[cB])

        with ExitStack() as pm:
            qs = ExitStack()
            QT = sb(qs, "QT", [128, 4, S], BF16)
            KT = sb(qs, "KT", [128, 2, S], BF16)
            Vaug = sb(qs, "Vaug", [128, NT, 2, 65], BF16)
            QTB, KTB, VB = Buf(), Buf(), Buf()
            hs = ExitStack()
            hT = sb(hs, "hT", [128, 8, S], BF16)
            hTsB = Buf()
            hTB = [Buf() for _ in range(NT)]
            p0 = ExitStack()
            mrB = [Buf(), Buf()]
            NWA = 4
            wa = [sb(p0, "wa%d" % i, [128, 8, 512], BF16) for i in range(NWA)]
            waB = [Buf() for _ in range(NWA)]
            dma(QS, cTt[:], cT_d, w=[cB])
            dma(QS, gmixT[:], gmix_d, w=[cB])
            dma(QS, gffnT[:], gffn_d, w=[cB])
            dma(QS, badaT[:], bada_d, w=[cB])
            op(ACT, lambda: nc.scalar.activation(out=cond[:], in_=cTt[:], func=AF.Silu), r=[cB], w=[modB])
            for n in range(NWA):
                dma(QP, wa[n][:], wada_d[n], w=[waB[n]])

            def ada_block(n, wbuf=None, wbufB=None, defer=False):
                if wbuf is None:
                    wbuf, wbufB = wa[n % NWA], waB[n % NWA]
                bk = nbank()
                for k in range(8):
                    op(PE, lambda k=k: nc.tensor.matmul(ps[bk][0:1, :], cond[:, k:k + 1], wbuf[:, k, :], start=(k == 0), stop=(k == 7)),
                       r=[modB, wbufB], w=[psB[bk]])
                def fin():
                    m = n % 2
                    op(DVE, lambda: nc.vector.tensor_copy(mrow[m][0:1, :], ps[bk][0:1, :]), r=[psB[bk]], w=[mrB[m]])
                    bk2 = nbank()
                    for q in range(4):
                        op(PE, lambda q=q: nc.tensor.matmul(ps[bk2][:, q:q + 1], mrow[m][0:1, q * 128:(q + 1) * 128], ones_f[0:1, 0:1], start=True, stop=True),
                           r=[mrB[m], cB], w=[psB[bk2]])
                    op(DVE, lambda: nc.vector.tensor_tensor(out=modT[:, 4 * n:4 * n + 4], in0=ps[bk2][:, 0:4], in1=badaT[:, 4 * n:4 * n + 4], op=ALU.add),
                       r=[psB[bk2], cB], w=[modB])
                if defer:
                    return fin
                fin()

            def mk_scale(dst, c0, g):
                op(DVE, lambda: nc.vector.tensor_scalar(tmp8[:], modT[:, c0:c0 + 8], 1.0, None, op0=ALU.add), r=[modB], w=[modB])
                op(DVE, lambda: nc.vector.tensor_tensor(out=dst[:], in0=tmp8[:], in1=g[:], op=ALU.mult), r=[modB, cB], w=[modB])

            for fn_ in [ada_block(n, defer=True) for n in range(4)]:
                fn_()
            mk_scale(scale1, 8, gmixT)
            sh1, sh2 = modT[:, 0:8], modT[:, 24:32]

            def phase0b():
                mk_scale(scale2, 32, gffnT)
                if DEBUG:
                    dma(QS, dbg["modT"], modT[:], r=[modB])

            def norm_to_T(stack_bufs, xsrc, xsrcB, ssT, rstdT, col, xn, xnB, junk, junkB, dstT, dstB, dst_cols, scaleT, biasT, parity, xn_on_act=False):
                op(ACT, lambda: nc.scalar.activation(out=junk[:], in_=xsrc, func=AF.Square, accum_out=ssT[:, col:col + 1]),
                   r=[xsrcB], w=[junkB, stack_bufs])
                op(ACT, lambda: nc.scalar.activation(out=rstdT[:, col:col + 1], in_=ssT[:, col:col + 1], func=AF.Ln, scale=1.0 / D, bias=epsT[:, 0:1]), r=[stack_bufs, cB], w=[stack_bufs])
                op(ACT, lambda: nc.scalar.activation(out=rstdT[:, col:col + 1], in_=rstdT[:, col:col + 1], func=AF.Exp, scale=-0.5), r=[stack_bufs], w=[stack_bufs])
                if xn_on_act:
                    op(ACT, lambda: nc.scalar.activation(out=xn[:], in_=xsrc, func=AF.Identity, scale=rstdT[:, col:col + 1]),
                       r=[xsrcB, stack_bufs], w=[xnB])
                else:
                    op(DVE, lambda: nc.vector.tensor_scalar(xn[:], xsrc, rstdT[:, col:col + 1], None, op0=ALU.mult),
                       r=[xsrcB, stack_bufs], w=[xnB])
                def part_b():
                    bk = nbank()
                    for k in range(8):
                        op(PE, lambda k=k, bk=bk: nc.tensor.transpose(psb[bk][:, k * 128:(k + 1) * 128], xn[:, k * 128:(k + 1) * 128], ident[:]),
                           r=[xnB, cB], w=[psB[bk]])
                    for k in range(8):
                        if parity == 0:
                            op(ACT, lambda k=k, bk=bk: nc.scalar.activation(out=dstT[:, k, dst_cols], in_=psb[bk][:, k * 128:(k + 1) * 128],
                                                                          func=AF.Identity, scale=scaleT[:, k:k + 1], bias=biasT[:, k:k + 1]),
                               r=[psB[bk], modB], w=[dstB])
                        else:
                            op(DVE, lambda k=k, bk=bk: nc.vector.tensor_scalar(dstT[:, k, dst_cols], psb[bk][:, k * 128:(k + 1) * 128],
                                                                             scaleT[:, k:k + 1], biasT[:, k:k + 1], op0=ALU.mult, op1=ALU.add),
                               r=[psB[bk], modB], w=[dstB])
                return part_b

            with ExitStack() as p1:
                xt = [sb(p1, "xt%d" % i, [128, 1024], F32) for i in range(3)]
                xtB = [Buf() for _ in range(3)]
                xn = [sb(p1, "xn%d" % i, [128, 1024], BF16) for i in range(2)]
                xnB = [Buf(), Buf()]
                junk = sb(p1, "junk", [128, 1024], BF16)
                junkB = Buf()
                ss = sb(p1, "ss", [128, NT], F32)
                rstd = sb(p1, "rstd", [128, NT], F32)
                stB = Buf()
                pend_b = None
                for tt in range(NT):
                    i = tt % 3
                    dma(QS, xt[i][:], x_d[tt * 128:(tt + 1) * 128, :], w=[xtB[i]])
                    pb = norm_to_T(stB, xt[i][:], xtB[i], ss, rstd, tt, xn[tt % 2], xnB[tt % 2], junk, junkB, hT, hTB[tt],
                                   slice(tt * 128, (tt + 1) * 128), scale1, sh1, 1, xn_on_act=True)
                    if pend_b is not None:
                        pend_b()
                    pend_b = pb
                pend_b()
                if DEBUG:
                    dma(QS, dbg["hT"], hT[:], r=hTB)
                cx.barrier()
            p0.close()
            if STOP_AFTER <= 1:
                return nc, dbg

            def mlstm_gen(p4):
                LP = sb(p4, "LP", [128, 2, NT, 4], F32)
                IA = sb(p4, "IA", [128, 2, NT, 4], F32)
                Aa = sb(p4, "Aa", [128, 2, NT, 4], F32)
                EBs = sb(p4, "EBs", [128, 2, NT, 4], F32)
                EG = sb(p4, "EG", [128, 2, NT, 4], F32)
                gB = Buf()
                convw = sb(p4, "convw", [128, 8, 5], F32)
                convb = sb(p4, "convb", [128, 8], F32)
                mng = sb(p4, "mng", [128, 4], F32)
                dma(QS, convw[:], convw_d, w=[cB])
                dma(QS, convb[:], convb_d, w=[cB])
                dma(QS, mng[:], mng_d, w=[cB])
                for d in range(2):
                    fc = 4 + 8 * d
                    op(ACT, lambda d=d, fc=fc: nc.scalar.activation(out=LP[:, d], in_=GT[:, :, fc:fc + 4], func=AF.Exp, scale=-1.0), r=[GTB], w=[gB])
                op(ACT, lambda: nc.scalar.activation(out=LP[:], in_=LP[:], func=AF.Ln, bias=1.0), r=[gB], w=[gB])
                for d in range(2):
                    ic = 8 * d
                    bk = nbank()
                    tri_ = triu if d == 0 else tril
                    op(PE, lambda: nc.tensor.matmul(ps[bk][:, 0:128], tri_[:], LP[:, d].rearrange("p c h -> p (c h)"), start=True, stop=True),
                       r=[gB, cB], w=[psB[bk]])
                    op(DVE, lambda: nc.vector.tensor_tensor(out=IA[:, d], in0=ps[bk][:, 0:128].rearrange("p (c h) -> p c h", h=4),
                                                            in1=GT[:, :, ic:ic + 4], op=ALU.add), r=[psB[bk], GTB], w=[gB])
                    op(ACT, lambda: nc.scalar.activation(out=EBs[:, d], in_=ps[bk][:, 0:128].rearrange("p (c h) -> p c h", h=4),
                                                         func=AF.Exp, scale=-1.0, bias=lns[:, 0:1]), r=[psB[bk], cB], w=[gB])
                bg = nbank()
                op(PE, lambda: nc.tensor.matmul(ps[bg][:, 0:256], ones_f[:], LP[:].rearrange("p d c h -> p (d c h)"), start=True, stop=True),
                   r=[gB, cB], w=[psB[bg]])
                op(ACT, lambda: nc.scalar.activation(out=EG[:].rearrange("p d c h -> p (d c h)"), in_=ps[bg][:, 0:256], func=AF.Exp, scale=-1.0),
                   r=[psB[bg]], w=[gB])
                op(ACT, lambda: nc.scalar.activation(out=Aa[:], in_=IA[:], func=AF.Exp), r=[gB], w=[gB])
                yield

                wm_ = [sb(p4, "wm%d" % i, [128, 8, 4, 128], BF16) for i in range(2)]
                wmB_ = [Buf(), Buf()]
                qkraw_ = [sb(p4, "qkraw%d" % i, [128, 2, NT * 129], BF16) for i in range(2)]
                qkpB_ = [[Buf(), Buf()], [Buf(), Buf()]]
                QKm = sb(p4, "QKm", [128, 2, S], BF16)
                QKB = [Buf(), Buf()]
                sigo_ = [sb(p4, "sigo%d" % i, [128, S], BF16) for i in range(2)]
                sigB_ = [Buf(), Buf()]
                Ktok = sb(p4, "Ktok", [128, NT, 128], BF16)
                KtB = Buf()
                Vm_ = [sb(p4, "Vm%d" % i, [128, NT, 129], BF16) for i in range(2)]
                VmB_ = [Buf(), Buf()]
                Cst = sb(p4, "Cst", [128, 2, NT, 129], BF16)
                CsB = [Buf(), Buf()]
                Pch = sb(p4, "Pch", [128, 2, 2, 129], F32)
                PcB = [[Buf(), Buf()], [Buf(), Buf()]]
                Dg = sb(p4, "Dg", [128, 2, 5, 128], BF16)
                DgB = Buf()
                Sm2 = [sb(p4, "Sm2%d" % i, [128, 2, 128], BF16) for i in range(2)]
                Sm = [[Sm2[i][:, d, :] for i in range(2)] for d in range(2)]
                SmB_ = [Buf(), Buf()]
                SmB = [[SmB_[0], SmB_[1]], [SmB_[0], SmB_[1]]]
                FG = 16
                Hraw = sb(p4, "Hraw", [128, FG, 2, 129], F32)
                HrB = Buf()
                hf = sb(p4, "hf", [128, FG, 128], F32)
                hb = sb(p4, "hb", [128, FG, 128], F32)
                rr = sb(p4, "rr", [128, FG, 2], F32)
                ssh = sb(p4, "ssh", [128, FG], F32)
                hn = sb(p4, "hn", [128, FG, 128], BF16)
                fB = Buf()
                hnB = Buf()
                y4 = [sb(p4, "y4%d" % i, [128, FG * 128], BF16) for i in range(2)]
                y4B = [Buf(), Buf()]
                hblk = [sb(p4, "hblk%d" % i, [128, 8, 512], BF16) for i in range(2)]
                hblkB = [Buf(), Buf()]
                hcnt = [0]
                for i_ in range(2):
                    op(DVE, lambda i_=i_: nc.vector.memset(Vm_[i_][:, :, 128:129], 1.0), w=[VmB_[i_]])
                op(DVE, lambda: nc.vector.memset(Cst[:, 0, 0, :], 0.0), w=[CsB[0]])
                op(DVE, lambda: nc.vector.memset(Cst[:, 1, NT - 1, :], 0.0), w=[CsB[1]])

                def proj_units(j):
                    sl_ = j % 2
                    wm, wmB = wm_[sl_], wmB_[sl_]
                    qkpre, qkpB = qkraw_[sl_][:, :, 0:S + 4], qkpB_[sl_]
                    sigo, sigB = sigo_[sl_], sigB_[sl_]
                    Vm, VmB = Vm_[sl_], VmB_[sl_]
                    cols = (768 + j * 128, 1280 + j * 128, 1792 + j * 128, 2304 + j * 128)
                    for wi, c0 in enumerate(cols):
                        dma(QP, wm[:, :, wi, :], win_d[:, :, c0:c0 + 128], w=[wmB])
                    op(DVE, lambda: nc.vector.memset(qkpre[:, :, 0:2], 0.0), w=qkpB)
                    op(DVE, lambda: nc.vector.memset(qkpre[:, :, S + 2:S + 4], 0.0), w=qkpB)
                    for tb in range(NB):
                        hb_ = hblk[hcnt[0] % 2]
                        hbB = hblkB[hcnt[0] % 2]
                        hcnt[0] += 1
                        dma(QS, hb_[:], hT_s[tb], r=[hTsB], w=[hbB])
                        for w2 in range(3):
                            wi = (0, 1, 3)[w2]
                            bk = nbank()
                            for k in range(8):
                                op(PE, lambda k=k: nc.tensor.matmul(ps[bk][:, :], wm[:, k, wi, :], hb_[:, k, :], start=(k == 0), stop=(k == 7)),
                                   r=[wmB, hbB], w=[psB[bk]])
                            if w2 < 2:
                                op(ACT, lambda: nc.scalar.copy(out=qkpre[:, w2, 2 + tb * 512:2 + (tb + 1) * 512], in_=ps[bk][:, :]),
                                   r=[psB[bk]], w=[qkpB[w2]])
                            else:
                                op(ACT, lambda: nc.scalar.activation(out=sigo[:, tb * 512:(tb + 1) * 512], in_=ps[bk][:, :], func=AF.Sigmoid),
                                   r=[psB[bk]], w=[sigB])
                            yield
                        bk = nbank()
                        for ti in range(4):
                            for k in range(8):
                                op(PE, lambda k=k, ti=ti: nc.tensor.matmul(ps[bk][:, ti * 128:(ti + 1) * 128], hb_[:, k, ti * 128:(ti + 1) * 128],
                                                                           wm[:, k, 2, :], start=(k == 0), stop=(k == 7)),
                                   r=[wmB, hbB], w=[psB[bk]])
                        op(DVE, lambda: nc.vector.tensor_copy(Vm[:, tb * 4:tb * 4 + 4, 0:128], ps[bk][:, :].rearrange("p (t d) -> p t d", t=4)),
                           r=[psB[bk]], w=[VmB])
                        yield

                fill = proj_units(0)
                for _ in fill:
                    yield
                for j in range(4):
                    sl_ = j % 2
                    qkraw = qkraw_[sl_]
                    qkpre, qkpB = qkraw[:, :, 0:S + 4], qkpB_[sl_]
                    sigo, sigB = sigo_[sl_], sigB_[sl_]
                    Vm, VmB = Vm_[sl_], VmB_[sl_]
                    Vt = qkraw[:].rearrange("p d (c v) -> p d c v", v=129)
                    VtB = qkpB
                    for w2 in range(2):
                        ch = j + 4 * w2
                        for k5 in range(5):
                            op(DVE, lambda w2=w2, ch=ch, k5=k5: nc.vector.tensor_scalar(Dg[:, w2, k5, :], ident[:], convw[:, ch, k5:k5 + 1], None, op0=ALU.mult),
                               r=[cB], w=[DgB])
                    for w2 in range(2):
                        ch = j + 4 * w2
                        for tb in range(NB):
                            bk = nbank()
                            for k5 in range(5):
                                op(PE, lambda k5=k5, bk=bk, w2=w2, tb=tb: nc.tensor.matmul(ps[bk][:, :], Dg[:, w2, k5, :],
                                                                                          qkpre[:, w2, tb * 512 + k5:tb * 512 + k5 + 512],
                                                                                          start=(k5 == 0), stop=(k5 == 4)),
                                   r=[DgB, qkpB[w2]], w=[psB[bk]])
                            op(ACT, lambda bk=bk, w2=w2, tb=tb, ch=ch: nc.scalar.activation(out=QKm[:, w2, tb * 512:(tb + 1) * 512], in_=ps[bk][:, :],
                                                                                           func=AF.Silu, bias=convb[:, ch:ch + 1]),
                               r=[psB[bk], cB], w=[QKB[w2]])
                            yield
                    for c8 in range(NT // 8):
                        bk = nbank()
                        for ci in range(8):
                            c = c8 * 8 + ci
                            op(PE, lambda bk=bk, ci=ci, c=c: nc.tensor.transpose(psb[bk][:, ci * 128:(ci + 1) * 128], QKm[:, 1, c * 128:(c + 1) * 128], ident[:]),
                               r=[QKB[1], cB], w=[psB[bk]])
                        op(DVE, lambda bk=bk, c8=c8: nc.vector.tensor_copy(Ktok[:, c8 * 8:c8 * 8 + 8, :], psb[bk][:, :].rearrange("p (c d) -> p c d", c=8)),
                           r=[psB[bk]], w=[KtB])
                        yield
                    for d in range(2):
                        op(DVE, lambda d=d: nc.vector.tensor_tensor(out=Vt[:, d], in0=Vm[:], in1=Aa[:, d, :, j].unsqueeze(2).broadcast_to([128, NT, 129]), op=ALU.mult),
                           r=[VmB, gB], w=[VtB[d]])
                    fill = proj_units(j + 1) if j < 3 else iter(())
                    for stp in range(NT - 1):
                        yield
                        next(fill, None)
                        for d in range(2):
                            c = stp if d == 0 else NT - 1 - stp
                            cn = c + 1 if d == 0 else c - 1
                            bk = nbank()
                            op(PE, lambda bk=bk, c=c, d=d: nc.tensor.matmul(ps[bk][:, 0:129], Ktok[:, c, :], Vt[:, d, c, :], start=True, stop=True),
                               r=[KtB, VtB[d]], w=[psB[bk]])
                            cur, prv = stp % 2, (stp + 1) % 2
                            if stp == 0:
                                op(DVE, lambda bk=bk, d=d, cur=cur: nc.vector.tensor_copy(Pch[:, d, cur, :], ps[bk][:, 0:129]), r=[psB[bk]], w=[PcB[d][cur]])
                            else:
                                cp = c - 1 if d == 0 else c + 1
                                op(DVE, lambda bk=bk, d=d, cur=cur, prv=prv, cp=cp: nc.vector.scalar_tensor_tensor(
                                    out=Pch[:, d, cur, :], in0=Pch[:, d, prv, :], scalar=EG[:, d, cp, j:j + 1], in1=ps[bk][:, 0:129],
                                    op0=ALU.mult, op1=ALU.add), r=[PcB[d][prv], psB[bk], gB], w=[PcB[d][cur]])
                            op(ACT, lambda d=d, cur=cur, c=c, cn=cn: nc.scalar.activation(out=Cst[:, d, cn, :], in_=Pch[:, d, cur, :], func=AF.Identity,
                                                                                         scale=EG[:, d, c, j:j + 1]),
                               r=[PcB[d][cur], gB], w=[CsB[d]])
                    def emit_s(c):
                        bs_ = nbank()
                        cs_ = slice(c * 128, (c + 1) * 128)
                        op(PE, lambda: nc.tensor.matmul(ps[bs_][:, 0:128], QKm[:, 1, cs_], QKm[:, 0, cs_], start=True, stop=True),
                           r=[QKB[0], QKB[1]], w=[psB[bs_]])
                        return bs_

                    look = len(cx.banks) == 8
                    bs_next = emit_s(0) if look else None
                    for c in range(NT):
                        yield
                        next(fill, None)
                        csl = slice(c * 128, (c + 1) * 128)
                        i = c % 2
                        if look:
                            bs = bs_next
                            if c + 1 < NT:
                                bs_next = emit_s(c + 1)
                        else:
                            bs = emit_s(c)
                        op(DVE, lambda bs=bs: nc.vector.tensor_tensor(out=Sm2[i][:], in0=ps[bs][:, 0:128].unsqueeze(1).broadcast_to([128, 2, 128]),
                                                                      in1=mask2[:], op=ALU.mult),
                           r=[psB[bs], cB], w=[SmB_[i]])
                        for d in range(2):
                            bn = nbank()
                            op(PE, lambda d=d, bn=bn: nc.tensor.matmul(ps[bn][:, 0:129], Sm[d][i], Vt[:, d, c, :], start=True, stop=False),
                               r=[SmB[d][i], VtB[d]], w=[psB[bn]])
                            op(PE, lambda d=d, bn=bn: nc.tensor.matmul(ps[bn][:, 0:129], QKm[:, 0, csl], Cst[:, d, c, :], start=False, stop=True),
                               r=[QKB[0], CsB[d]], w=[psB[bn]])
                            op(ACT, lambda d=d, bn=bn: nc.scalar.copy(out=Hraw[:, c % FG, d, :], in_=ps[bn][:, 0:129]), r=[psB[bn]], w=[HrB])
                        if c % FG == FG - 1:
                            c0 = c - (FG - 1)
                            ebv = EBs[:, :, c0:c0 + FG, j].rearrange("p d c -> p c d")
                            op(ACT, lambda: nc.scalar.activation(out=rr[:], in_=Hraw[:, :, :, 128], func=AF.Abs), r=[HrB], w=[fB])
                            op(DVE, lambda ebv=ebv: nc.vector.tensor_tensor(out=rr[:], in0=rr[:], in1=ebv, op=ALU.mult), r=[fB, gB], w=[fB])
                            op(DVE, lambda: nc.vector.tensor_scalar(rr[:], rr[:], 1.0, None, op0=ALU.max), r=[fB], w=[fB])
                            op(DVE, lambda: nc.vector.reciprocal(rr[:], rr[:]), r=[fB], w=[fB])
                            op(DVE, lambda ebv=ebv: nc.vector.tensor_tensor(out=rr[:], in0=rr[:], in1=ebv, op=ALU.mult), r=[fB, gB], w=[fB])
                            op(DVE, lambda: nc.vector.tensor_tensor(out=hf[:], in0=Hraw[:, :, 0, 0:128], in1=rr[:, :, 0:1].broadcast_to([128, FG, 128]), op=ALU.mult),
                               r=[HrB, fB], w=[fB])
                            op(DVE, lambda: nc.vector.tensor_tensor(out=hb[:], in0=Hraw[:, :, 1, 0:128], in1=rr[:, :, 1:2].broadcast_to([128, FG, 128]), op=ALU.mult),
                               r=[HrB, fB], w=[fB])
                            op(DVE, lambda: nc.vector.tensor_tensor(out=hf[:], in0=hf[:], in1=hb[:], op=ALU.add), r=[fB], w=[fB])
                            op(DVE, lambda: nc.vector.tensor_tensor(out=hb[:], in0=hf[:], in1=hf[:], op=ALU.mult), r=[fB], w=[fB])
                            op(DVE, lambda: nc.vector.tensor_reduce(out=ssh[:], in_=hb[:], axis=AX.X, op=ALU.add), r=[fB], w=[fB])
                            op(ACT, lambda: nc.scalar.activation(out=ssh[:], in_=ssh[:], func=AF.Ln, scale=1.0 / 128, bias=epsT[:, 0:1]), r=[fB, cB], w=[fB])
                            op(ACT, lambda: nc.scalar.activation(out=ssh[:], in_=ssh[:], func=AF.Exp, scale=-0.5), r=[fB], w=[fB])
                            op(DVE, lambda: nc.vector.tensor_tensor(out=hn[:], in0=hf[:], in1=ssh[:].unsqueeze(2).broadcast_to([128, FG, 128]), op=ALU.mult),
                               r=[fB], w=[hnB])
                            yi = (c // FG) % 2
                            for c8 in range(FG // 8):
                                bt = nbank()
                                for ci in range(8):
                                    op(PE, lambda bt=bt, ci=ci, c8=c8: nc.tensor.transpose(psb[bt][:, ci * 128:(ci + 1) * 128], hn[:, c8 * 8 + ci, :], ident[:]),
                                       r=[hnB, cB], w=[psB[bt]])
                                op(DVE, lambda bt=bt, c8=c8: nc.vector.scalar_tensor_tensor(out=y4[yi][:, c8 * 1024:(c8 + 1) * 1024], in0=psb[bt][:, :],
                                                                                           scalar=mng[:, j:j + 1], in1=sigo[:, (c0 + c8 * 8) * 128:(c0 + c8 * 8 + 8) * 128],
                                                                                           op0=ALU.mult, op1=ALU.mult),
                                   r=[psB[bt], cB, sigB], w=[y4B[yi]])
                            dma(QP, mix_s[512 + j * 128:512 + (j + 1) * 128, c0 * 128:(c0 + FG) * 128], y4[yi][:, :], r=[y4B[yi]], w=[mixB])
                    for _ in fill:
                        yield
                yield

            with ExitStack() as pa:
                with ExitStack() as p2:
                    watt = sb(p2, "watt", [128, 8, 784], BF16)
                    wattB = Buf()
                    cos_t = sb(p2, "cos_t", [128, NT, 32], F32)
                    sin_t = sb(p2, "sin_t", [128, NT, 32], F32)
                    g10 = sb(p2, "g10", [128, 640], F32)
                    gateb = sb(p2, "gateb", [128, 16], F32)
                    qkf = [sb(p2, "qkf%d" % i, [128, 640], F32) for i in range(3)]
                    qkfB = [Buf(), Buf(), Buf()]
                    sqb_ = [sb(p2, "sqb%d" % i, [128, 640], F32) for i in range(3)]
                    qg_ = [sb(p2, "qg%d" % i, [128, 640], F32) for i in range(3)]
                    tcs_ = [sb(p2, "tcs%d" % i, [128, 640], F32) for i in range(3)]
                    tsn_ = [sb(p2, "tsn%d" % i, [128, 640], F32) for i in range(3)]
                    ro_ = [sb(p2, "ro%d" % i, [128, 640], F32) for i in range(3)]
                    ss10_ = [sb(p2, "ss10%d" % i, [128, 10], F32) for i in range(3)]
                    wkB_ = [Buf(), Buf(), Buf()]
                    ssB_ = [Buf(), Buf(), Buf()]
                    qn = [sb(p2, "qn%d" % i, [128, 640], BF16) for i in range(3)]
                    qnB = [Buf(), Buf(), Buf()]
                    kd = [sb(p2, "kd%d" % i, [128, 256], BF16) for i in range(3)]
                    kdB = [Buf(), Buf(), Buf()]
                    dma(QP, watt[:, :, 0:768], win_d[:, :, 0:768], w=[wattB])
                    dma(QP, watt[:, :, 768:784], win_d[:, :, 2816:2832], w=[wattB])
                    dma(QS, cos_t[:], cos_d, w=[cB])
                    dma(QS, sin_t[:], sin_d, w=[cB])
                    dma(QS, g10[:], g10_d, w=[cB])
                    dma(QS, gateb[:], gateb_d, w=[cB])
                    for tb in range(NB):
                        dma(QS, hT_s[tb], hT[:, :, tb * 512:(tb + 1) * 512], r=hTB[tb * 4:tb * 4 + 4], w=[hTsB])
                    op(DVE, lambda: nc.vector.memset(Vaug[:, :, :, 64:65], 1.0), w=[VB])

                    def v5(t):
                        return t[:].rearrange("p (h a b c) -> p h a b c", h=10, a=2, b=2, c=16)

                    pend2 = []
                    for tt in range(NT):
                        i = tt % 3
                        tsl = slice(tt * 128, (tt + 1) * 128)
                        sqb, qg, tcs, tsn, ro, ss10, wkB = sqb_[i], qg_[i], tcs_[i], tsn_[i], ro_[i], ss10_[i], wkB_[i]
                        bq = nbank()
                        for k in range(8):
                            op(PE, lambda k=k, bq=bq: nc.tensor.matmul(ps[bq][:, :], hT[:, k, tsl], watt[:, k, 0:512],
                                                                     start=(k == 0), stop=(k == 7)),
                               r=[hTB[tt], wattB], w=[psB[bq]])
                        bkv = nbank()
                        for k in range(8):
                            op(PE, lambda k=k, bkv=bkv: nc.tensor.matmul(ps[bkv][:, 0:272], hT[:, k, tsl], watt[:, k, 512:784],
                                                                       start=(k == 0), stop=(k == 7)),
                               r=[hTB[tt], wattB], w=[psB[bkv]])
                        op(ACT, lambda: nc.scalar.copy(out=qkf[i][:, 0:512], in_=ps[bq][:, :]), r=[psB[bq]], w=[qkfB[i]])
                        op(ACT, lambda: nc.scalar.copy(out=qkf[i][:, 512:640], in_=ps[bkv][:, 0:128]), r=[psB[bkv]], w=[qkfB[i]])
                        op(DVE, lambda: nc.vector.tensor_copy(Vaug[:, tt, :, 0:64], ps[bkv][:, 128:256].rearrange("p (g d) -> p g d", g=2)),
                           r=[psB[bkv]], w=[VB])
                        op(DVE, lambda: nc.vector.tensor_tensor(out=GT[:, tt, :], in0=ps[bkv][:, 256:272], in1=gateb[:], op=ALU.add),
                           r=[psB[bkv], cB], w=[GTB])
                        for hh in range(10):
                            op(ACT, lambda hh=hh: nc.scalar.activation(out=sqb[:, hh * 64:(hh + 1) * 64], in_=qkf[i][:, hh * 64:(hh + 1) * 64], func=AF.Square,
                                                                       accum_out=ss10[:, hh:hh + 1]), r=[qkfB[i]], w=[ssB_[i]])
                        op(ACT, lambda: nc.scalar.activation(out=ss10[:], in_=ss10[:], func=AF.Ln, scale=1.0 / 64, bias=epsT[:, 0:1]), r=[ssB_[i], cB], w=[ssB_[i]])
                        op(ACT, lambda: nc.scalar.activation(out=ss10[:], in_=ss10[:], func=AF.Exp, scale=-0.5), r=[ssB_[i]], w=[ssB_[i]])
                        op(DVE, lambda: nc.vector.tensor_tensor(out=qg[:], in0=qkf[i][:], in1=g10[:], op=ALU.mult), r=[qkfB[i], cB], w=[wkB])
                        for a in range(2):
                            cosa = cos_t[:, tt, a * 16:(a + 1) * 16].unsqueeze(1).unsqueeze(1).broadcast_to([128, 10, 2, 16])
                            sina = sin_t[:, tt, a * 16:(a + 1) * 16].unsqueeze(1).unsqueeze(1).broadcast_to([128, 10, 2, 16])
                            op(DVE, lambda a=a, cosa=cosa: nc.vector.tensor_tensor(out=v5(tcs)[:, :, a], in0=v5(qg)[:, :, a], in1=cosa, op=ALU.mult), r=[wkB, cB], w=[wkB])
                            op(DVE, lambda a=a, sina=sina: nc.vector.tensor_tensor(out=v5(tsn)[:, :, a], in0=v5(qg)[:, :, a], in1=sina, op=ALU.mult), r=[wkB, cB], w=[wkB])
                        op(DVE, lambda: nc.vector.tensor_tensor(out=v5(ro)[:, :, :, 0, :], in0=v5(tcs)[:, :, :, 0, :], in1=v5(tsn)[:, :, :, 1, :],
                                                                op=ALU.subtract), r=[wkB], w=[wkB])
                        op(DVE, lambda: nc.vector.tensor_tensor(out=v5(ro)[:, :, :, 1, :], in0=v5(tcs)[:, :, :, 1, :], in1=v5(tsn)[:, :, :, 0, :],
                                                                op=ALU.add), r=[wkB], w=[wkB])
                        op(DVE, lambda: nc.vector.tensor_tensor(out=qn[i][:].rearrange("p (h d) -> p h d", h=10),
                                                                in0=ro[:].rearrange("p (h d) -> p h d", h=10),
                                                                in1=ss10[:].unsqueeze(2).broadcast_to([128, 10, 64]), op=ALU.mult),
                           r=[wkB, ssB_[i]], w=[qnB[i]])
                        op(DVE, lambda: nc.vector.tensor_copy(kd[i][:].rearrange("p (g u d) -> p g u d", g=2, u=2),
                                                              qn[i][:, 512:640].rearrange("p (g d) -> p g d", g=2).unsqueeze(2).broadcast_to([128, 2, 2, 64])),
                           r=[qnB[i]], w=[kdB[i]])

                        def stage_b(i=i, tsl=tsl):
                            bt = nbank()
                            for pr in range(4):
                                op(PE, lambda pr=pr, bt=bt: nc.tensor.transpose(psb[bt][:, pr * 128:(pr + 1) * 128], qn[i][:, pr * 128:(pr + 1) * 128], ident[:]),
                                   r=[qnB[i], cB], w=[psB[bt]])
                            for g in range(2):
                                op(PE, lambda g=g, bt=bt: nc.tensor.transpose(psb[bt][:, 512 + g * 128:512 + (g + 1) * 128], kd[i][:, g * 128:(g + 1) * 128], ident[:]),
                                   r=[kdB[i], cB], w=[psB[bt]])
                            op(ACT, lambda: nc.scalar.copy(out=QT[:, :, tsl], in_=psb[bt][:, 0:512].rearrange("p (h t) -> p h t", h=4)),
                               r=[psB[bt]], w=[QTB])
                            op(ACT, lambda: nc.scalar.copy(out=KT[:, :, tsl], in_=psb[bt][:, 512:768].rearrange("p (h t) -> p h t", h=2)),
                               r=[psB[bt]], w=[KTB])

                        pend2.append(stage_b)
                        if len(pend2) > 2:
                            pend2.pop(0)()
                    for fb in pend2:
                        fb()
                    if DEBUG:
                        dma(QS, dbg["QT"], QT[:], r=[QTB])
                        dma(QS, dbg["KT"], KT[:], r=[KTB])
                    cx.barrier()
                hs.close()
                if STOP_AFTER <= 2:
                    return nc, dbg

                with ExitStack() as p3:
                    wa2 = [sb(p3, "wa2%d" % i, [128, 8, 512], BF16) for i in range(8)]
                    wa2B = [Buf() for _ in range(8)]
                    NPT = 3
                    PT = [sb(p3, "PT%d" % i, [128, 1024], BF16) for i in range(NPT)]
                    PTB = [Buf() for _ in range(NPT)]
                    if not PACKPV:
                        rrow = [[sb(p3, "rrow%d%d" % (i, e), [128, 512], F32) for e in range(2)] for i in range(1)] * 2
                        Osb = [[sb(p3, "Osb%d%d" % (i, e), [65, 512], F32) for e in range(2)] for i in range(1)] * 2
                        yst = [[sb(p3, "yst%d%d" % (i, e), [64, 512], BF16) for e in range(2)] for i in range(1)] * 2
                    rrB = [[Buf(), Buf()]] * 2
                    OsB = [[Buf(), Buf()]] * 2
                    ysB = [[Buf(), Buf()]] * 2
                    if PACKPV:
                        acc = [sb(p3, "acc%d" % i, [128, 1024], F32) for i in range(2)]
                        rinv = sb(p3, "rinv", [128, 512], F32)
                        Osb2 = sb(p3, "Osb2", [128, 512], F32)
                        yst2 = [sb(p3, "yst2%d" % i, [128, 512], BF16) for i in range(2)]
                    accB = [Buf(), Buf()]
                    accPB = [Buf(), Buf()]
                    rinvB, Osb2B, yst2B = Buf(), Buf(), [Buf(), Buf()]
                    OB2 = [Buf(excl=True), Buf(excl=True)]
                    NSL = 2 if INTERLEAVE else 3
                    pairB = [Buf(excl=True) for _ in range(NSL)]
                    OB = [[Buf(excl=True), Buf(excl=True)]] * 2
                    LA = 1 if INTERLEAVE else 2
                    slot_ctr = [0]
                    gslot = {}

                    def next_slot():
                        v = slot_ctr[0] % NSL
                        slot_ctr[0] += 1
                        return v

                    def fin_a(o):
                        if PACKPV:
                            op(ACT, lambda: nc.scalar.copy(out=Osb2[:, :], in_=ps[6 + o % 2][:, :]), r=[OB2[o % 2]], w=[Osb2B])
                            return
                        i = o % 2
                        for e in range(2):
                            ob = 6 + e
                            op(ACT, lambda e=e, ob=ob: nc.scalar.copy(out=Osb[i][e][:, :], in_=ps[ob][0:65, :]), r=[OB[i][e]], w=[OsB[i][e]])
                            op(DVE, lambda e=e: nc.vector.reciprocal(rrow[i][e][64:65, :], Osb[i][e][64:65, :]), r=[OsB[i][e]], w=[rrB[i][e]])

                    def bc_s(o, sl):
                        if PACKPV:
                            for e in range(2):
                                op(PE, lambda e=e: nc.tensor.matmul(ps[2 * sl][e * 64:(e + 1) * 64, :], ones_f[:, 0:64], acc[o % 2][:, e * 512:(e + 1) * 512],
                                                                    start=True, stop=True, tile_position=(0, e * 64)),
                                   r=[accB[o % 2], accPB[o % 2], cB], w=[pairB[sl]])
                            return
                        i = o % 2
                        for e in range(2):
                            op(PE, lambda e=e: nc.tensor.matmul(ps[2 * sl + e][0:64, :], ones_f[64:65, 0:64], rrow[i][e][64:65, :], start=True, stop=True),
                               r=[rrB[i][e], cB], w=[pairB[sl]])

                    def bc_rest(pr, qb, o, sl):
                        if PACKPV:
                            op(DVE, lambda: nc.vector.reciprocal(rinv[:, :], ps[2 * sl][:, :]), r=[pairB[sl]], w=[rinvB])
                            op(DVE, lambda: nc.vector.tensor_tensor(out=yst2[o % 2][:, :], in0=Osb2[:, :], in1=rinv[:, :], op=ALU.mult),
                               r=[Osb2B, rinvB], w=[yst2B[o % 2]])
                            dma(QP, mix_s[pr * 128:(pr + 1) * 128, qb * 512:(qb + 1) * 512], yst2[o % 2][:, :], r=[yst2B[o % 2]], w=[mixB])
                            return
                        i = o % 2
                        for e in range(2):
                            h = 2 * pr + e
                            op(DVE, lambda e=e: nc.vector.tensor_tensor(out=yst[i][e][:, :], in0=Osb[i][e][0:64, :], in1=ps[2 * sl + e][0:64, :], op=ALU.mult),
                               r=[OsB[i][e], pairB[sl]], w=[ysB[i][e]])
                            dma(QP, mix_s[h * 64:(h + 1) * 64, qb * 512:(qb + 1) * 512], yst[i][e][:, :], r=[ysB[i][e]], w=[mixB])

                    for wv in range(40):
                        op(PE, lambda wv=wv: nc.tensor.matmul(ps[6][:, :], ident[:], QT[:, wv % 4, (wv % 8) * 512:(wv % 8 + 1) * 512], start=True, stop=True),
                           r=[cB, QTB], w=[OB[0][0]])
                    groups = [(pr, qb, kt) for pr in range(4) for qb in range(NB) for kt in range(NT)]
                    seq = []
                    due = []
                    for gi, (pr, qb, kt) in enumerate(groups):
                        seq.append(("S", pr, qb, kt, gi // NT))
                        due = [(n - 1, it_) for (n, it_) in due]
                        while due and due[0][0] <= 0:
                            seq.append(due.pop(0)[1])
                        if kt == NT - 1:
                            due.append((3, ("BC", pr, qb, 0, gi // NT)))
                    for _ in range(3):
                        seq.append(("NOP", 0, 0, 0, 0))
                    seq.extend(it_ for (_, it_) in due)
                    N = len(seq)
                    real_idx = 0
                    mdone = True
                    if INTERLEAVE:
                        cx.banks = [4, 5]
                        p4s = ExitStack()
                        mgen = mlstm_gen(p4s)
                        mdone = False
                    for it in range(N + LA):
                        if it % 24 == 8 and it // 24 < 8:
                            n_ = 4 + it // 24
                            dma(QP, wa2[n_ - 4][:], wada_d[n_], w=[wa2B[n_ - 4]])
                        if it % 24 == 8 and 10 <= it // 24 < 14:
                            q4 = it // 24 - 10
                            rs = slice(q4 * 704, (q4 + 1) * 704)
                            dma(QP, wg_s[rs, :], wg_d[rs, :], w=[wsB])
                            dma(QP, wu_s[rs, :], wu_d[rs, :], w=[wsB])
                        if INTERLEAVE and not mdone and it % 2 == 0:
                            try:
                                next(mgen)
                            except StopIteration:
                                mdone = True
                        if it < N:
                            kind, pr, qb, kt, o = seq[it]
                            if kind != "NOP":
                                sl = next_slot()
                                gslot[it] = sl
                            if kind == "S":
                                g = pr // 2
                                for e in range(2):
                                    hp = e * 64
                                    op(PE, lambda e=e, hp=hp: nc.tensor.matmul(ps[2 * sl + e][:, :], KT[hp:hp + 64, g, kt * 128:(kt + 1) * 128],
                                                                               QT[hp:hp + 64, pr, qb * 512:(qb + 1) * 512], start=True, stop=True),
                                       r=[KTB, QTB], w=[pairB[sl]])
                            elif kind == "BC":
                                bc_s(o, sl)
                        if it >= LA:
                            j = it - LA
                            kind, pr, qb, kt, o = seq[j]
                            if kind == "NOP":
                                continue
                            sl = gslot.pop(j)
                            if kind == "BC":
                                bc_rest(pr, qb, o, sl)
                                continue
                            g = pr // 2
                            pi = real_idx % NPT
                            real_idx += 1
                            op(ACT, lambda: nc.scalar.activation(out=PT[pi][:], in_=psall[:, sl * 1024:(sl + 1) * 1024], func=AF.Exp, scale=0.125),
                               r=[pairB[sl]], w=[PTB[pi]])
                            for e in range(2):
                                if PACKPV:
                                    ob = 6 + o % 2
                                    op(PE, lambda e=e, ob=ob: nc.tensor.matmul(ps[ob][e * 64:(e + 1) * 64, :], Vaug[:, kt, g, 0:64], PT[pi][:, e * 512:(e + 1) * 512],
                                                                               start=(kt == 0), stop=(kt == NT - 1), tile_position=(0, e * 64)),
                                       r=[VB, PTB[pi]], w=[OB2[o % 2]])
                                    continue
                                ob = 6 + e
                                op(PE, lambda e=e, ob=ob: nc.tensor.matmul(ps[ob][0:65, :], Vaug[:, kt, g, :], PT[pi][:, e * 512:(e + 1) * 512],
                                                                           start=(kt == 0), stop=(kt == NT - 1)),
                                   r=[VB, PTB[pi]], w=[OB[o % 2][e]])
                            if PACKPV:
                                CS = 704
                                if kt == 0:
                                    op(DVE, lambda: nc.vector.tensor_copy(acc[o % 2][:, 0:CS], PT[pi][:, 0:CS]), r=[PTB[pi]], w=[accB[o % 2]])
                                    op(POOL, lambda: nc.gpsimd.tensor_copy(acc[o % 2][:, CS:1024], PT[pi][:, CS:1024]), r=[PTB[pi]], w=[accPB[o % 2]])
                                else:
                                    op(DVE, lambda: nc.vector.tensor_tensor(out=acc[o % 2][:, 0:CS], in0=acc[o % 2][:, 0:CS], in1=PT[pi][:, 0:CS], op=ALU.add),
                                       r=[PTB[pi], accB[o % 2]], w=[accB[o % 2]])
                                    op(POOL, lambda: nc.gpsimd.tensor_tensor(out=acc[o % 2][:, CS:1024], in0=acc[o % 2][:, CS:1024], in1=PT[pi][:, CS:1024], op=ALU.add),
                                       r=[PTB[pi], accPB[o % 2]], w=[accPB[o % 2]])
                            if kt == NT - 1:
                                fin_a(o)
                    if INTERLEAVE:
                        for _ in mgen:
                            pass
                    cx.banks = list(range(8))
                    cx.barrier()
                    fins = [ada_block(n, wa2[n - 4], wa2B[n - 4], defer=True) for n in range(4, 12)]
                    for fn_ in fins:
                        fn_()
                    phase0b()
                    cx.barrier()
                    if INTERLEAVE:
                        p4s.close()
            qs.close()
            if not INTERLEAVE:
                p4s = ExitStack()
                for _ in mlstm_gen(p4s):
                    pass
                cx.barrier()
                p4s.close()
        with ExitStack() as p5:
            if DEBUG:
                dma(QS, dbg["mix"], mix_s, r=[mixB])
            wo = sb(p5, "wo", [128, 8, 1024], BF16)
            wdT = sb(p5, "wdT", [128, NJ, 1024], BF16)
            woB, wdB = Buf(), Buf()
            for q4 in range(4):
                dma(QP, wo[:, q4 * 2:q4 * 2 + 2, :], wout_d[:, q4 * 2:q4 * 2 + 2, :], w=[woB])
            for q4 in range(11):
                dma(QP, wdT[:, q4 * 2:q4 * 2 + 2, :], wd_d[:, q4 * 2:q4 * 2 + 2, :], w=[wdB])
            g1b = sb(p5, "g1b", [128, 1024], F32)
            g2b = sb(p5, "g2b", [128, 1024], F32)
            fngb = sb(p5, "fngb", [128, 1024], F32)
            dgf = [sb(p5, "dgf%d" % i, [128, 128], F32) for i in range(2)]
            dgB = [Buf(), Buf()]
            dma(QS, fngb[:], fng_d, w=[cB])
            cnt = 0
            for (dst, c0) in ((g1b, 16), (g2b, 40)):
                for k in range(8):
                    di = cnt % 2
                    cnt += 1
                    op(DVE, lambda di=di, k=k, c0=c0: nc.vector.tensor_scalar(dgf[di][:], ident_f[:], modT[:, c0 + k:c0 + k + 1], None, op0=ALU.mult),
                       r=[modB, cB], w=[dgB[di]])
                    bk = nbank()
                    op(PE, lambda di=di, bk=bk: nc.tensor.matmul(ps[bk][:, 0:128], ones_f[:], dgf[di][:], start=True, stop=True),
                       r=[dgB[di], cB], w=[psB[bk]])
                    op(ACT, lambda bk=bk, dst=dst, k=k: nc.scalar.copy(out=dst[:, k * 128:(k + 1) * 128], in_=ps[bk][:, 0:128]),
                       r=[psB[bk]], w=[modB])
            NW = 4
            wgt = [sb(p5, "wgt%d" % i, [128, 1024], BF16) for i in range(NW)]
            wut = [sb(p5, "wut%d" % i, [128, 1024], BF16) for i in range(NW)]
            wgB = [Buf() for _ in range(NW)]
            wuB = [Buf() for _ in range(NW)]
            mb = sb(p5, "mb", [128, 8, 512], BF16)
            mbB = Buf()
            xt = [sb(p5, "xt5%d" % i, [128, 1024], F32) for i in range(2)]
            xtB = [Buf(), Buf()]
            x1 = sb(p5, "x1", [128, 2, 4, 1024], F32)
            x1B = [[Buf() for _ in range(4)] for _ in range(2)]
            xn = [sb(p5, "xn5%d" % i, [128, 1024], BF16) for i in range(4)]
            xnB = [Buf() for _ in range(4)]
            junk = sb(p5, "junk5", [128, 1024], BF16)
            junkB = Buf()
            ss = sb(p5, "ss5", [128, NT], F32)
            rstd = sb(p5, "rstd5", [128, NT], F32)
            ss3 = sb(p5, "ss35", [128, NT], F32)
            stB = Buf()
            st3B = Buf()
            h2T = sb(p5, "h2T", [128, 8, 512], BF16)
            h2B = [Buf() for _ in range(4)]
            act = sb(p5, "act", [128, NJ, 512], BF16)
            actB = [Buf() for _ in range(NJ)]
            sg = [sb(p5, "sg%d" % i, [128, 512], F32) for i in range(2)]
            sgB = [Buf(), Buf()]
            x2t = [sb(p5, "x2t%d" % i, [128, 1024], F32) for i in range(2)]
            x2B = [Buf(), Buf()]
            ot = [sb(p5, "ot%d" % i, [128, 1024], F32) for i in range(2)]
            otB = [Buf(), Buf()]
            mixv = mix_s.rearrange("(k p) t -> p k t", p=128)
            wcnt = [0]

            def pro_a(tb, ti):
                tt = tb * 4 + ti
                i = tt % 2
                xb = tb % 2
                dma(QS, xt[i][:], x_d[tt * 128:(tt + 1) * 128, :], w=[xtB[i]])
                for hf_ in range(2):
                    bk = nbank()
                    for k in range(8):
                        op(PE, lambda k=k: nc.tensor.matmul(ps[bk][:, :], mb[:, k, ti * 128:(ti + 1) * 128],
                                                            wo[:, k, hf_ * 512:(hf_ + 1) * 512], start=(k == 0), stop=(k == 7)),
                           r=[mbB, woB], w=[psB[bk]])
                    hs = slice(hf_ * 512, (hf_ + 1) * 512)
                    op(DVE, lambda: nc.vector.tensor_tensor(out=x1[:, xb, ti, hs], in0=ps[bk][:, :], in1=g1b[:, hs], op=ALU.mult),
                       r=[psB[bk], modB], w=[x1B[xb][ti]])
                op(DVE, lambda: nc.vector.tensor_tensor(out=x1[:, xb, ti, :], in0=x1[:, xb, ti, :], in1=xt[i][:], op=ALU.add),
                   r=[x1B[xb][ti], xtB[i]], w=[x1B[xb][ti]])
                return norm_to_T(stB, x1[:, xb, ti, :], x1B[xb][ti], ss, rstd, tt, xn[ti], xnB[ti], junk, junkB, h2T, h2B[ti],
                                 slice(ti * 128, (ti + 1) * 128), scale2, sh2, tt % 2, xn_on_act=True)

            def phase_a(tb):
                for j in range(NJ):
                    wi = wcnt[0] % NW
                    wcnt[0] += 1
                    dma(QS, wgt[wi][:], wg_s[j * 128:(j + 1) * 128, :], r=[wsB], w=[wgB[wi]])
                    dma(QS, wut[wi][:], wu_s[j * 128:(j + 1) * 128, :], r=[wsB], w=[wuB[wi]])
                    bg_ = nbank()
                    for k in range(8):
                        op(PE, lambda k=k: nc.tensor.matmul(ps[bg_][:, :], wgt[wi][:, k * 128:(k + 1) * 128], h2T[:, k, :],
                                                            start=(k == 0), stop=(k == 7)), r=[wgB[wi]] + h2B, w=[psB[bg_]])
                    bu_ = nbank()
                    for k in range(8):
                        op(PE, lambda k=k: nc.tensor.matmul(ps[bu_][:, :], wut[wi][:, k * 128:(k + 1) * 128], h2T[:, k, :],
                                                            start=(k == 0), stop=(k == 7)), r=[wuB[wi]] + h2B, w=[psB[bu_]])
                    si = j % 2
                    op(ACT, lambda: nc.scalar.activation(out=sg[si][:], in_=ps[bg_][:, :], func=AF.Silu), r=[psB[bg_]], w=[sgB[si]])
                    op(DVE, lambda: nc.vector.tensor_tensor(out=act[:, j, :], in0=sg[si][:], in1=ps[bu_][:, :], op=ALU.mult),
                       r=[sgB[si], psB[bu_]], w=[actB[j]])

            def phase_b(tb, ti):
                tt = tb * 4 + ti
                i = tt % 2
                xb = tb % 2
                for hf_ in range(2):
                    bk = nbank()
                    hs = slice(hf_ * 512, (hf_ + 1) * 512)
                    for j in range(NJ):
                        op(PE, lambda j=j: nc.tensor.matmul(ps[bk][:, :], act[:, j, ti * 128:(ti + 1) * 128], wdT[:, j, hs],
                                                            start=(j == 0), stop=(j == NJ - 1)),
                           r=[actB[j], wdB], w=[psB[bk]])
                    op(DVE, lambda: nc.vector.tensor_tensor(out=x2t[i][:, hs], in0=ps[bk][:, :], in1=g2b[:, hs], op=ALU.mult),
                       r=[psB[bk], modB], w=[x2B[i]])
                op(DVE, lambda: nc.vector.tensor_tensor(out=x2t[i][:], in0=x2t[i][:], in1=x1[:, xb, ti, :], op=ALU.add),
                   r=[x2B[i], x1B[xb][ti]], w=[x2B[i]])
                op(ACT, lambda: nc.scalar.activation(out=junk[:], in_=x2t[i][:], func=AF.Square, accum_out=ss3[:, tt:tt + 1]),
                   r=[x2B[i]], w=[junkB, st3B])
                op(ACT, lambda: nc.scalar.activation(out=ss3[:, tt:tt + 1], in_=ss3[:, tt:tt + 1], func=AF.Ln, scale=1.0 / D, bias=epsT[:, 0:1]), r=[st3B, cB], w=[st3B])
                op(ACT, lambda: nc.scalar.activation(out=ss3[:, tt:tt + 1], in_=ss3[:, tt:tt + 1], func=AF.Exp, scale=-0.5), r=[st3B], w=[st3B])
                op(DVE, lambda: nc.vector.scalar_tensor_tensor(out=ot[i][:], in0=x2t[i][:], scalar=ss3[:, tt:tt + 1], in1=fngb[:],
                                                               op0=ALU.mult, op1=ALU.mult), r=[x2B[i], st3B, cB], w=[otB[i]])
                dma(QP, out_d[tt * 128:(tt + 1) * 128, :], ot[i][:], r=[otB[i]])

            dma(QS, mb[:], mixv[:, :, 0:512], r=[mixB], w=[mbB])
            for pb in [pro_a(0, ti) for ti in range(4)]:
                pb()
            for tb in range(NB):
                phase_a(tb)
                if tb + 1 < NB:
                    dma(QS, mb[:], mixv[:, :, (tb + 1) * 512:(tb + 2) * 512], r=[mixB], w=[mbB])
                for ti in range(4):
                    pb = pro_a(tb + 1, ti) if tb + 1 < NB else None
                    phase_b(tb, ti)
                    if pb is not None:
                        pb()
            cx.barrier()
    return nc, dbg


def _prep_shared(inp):
    f = np.float32
    w_ada = inp["w_ada"][0]
    sh = {}
    sh["w_ada"] = np.ascontiguousarray(w_ada.reshape(8, 128, 12, 512).transpose(2, 1, 0, 3))
    sh["b_adaT"] = np.ascontiguousarray(inp["b_ada"][0].reshape(48, 128).T)
    sh["gmixT"] = np.ascontiguousarray(inp["norm_mix_g"][0].reshape(8, 128).T)
    sh["gffnT"] = np.ascontiguousarray(inp["norm_ffn_g"][0].reshape(8, 128).T)
    sh["w_in"] = np.ascontiguousarray(inp["w_in"][0].reshape(8, 128, 2832).transpose(1, 0, 2))
    g10 = np.concatenate([np.tile(inp["q_norm_g"][0], 8), np.tile(inp["k_norm_g"][0], 2)])
    sh["g10"] = np.ascontiguousarray(np.broadcast_to(g10[None, :], (128, 640))).astype(f)
    tok = np.arange(S)
    row = (tok // 64).astype(f)
    col = (tok % 64).astype(f)
    inv_freq = (f(10000.0) ** (-np.arange(0, 32, 2, dtype=f) / f(32))).astype(f)
    ang_r = (row[:, None] * inv_freq[None, :]).astype(f)
    ang_c = (col[:, None] * inv_freq[None, :]).astype(f)
    cos = np.concatenate([np.cos(ang_r), np.cos(ang_c)], axis=1).astype(f)
    sin = np.concatenate([np.sin(ang_r), np.sin(ang_c)], axis=1).astype(f)
    sh["ropecos"] = np.ascontiguousarray(cos.reshape(NT, 128, 32).transpose(1, 0, 2))
    sh["ropesin"] = np.ascontiguousarray(sin.reshape(NT, 128, 32).transpose(1, 0, 2))
    sh["convwT"] = np.ascontiguousarray(inp["conv_w"][0].reshape(5, 8, 128).transpose(2, 1, 0))
    sh["convbT"] = np.ascontiguousarray(inp["conv_b"][0].reshape(8, 128).T)
    sh["gateb"] = np.ascontiguousarray(np.broadcast_to(inp["gate_b"][0][None, :], (128, 16))).astype(f)
    sh["mngT"] = np.ascontiguousarray(inp["mlstm_norm_g"][0].reshape(4, 128).T)
    sh["w_out"] = np.ascontiguousarray(inp["w_out"][0].reshape(8, 128, 1024).transpose(1, 0, 2))
    for nm in ("w_gate", "w_up"):
        w = inp[nm][0].reshape(8, 128, NJ, 128).transpose(2, 1, 0, 3)
        sh[nm] = np.ascontiguousarray(w.reshape(NJ * 128, 1024))
    sh["w_down"] = np.ascontiguousarray(inp["w_down"][0].reshape(NJ, 128, 1024).transpose(1, 0, 2))
    sh["fngb"] = np.ascontiguousarray(np.broadcast_to(inp["final_norm_g"][None, :], (128, 1024))).astype(f)
    sh["ident"] = np.eye(128, dtype=f)
    sh["triu"] = np.triu(np.ones((128, 128), dtype=f))
    sh["tril"] = np.tril(np.ones((128, 128), dtype=f))
    return sh


def kernel(**inputs):
    inp = {k: np.asarray(v) for k, v in inputs.items()}
    B = inp["x"].shape[0]
    nc, _ = build()
    sh = _prep_shared(inp)
    in_maps = []
    for b in range(B):
        m = dict(sh)
        m["x"] = np.ascontiguousarray(inp["x"][b])
        m["cT"] = np.ascontiguousarray(inp["c"][b].reshape(8, 128).T)
        in_maps.append(m)
    res = run_bass_kernel_spmd(nc, in_maps, core_ids=list(range(B)))
    return np.stack([np.asarray(r["out"]) for r in res.results], axis=0).astype(np.float32)
```

```python
import math
from contextlib import ExitStack

import numpy as np
import concourse.bass as bass
import concourse.mybir as mybir
from concourse.bass_utils import run_bass_kernel_spmd

F32 = mybir.dt.float32
BF16 = mybir.dt.bfloat16
AF = mybir.ActivationFunctionType
ALU = mybir.AluOpType
AX = mybir.AxisListType

S = 4096
D = 1024
NT = 32
NB = 8
NJ = 22
EPS = 1e-6
DEBUG = False
STOP_AFTER = 99
INTERLEAVE = False
PACKPV = False


class Sem:
    def __init__(self, h):
        self.h = h
        self.val = 0


class Buf:
    __slots__ = ("w", "r", "excl")

    def __init__(self, excl=False):
        self.w = None
        self.r = {}
        self.excl = excl


class Eng:
    def __init__(self, e, sem, selfsync=True):
        self.e = e
        self.sem = sem
        self.selfsync = selfsync
        self.waited = {}

    def wait_tok(self, tok):
        if tok is None:
            return
        sem, val = tok
        if sem is self.sem and not self.selfsync:
            return
        if self.waited.get(id(sem), 0) >= val:
            return
        self.e.wait_ge(sem.h, val)
        self.waited[id(sem)] = val


class DmaQ:
    def __init__(self, eng, sems):
        self.eng = eng
        self.sems = sems
        self.i = 0


class Ctx:
    def __init__(self, nc, es):
        self.nc = nc
        self.es = es
        self.nsem = 0
        self.all_sems = []
        self.PE = Eng(nc.tensor, self.new_sem("pe"), selfsync=False)
        self.ACT = Eng(nc.scalar, self.new_sem("act"))
        self.DVE = Eng(nc.vector, self.new_sem("dve"))
        self.POOL = Eng(nc.gpsimd, self.new_sem("pool"))
        self.SP = Eng(nc.sync, self.new_sem("sp"))
        self.engs = [self.PE, self.ACT, self.DVE, self.POOL, self.SP]
        self.qsp = DmaQ(self.SP, [self.new_sem("dsp%d" % i) for i in range(40)])
        self.qpool = DmaQ(self.POOL, [self.new_sem("dpl%d" % i) for i in range(40)])
        self.bank_rr = 0

    def new_sem(self, name):
        s = Sem(self.es.enter_context(self.nc.semaphore(name)))
        self.all_sems.append(s)
        return s

    def _deps(self, E, r, w):
        for b in r:
            E.wait_tok(b.w)
        for b in w:
            E.wait_tok(b.w)
            for tok in b.r.values():
                E.wait_tok(tok)

    def _mark(self, tok, r, w):
        for b in r:
            b.r[id(tok[0])] = tok
        for b in w:
            b.w = tok
            b.r = {}

    def op(self, E, fn, r=(), w=()):
        ex = [b for b in r if b.excl]
        if ex:
            r = [b for b in r if not b.excl]
            w = list(w) + ex
        self._deps(E, r, w)
        inst = fn()
        E.sem.val += 1
        inst.then_inc(E.sem.h, 1)
        self._mark((E.sem, E.sem.val), r, w)

    def dma(self, Q, out, in_, r=(), w=()):
        s = Q.sems[Q.i % len(Q.sems)]
        Q.i += 1
        E = Q.eng
        if s.val > 0:
            E.wait_tok((s, s.val))
        self._deps(E, r, w)
        inst = E.e.dma_start(out=out, in_=in_)
        s.val += 16
        inst.then_inc(s.h, 16)
        self._mark((s, s.val), r, w)

    def barrier(self):
        for E in self.engs:
            for s in self.all_sems:
                if s.val > 0:
                    E.wait_tok((s, s.val))


def build():
    nc = bass.Bass("TRN2", target_bir_lowering=False)

    def din(name, shape, dt=F32):
        return nc.dram_tensor(name, list(shape), dt, kind="ExternalInput").ap()

    x_d = din("x", [S, D])
    cT_d = din("cT", [128, 8])
    wada_d = din("w_ada", [12, 128, 8, 512])
    bada_d = din("b_adaT", [128, 48])
    gmix_d = din("gmixT", [128, 8])
    gffn_d = din("gffnT", [128, 8])
    win_d = din("w_in", [128, 8, 2832])
    g10_d = din("g10", [128, 640])
    cos_d = din("ropecos", [128, NT, 32])
    sin_d = din("ropesin", [128, NT, 32])
    convw_d = din("convwT", [128, 8, 5])
    convb_d = din("convbT", [128, 8])
    gateb_d = din("gateb", [128, 16])
    mng_d = din("mngT", [128, 4])
    wout_d = din("w_out", [128, 8, 1024])
    wg_d = din("w_gate", [NJ * 128, 1024])
    wu_d = din("w_up", [NJ * 128, 1024])
    wd_d = din("w_down", [128, NJ, 1024])
    fng_d = din("fngb", [128, 1024])
    ident_d = din("ident", [128, 128])
    triu_d = din("triu", [128, 128])
    tril_d = din("tril", [128, 128])
    out_d = nc.dram_tensor("out", [S, D], F32, kind="ExternalOutput").ap()
    mix_s = nc.dram_tensor("mix_s", [D, S], BF16).ap()
    wg_s = nc.dram_tensor("wg_s", [NJ * 128, 1024], BF16).ap()
    wu_s = nc.dram_tensor("wu_s", [NJ * 128, 1024], BF16).ap()
    hT_s = nc.dram_tensor("hT_s", [NB, 128, 8, 512], BF16).ap()
    dbg = {}
    if DEBUG:
        dbg["hT"] = nc.dram_tensor("dbg_hT", [128, 8, S], BF16, kind="ExternalOutput").ap()
        dbg["QT"] = nc.dram_tensor("dbg_QT", [128, 4, S], BF16, kind="ExternalOutput").ap()
        dbg["KT"] = nc.dram_tensor("dbg_KT", [128, 2, S], BF16, kind="ExternalOutput").ap()
        dbg["mix"] = nc.dram_tensor("dbg_mix", [D, S], BF16, kind="ExternalOutput").ap()
        dbg["modT"] = nc.dram_tensor("dbg_modT", [128, 48], F32, kind="ExternalOutput").ap()

    with ExitStack() as es:
        cx = Ctx(nc, es)
        PE, ACT, DVE, POOL, SP = cx.PE, cx.ACT, cx.DVE, cx.POOL, cx.SP
        op, dma = cx.op, cx.dma
        QS, QP = cx.qsp, cx.qpool

        def sb(stack, name, shape, dt):
            return stack.enter_context(nc.sbuf_tensor("s_" + name, list(shape), dt))

        psall = es.enter_context(nc.psum_tensor("psall", [128, 4096], F32))
        psall_b = psall.bitcast(BF16)
        ps = [psall[:, i * 512:(i + 1) * 512] for i in range(8)]
        psb = [psall_b[:, i * 1024:(i + 1) * 1024] for i in range(8)]
        psB = [Buf(excl=True) for _ in range(8)]

        cx.banks = list(range(8))

        def nbank():
            i = cx.banks[cx.bank_rr % len(cx.banks)]
            cx.bank_rr += 1
            return i

        ident = sb(es, "ident", [128, 128], BF16)
        mask2 = sb(es, "mask2", [128, 2, 128], BF16)
        masku = mask2[:, 0, :]
        maskl = mask2[:, 1, :]
        triu = sb(es, "triu_f", [128, 128], F32)
        tril = sb(es, "tril_f", [128, 128], F32)
        ones_f = sb(es, "ones_f", [128, 128], F32)
        modT = sb(es, "modT", [128, 48], F32)
        scale1 = sb(es, "scale1", [128, 8], F32)
        scale2 = sb(es, "scale2", [128, 8], F32)
        ident_f = sb(es, "ident_f", [128, 128], F32)
        GT = sb(es, "GT", [128, NT, 16], F32)
        lns = sb(es, "lns", [128, 1], F32)
        epsT = sb(es, "epsT", [128, 1], F32)
        cTt = sb(es, "cTt", [128, 8], F32)
        cond = sb(es, "cond", [128, 8], BF16)
        gmixT = sb(es, "gmixT", [128, 8], F32)
        gffnT = sb(es, "gffnT", [128, 8], F32)
        tmp8 = sb(es, "tmp8", [128, 8], F32)
        badaT = sb(es, "badaT", [128, 48], F32)
        mrow = [sb(es, "mrow%d" % i, [1, 512], F32) for i in range(2)]
        cB = Buf()
        modB = Buf()
        GTB = Buf()
        mixB = Buf()
        wsB = Buf()

        dma(QP, ident[:], ident_d, w=[cB])
        dma(QP, mask2[:, 0, :], triu_d, w=[cB])
        dma(QP, mask2[:, 1, :], tril_d, w=[cB])
        dma(QS, triu[:], triu_d, w=[cB])
        dma(QS, tril[:], tril_d, w=[cB])
        dma(QS, ident_f[:], ident_d, w=[cB])
        op(DVE, lambda: nc.vector.memset(ones_f[:], 1.0), w=[cB])
        op(DVE, lambda: nc.vector.memset(lns[:], math.log(128.0 ** -0.5)), w=[cB])
        op(DVE, lambda: nc.vector.memset(epsT[:], EPS), w=[cB])

        with ExitStack() as pm:
            qs = ExitStack()
            QT = sb(qs, "QT", [128, 4, S], BF16)
            KT = sb(qs, "KT", [128, 2, S], BF16)
            Vaug = sb(qs, "Vaug", [128, NT, 2, 65], BF16)
            QTB, KTB, VB = Buf(), Buf(), Buf()
            hs = ExitStack()
            hT = sb(hs, "hT", [128, 8, S], BF16)
            hTsB = Buf()
            hTB = [Buf() for _ in range(NT)]
            p0 = ExitStack()
            mrB = [Buf(), Buf()]
            NWA = 4
            wa = [sb(p0, "wa%d" % i, [128, 8, 512], BF16) for i in range(NWA)]
            waB = [Buf() for _ in range(NWA)]
            dma(QS, cTt[:], cT_d, w=[cB])
            dma(QS, gmixT[:], gmix_d, w=[cB])
            dma(QS, gffnT[:], gffn_d, w=[cB])
            dma(QS, badaT[:], bada_d, w=[cB])
            op(ACT, lambda: nc.scalar.activation(out=cond[:], in_=cTt[:], func=AF.Silu), r=[cB], w=[modB])
            for n in range(NWA):
                dma(QP, wa[n][:], wada_d[n], w=[waB[n]])

            def ada_block(n, wbuf=None, wbufB=None, defer=False):
                if wbuf is None:
                    wbuf, wbufB = wa[n % NWA], waB[n % NWA]
                bk = nbank()
                for k in range(8):
                    op(PE, lambda k=k: nc.tensor.matmul(ps[bk][0:1, :], cond[:, k:k + 1], wbuf[:, k, :], start=(k == 0), stop=(k == 7)),
                       r=[modB, wbufB], w=[psB[bk]])
                def fin():
                    m = n % 2
                    op(DVE, lambda: nc.vector.tensor_copy(mrow[m][0:1, :], ps[bk][0:1, :]), r=[psB[bk]], w=[mrB[m]])
                    bk2 = nbank()
                    for q in range(4):
                        op(PE, lambda q=q: nc.tensor.matmul(ps[bk2][:, q:q + 1], mrow[m][0:1, q * 128:(q + 1) * 128], ones_f[0:1, 0:1], start=True, stop=True),
                           r=[mrB[m], cB], w=[psB[bk2]])
                    op(DVE, lambda: nc.vector.tensor_tensor(out=modT[:, 4 * n:4 * n + 4], in0=ps[bk2][:, 0:4], in1=badaT[:, 4 * n:4 * n + 4], op=ALU.add),
                       r=[psB[bk2], cB], w=[modB])
                if defer:
                    return fin
                fin()

            def mk_scale(dst, c0, g):
                op(DVE, lambda: nc.vector.tensor_scalar(tmp8[:], modT[:, c0:c0 + 8], 1.0, None, op0=ALU.add), r=[modB], w=[modB])
                op(DVE, lambda: nc.vector.tensor_tensor(out=dst[:], in0=tmp8[:], in1=g[:], op=ALU.mult), r=[modB, cB], w=[modB])

            for fn_ in [ada_block(n, defer=True) for n in range(4)]:
                fn_()
            mk_scale(scale1, 8, gmixT)
            sh1, sh2 = modT[:, 0:8], modT[:, 24:32]

            def phase0b():
                mk_scale(scale2, 32, gffnT)
                if DEBUG:
                    dma(QS, dbg["modT"], modT[:], r=[modB])

            def norm_to_T(stack_bufs, xsrc, xsrcB, ssT, rstdT, col, xn, xnB, junk, junkB, dstT, dstB, dst_cols, scaleT, biasT, parity, xn_on_act=False):
                op(ACT, lambda: nc.scalar.activation(out=junk[:], in_=xsrc, func=AF.Square, accum_out=ssT[:, col:col + 1]),
                   r=[xsrcB], w=[junkB, stack_bufs])
                op(ACT, lambda: nc.scalar.activation(out=rstdT[:, col:col + 1], in_=ssT[:, col:col + 1], func=AF.Ln, scale=1.0 / D, bias=epsT[:, 0:1]), r=[stack_bufs, cB], w=[stack_bufs])
                op(ACT, lambda: nc.scalar.activation(out=rstdT[:, col:col + 1], in_=rstdT[:, col:col + 1], func=AF.Exp, scale=-0.5), r=[stack_bufs], w=[stack_bufs])
                if xn_on_act:
                    op(ACT, lambda: nc.scalar.activation(out=xn[:], in_=xsrc, func=AF.Identity, scale=rstdT[:, col:col + 1]),
                       r=[xsrcB, stack_bufs], w=[xnB])
                else:
                    op(DVE, lambda: nc.vector.tensor_scalar(xn[:], xsrc, rstdT[:, col:col + 1], None, op0=ALU.mult),
                       r=[xsrcB, stack_bufs], w=[xnB])
                def part_b():
                    bk = nbank()
                    for k in range(8):
                        op(PE, lambda k=k, bk=bk: nc.tensor.transpose(psb[bk][:, k * 128:(k + 1) * 128], xn[:, k * 128:(k + 1) * 128], ident[:]),
                           r=[xnB, cB], w=[psB[bk]])
                    for k in range(8):
                        if parity == 0:
                            op(ACT, lambda k=k, bk=bk: nc.scalar.activation(out=dstT[:, k, dst_cols], in_=psb[bk][:, k * 128:(k + 1) * 128],
                                                                          func=AF.Identity, scale=scaleT[:, k:k + 1], bias=biasT[:, k:k + 1]),
                               r=[psB[bk], modB], w=[dstB])
                        else:
                            op(DVE, lambda k=k, bk=bk: nc.vector.tensor_scalar(dstT[:, k, dst_cols], psb[bk][:, k * 128:(k + 1) * 128],
                                                                             scaleT[:, k:k + 1], biasT[:, k:k + 1], op0=ALU.mult, op1=ALU.add),
                               r=[psB[bk], modB], w=[dstB])
                return part_b

            with ExitStack() as p1:
                xt = [sb(p1, "xt%d" % i, [128, 1024], F32) for i in range(3)]
                xtB = [Buf() for _ in range(3)]
                xn = [sb(p1, "xn%d" % i, [128, 1024], BF16) for i in range(2)]
                xnB = [Buf(), Buf()]
                junk = sb(p1, "junk", [128, 1024], BF16)
                junkB = Buf()
                ss = sb(p1, "ss", [128, NT], F32)
                rstd = sb(p1, "rstd", [128, NT], F32)
                stB = Buf()
                pend_b = None
                for tt in range(NT):
                    i = tt % 3
                    dma(QS, xt[i][:], x_d[tt * 128:(tt + 1) * 128, :], w=[xtB[i]])
                    pb = norm_to_T(stB, xt[i][:], xtB[i], ss, rstd, tt, xn[tt % 2], xnB[tt % 2], junk, junkB, hT, hTB[tt],
                                   slice(tt * 128, (tt + 1) * 128), scale1, sh1, 1, xn_on_act=True)
                    if pend_b is not None:
                        pend_b()
                    pend_b = pb
                pend_b()
                if DEBUG:
                    dma(QS, dbg["hT"], hT[:], r=hTB)
                cx.barrier()
            p0.close()
            if STOP_AFTER <= 1:
                return nc, dbg

            def mlstm_gen(p4):
                LP = sb(p4, "LP", [128, 2, NT, 4], F32)
                IA = sb(p4, "IA", [128, 2, NT, 4], F32)
                Aa = sb(p4, "Aa", [128, 2, NT, 4], F32)
                EBs = sb(p4, "EBs", [128, 2, NT, 4], F32)
                EG = sb(p4, "EG", [128, 2, NT, 4], F32)
                gB = Buf()
                convw = sb(p4, "convw", [128, 8, 5], F32)
                convb = sb(p4, "convb", [128, 8], F32)
                mng = sb(p4, "mng", [128, 4], F32)
                dma(QS, convw[:], convw_d, w=[cB])
                dma(QS, convb[:], convb_d, w=[cB])
                dma(QS, mng[:], mng_d, w=[cB])
                for d in range(2):
                    fc = 4 + 8 * d
                    op(ACT, lambda d=d, fc=fc: nc.scalar.activation(out=LP[:, d], in_=GT[:, :, fc:fc + 4], func=AF.Exp, scale=-1.0), r=[GTB], w=[gB])
                op(ACT, lambda: nc.scalar.activation(out=LP[:], in_=LP[:], func=AF.Ln, bias=1.0), r=[gB], w=[gB])
                for d in range(2):
                    ic = 8 * d
                    bk = nbank()
                    tri_ = triu if d == 0 else tril
                    op(PE, lambda: nc.tensor.matmul(ps[bk][:, 0:128], tri_[:], LP[:, d].rearrange("p c h -> p (c h)"), start=True, stop=True),
                       r=[gB, cB], w=[psB[bk]])
                    op(DVE, lambda: nc.vector.tensor_tensor(out=IA[:, d], in0=ps[bk][:, 0:128].rearrange("p (c h) -> p c h", h=4),
                                                            in1=GT[:, :, ic:ic + 4], op=ALU.add), r=[psB[bk], GTB], w=[gB])
                    op(ACT, lambda: nc.scalar.activation(out=EBs[:, d], in_=ps[bk][:, 0:128].rearrange("p (c h) -> p c h", h=4),
                                                         func=AF.Exp, scale=-1.0, bias=lns[:, 0:1]), r=[psB[bk], cB], w=[gB])
                bg = nbank()
                op(PE, lambda: nc.tensor.matmul(ps[bg][:, 0:256], ones_f[:], LP[:].rearrange("p d c h -> p (d c h)"), start=True, stop=True),
                   r=[gB, cB], w=[psB[bg]])
                op(ACT, lambda: nc.scalar.activation(out=EG[:].rearrange("p d c h -> p (d c h)"), in_=ps[bg][:, 0:256], func=AF.Exp, scale=-1.0),
                   r=[psB[bg]], w=[gB])
                op(ACT, lambda: nc.scalar.activation(out=Aa[:], in_=IA[:], func=AF.Exp), r=[gB], w=[gB])
                yield

                wm_ = [sb(p4, "wm%d" % i, [128, 8, 4, 128], BF16) for i in range(2)]
                wmB_ = [Buf(), Buf()]
                qkraw_ = [sb(p4, "qkraw%d" % i, [128, 2, NT * 129], BF16) for i in range(2)]
                qkpB_ = [[Buf(), Buf()], [Buf(), Buf()]]
                QKm = sb(p4, "QKm", [128, 2, S], BF16)
                QKB = [Buf(), Buf()]
                sigo_ = [sb(p4, "sigo%d" % i, [128, S], BF16) for i in range(2)]
                sigB_ = [Buf(), Buf()]
                Ktok = sb(p4, "Ktok", [128, NT, 128], BF16)
                KtB = Buf()
                Vm_ = [sb(p4, "Vm%d" % i, [128, NT, 129], BF16) for i in range(2)]
                VmB_ = [Buf(), Buf()]
                Cst = sb(p4, "Cst", [128, 2, NT, 129], BF16)
                CsB = [Buf(), Buf()]
                Pch = sb(p4, "Pch", [128, 2, 2, 129], F32)
                PcB = [[Buf(), Buf()], [Buf(), Buf()]]
                Dg = sb(p4, "Dg", [128, 2, 5, 128], BF16)
                DgB = Buf()
                Sm2 = [sb(p4, "Sm2%d" % i, [128, 2, 128], BF16) for i in range(2)]
                Sm = [[Sm2[i][:, d, :] for i in range(2)] for d in range(2)]
                SmB_ = [Buf(), Buf()]
                SmB = [[SmB_[0], SmB_[1]], [SmB_[0], SmB_[1]]]
                FG = 16
                Hraw = sb(p4, "Hraw", [128, FG, 2, 129], F32)
                HrB = Buf()
                hf = sb(p4, "hf", [128, FG, 128], F32)
                hb = sb(p4, "hb", [128, FG, 128], F32)
                rr = sb(p4, "rr", [128, FG, 2], F32)
                ssh = sb(p4, "ssh", [128, FG], F32)
                hn = sb(p4, "hn", [128, FG, 128], BF16)
                fB = Buf()
                hnB = Buf()
                y4 = [sb(p4, "y4%d" % i, [128, FG * 128], BF16) for i in range(2)]
                y4B = [Buf(), Buf()]
                hblk = [sb(p4, "hblk%d" % i, [128, 8, 512], BF16) for i in range(2)]
                hblkB = [Buf(), Buf()]
                hcnt = [0]
                for i_ in range(2):
                    op(DVE, lambda i_=i_: nc.vector.memset(Vm_[i_][:, :, 128:129], 1.0), w=[VmB_[i_]])
                op(DVE, lambda: nc.vector.memset(Cst[:, 0, 0, :], 0.0), w=[CsB[0]])
                op(DVE, lambda: nc.vector.memset(Cst[:, 1, NT - 1, :], 0.0), w=[CsB[1]])

                def proj_units(j):
                    sl_ = j % 2
                    wm, wmB = wm_[sl_], wmB_[sl_]
                    qkpre, qkpB = qkraw_[sl_][:, :, 0:S + 4], qkpB_[sl_]
                    sigo, sigB = sigo_[sl_], sigB_[sl_]
                    Vm, VmB = Vm_[sl_], VmB_[sl_]
                    cols = (768 + j * 128, 1280 + j * 128, 1792 + j * 128, 2304 + j * 128)
                    for wi, c0 in enumerate(cols):
                        dma(QP, wm[:, :, wi, :], win_d[:, :, c0:c0 + 128], w=[wmB])
                    op(DVE, lambda: nc.vector.memset(qkpre[:, :, 0:2], 0.0), w=qkpB)
                    op(DVE, lambda: nc.vector.memset(qkpre[:, :, S + 2:S + 4], 0.0), w=qkpB)
                    for tb in range(NB):
                        hb_ = hblk[hcnt[0] % 2]
                        hbB = hblkB[hcnt[0] % 2]
                        hcnt[0] += 1
                        dma(QS, hb_[:], hT_s[tb], r=[hTsB], w=[hbB])
                        for w2 in range(3):
                            wi = (0, 1, 3)[w2]
                            bk = nbank()
                            for k in range(8):
                                op(PE, lambda k=k: nc.tensor.matmul(ps[bk][:, :], wm[:, k, wi, :], hb_[:, k, :], start=(k == 0), stop=(k == 7)),
                                   r=[wmB, hbB], w=[psB[bk]])
                            if w2 < 2:
                                op(ACT, lambda: nc.scalar.copy(out=qkpre[:, w2, 2 + tb * 512:2 + (tb + 1) * 512], in_=ps[bk][:, :]),
                                   r=[psB[bk]], w=[qkpB[w2]])
                            else:
                                op(ACT, lambda: nc.scalar.activation(out=sigo[:, tb * 512:(tb + 1) * 512], in_=ps[bk][:, :], func=AF.Sigmoid),
                                   r=[psB[bk]], w=[sigB])
                            yield
                        bk = nbank()
                        for ti in range(4):
                            for k in range(8):
                                op(PE, lambda k=k, ti=ti: nc.tensor.matmul(ps[bk][:, ti * 128:(ti + 1) * 128], hb_[:, k, ti * 128:(ti + 1) * 128],
                                                                           wm[:, k, 2, :], start=(k == 0), stop=(k == 7)),
                                   r=[wmB, hbB], w=[psB[bk]])
                        op(DVE, lambda: nc.vector.tensor_copy(Vm[:, tb * 4:tb * 4 + 4, 0:128], ps[bk][:, :].rearrange("p (t d) -> p t d", t=4)),
                           r=[psB[bk]], w=[VmB])
                        yield

                fill = proj_units(0)
                for _ in fill:
                    yield
                for j in range(4):
                    sl_ = j % 2
                    qkraw = qkraw_[sl_]
                    qkpre, qkpB = qkraw[:, :, 0:S + 4], qkpB_[sl_]
                    sigo, sigB = sigo_[sl_], sigB_[sl_]
                    Vm, VmB = Vm_[sl_], VmB_[sl_]
                    Vt = qkraw[:].rearrange("p d (c v) -> p d c v", v=129)
                    VtB = qkpB
                    for w2 in range(2):
                        ch = j + 4 * w2
                        for k5 in range(5):
                            op(DVE, lambda w2=w2, ch=ch, k5=k5: nc.vector.tensor_scalar(Dg[:, w2, k5, :], ident[:], convw[:, ch, k5:k5 + 1], None, op0=ALU.mult),
                               r=[cB], w=[DgB])
                    for w2 in range(2):
                        ch = j + 4 * w2
                        for tb in range(NB):
                            bk = nbank()
                            for k5 in range(5):
                                op(PE, lambda k5=k5, bk=bk, w2=w2, tb=tb: nc.tensor.matmul(ps[bk][:, :], Dg[:, w2, k5, :],
                                                                                          qkpre[:, w2, tb * 512 + k5:tb * 512 + k5 + 512],
                                                                                          start=(k5 == 0), stop=(k5 == 4)),
                                   r=[DgB, qkpB[w2]], w=[psB[bk]])
                            op(ACT, lambda bk=bk, w2=w2, tb=tb, ch=ch: nc.scalar.activation(out=QKm[:, w2, tb * 512:(tb + 1) * 512], in_=ps[bk][:, :],
                                                                                           func=AF.Silu, bias=convb[:, ch:ch + 1]),
                               r=[psB[bk], cB], w=[QKB[w2]])
                            yield
                    for c8 in range(NT // 8):
                        bk = nbank()
                        for ci in range(8):
                            c = c8 * 8 + ci
                            op(PE, lambda bk=bk, ci=ci, c=c: nc.tensor.transpose(psb[bk][:, ci * 128:(ci + 1) * 128], QKm[:, 1, c * 128:(c + 1) * 128], ident[:]),
                               r=[QKB[1], cB], w=[psB[bk]])
                        op(DVE, lambda bk=bk, c8=c8: nc.vector.tensor_copy(Ktok[:, c8 * 8:c8 * 8 + 8, :], psb[bk][:, :].rearrange("p (c d) -> p c d", c=8)),
                           r=[psB[bk]], w=[KtB])
                        yield
                    for d in range(2):
                        op(DVE, lambda d=d: nc.vector.tensor_tensor(out=Vt[:, d], in0=Vm[:], in1=Aa[:, d, :, j].unsqueeze(2).broadcast_to([128, NT, 129]), op=ALU.mult),
                           r=[VmB, gB], w=[VtB[d]])
                    fill = proj_units(j + 1) if j < 3 else iter(())
                    for stp in range(NT - 1):
                        yield
                        next(fill, None)
                        for d in range(2):
                            c = stp if d == 0 else NT - 1 - stp
                            cn = c + 1 if d == 0 else c - 1
                            bk = nbank()
                            op(PE, lambda bk=bk, c=c, d=d: nc.tensor.matmul(ps[bk][:, 0:129], Ktok[:, c, :], Vt[:, d, c, :], start=True, stop=True),
                               r=[KtB, VtB[d]], w=[psB[bk]])
                            cur, prv = stp % 2, (stp + 1) % 2
                            if stp == 0:
                                op(DVE, lambda bk=bk, d=d, cur=cur: nc.vector.tensor_copy(Pch[:, d, cur, :], ps[bk][:, 0:129]), r=[psB[bk]], w=[PcB[d][cur]])
                            else:
                                cp = c - 1 if d == 0 else c + 1
                                op(DVE, lambda bk=bk, d=d, cur=cur, prv=prv, cp=cp: nc.vector.scalar_tensor_tensor(
                                    out=Pch[:, d, cur, :], in0=Pch[:, d, prv, :], scalar=EG[:, d, cp, j:j + 1], in1=ps[bk][:, 0:129],
                                    op0=ALU.mult, op1=ALU.add), r=[PcB[d][prv], psB[bk], gB], w=[PcB[d][cur]])
                            op(ACT, lambda d=d, cur=cur, c=c, cn=cn: nc.scalar.activation(out=Cst[:, d, cn, :], in_=Pch[:, d, cur, :], func=AF.Identity,
                                                                                         scale=EG[:, d, c, j:j + 1]),
                               r=[PcB[d][cur], gB], w=[CsB[d]])
                    def emit_s(c):
                        bs_ = nbank()
                        cs_ = slice(c * 128, (c + 1) * 128)
                        op(PE, lambda: nc.tensor.matmul(ps[bs_][:, 0:128], QKm[:, 1, cs_], QKm[:, 0, cs_], start=True, stop=True),
                           r=[QKB[0], QKB[1]], w=[psB[bs_]])
                        return bs_

                    look = len(cx.banks) == 8
                    bs_next = emit_s(0) if look else None
                    for c in range(NT):
                        yield
                        next(fill, None)
                        csl = slice(c * 128, (c + 1) * 128)
                        i = c % 2
                        if look:
                            bs = bs_next
                            if c + 1 < NT:
                                bs_next = emit_s(c + 1)
                        else:
                            bs = emit_s(c)
                        op(DVE, lambda bs=bs: nc.vector.tensor_tensor(out=Sm2[i][:], in0=ps[bs][:, 0:128].unsqueeze(1).broadcast_to([128, 2, 128]),
                                                                      in1=mask2[:], op=ALU.mult),
                           r=[psB[bs], cB], w=[SmB_[i]])
                        for d in range(2):
                            bn = nbank()
                            op(PE, lambda d=d, bn=bn: nc.tensor.matmul(ps[bn][:, 0:129], Sm[d][i], Vt[:, d, c, :], start=True, stop=False),
                               r=[SmB[d][i], VtB[d]], w=[psB[bn]])
                            op(PE, lambda d=d, bn=bn: nc.tensor.matmul(ps[bn][:, 0:129], QKm[:, 0, csl], Cst[:, d, c, :], start=False, stop=True),
                               r=[QKB[0], CsB[d]], w=[psB[bn]])
                            op(ACT, lambda d=d, bn=bn: nc.scalar.copy(out=Hraw[:, c % FG, d, :], in_=ps[bn][:, 0:129]), r=[psB[bn]], w=[HrB])
                        if c % FG == FG - 1:
                            c0 = c - (FG - 1)
                            ebv = EBs[:, :, c0:c0 + FG, j].rearrange("p d c -> p c d")
                            op(ACT, lambda: nc.scalar.activation(out=rr[:], in_=Hraw[:, :, :, 128], func=AF.Abs), r=[HrB], w=[fB])
                            op(DVE, lambda ebv=ebv: nc.vector.tensor_tensor(out=rr[:], in0=rr[:], in1=ebv, op=ALU.mult), r=[fB, gB], w=[fB])
                            op(DVE, lambda: nc.vector.tensor_scalar(rr[:], rr[:], 1.0, None, op0=ALU.max), r=[fB], w=[fB])
                            op(DVE, lambda: nc.vector.reciprocal(rr[:], rr[:]), r=[fB], w=[fB])
                            op(DVE, lambda ebv=ebv: nc.vector.tensor_tensor(out=rr[:], in0=rr[:], in1=ebv, op=ALU.mult), r=[fB, gB], w=[fB])
                            op(DVE, lambda: nc.vector.tensor_tensor(out=hf[:], in0=Hraw[:, :, 0, 0:128], in1=rr[:, :, 0:1].broadcast_to([128, FG, 128]), op=ALU.mult),
                               r=[HrB, fB], w=[fB])
                            op(DVE, lambda: nc.vector.tensor_tensor(out=hb[:], in0=Hraw[:, :, 1, 0:128], in1=rr[:, :, 1:2].broadcast_to([128, FG, 128]), op=ALU.mult),
                               r=[HrB, fB], w=[fB])
                            op(DVE, lambda: nc.vector.tensor_tensor(out=hf[:], in0=hf[:], in1=hb[:], op=ALU.add), r=[fB], w=[fB])
                            op(DVE, lambda: nc.vector.tensor_tensor(out=hb[:], in0=hf[:], in1=hf[:], op=ALU.mult), r=[fB], w=[fB])
                            op(DVE, lambda: nc.vector.tensor_reduce(out=ssh[:], in_=hb[:], axis=AX.X, op=ALU.add), r=[fB], w=[fB])
                            op(ACT, lambda: nc.scalar.activation(out=ssh[:], in_=ssh[:], func=AF.Ln, scale=1.0 / 128, bias=epsT[:, 0:1]), r=[fB, cB], w=[fB])
                            op(ACT, lambda: nc.scalar.activation(out=ssh[:], in_=ssh[:], func=AF.Exp, scale=-0.5), r=[fB], w=[fB])
                            op(DVE, lambda: nc.vector.tensor_tensor(out=hn[:], in0=hf[:], in1=ssh[:].unsqueeze(2).broadcast_to([128, FG, 128]), op=ALU.mult),
                               r=[fB], w=[hnB])
                            yi = (c // FG) % 2
                            for c8 in range(FG // 8):
                                bt = nbank()
                                for ci in range(8):
                                    op(PE, lambda bt=bt, ci=ci, c8=c8: nc.tensor.transpose(psb[bt][:, ci * 128:(ci + 1) * 128], hn[:, c8 * 8 + ci, :], ident[:]),
                                       r=[hnB, cB], w=[psB[bt]])
                                op(DVE, lambda bt=bt, c8=c8: nc.vector.scalar_tensor_tensor(out=y4[yi][:, c8 * 1024:(c8 + 1) * 1024], in0=psb[bt][:, :],
                                                                                           scalar=mng[:, j:j + 1], in1=sigo[:, (c0 + c8 * 8) * 128:(c0 + c8 * 8 + 8) * 128],
                                                                                           op0=ALU.mult, op1=ALU.mult),
                                   r=[psB[bt], cB, sigB], w=[y4B[yi]])
                            dma(QP, mix_s[512 + j * 128:512 + (j + 1) * 128, c0 * 128:(c0 + FG) * 128], y4[yi][:, :], r=[y4B[yi]], w=[mixB])
                    for _ in fill:
                        yield
                yield

            with ExitStack() as pa:
                with ExitStack() as p2:
                    watt = sb(p2, "watt", [128, 8, 784], BF16)
                    wattB = Buf()
                    cos_t = sb(p2, "cos_t", [128, NT, 32], F32)
                    sin_t = sb(p2, "sin_t", [128, NT, 32], F32)
                    g10 = sb(p2, "g10", [128, 640], F32)
                    gateb = sb(p2, "gateb", [128, 16], F32)
                    qkf = [sb(p2, "qkf%d" % i, [128, 640], F32) for i in range(3)]
                    qkfB = [Buf(), Buf(), Buf()]
                    sqb_ = [sb(p2, "sqb%d" % i, [128, 640], F32) for i in range(3)]
                    qg_ = [sb(p2, "qg%d" % i, [128, 640], F32) for i in range(3)]
                    tcs_ = [sb(p2, "tcs%d" % i, [128, 640], F32) for i in range(3)]
                    tsn_ = [sb(p2, "tsn%d" % i, [128, 640], F32) for i in range(3)]
                    ro_ = [sb(p2, "ro%d" % i, [128, 640], F32) for i in range(3)]
                    ss10_ = [sb(p2, "ss10%d" % i, [128, 10], F32) for i in range(3)]
                    wkB_ = [Buf(), Buf(), Buf()]
                    ssB_ = [Buf(), Buf(), Buf()]
                    qn = [sb(p2, "qn%d" % i, [128, 640], BF16) for i in range(3)]
                    qnB = [Buf(), Buf(), Buf()]
                    kd = [sb(p2, "kd%d" % i, [128, 256], BF16) for i in range(3)]
                    kdB = [Buf(), Buf(), Buf()]
                    dma(QP, watt[:, :, 0:768], win_d[:, :, 0:768], w=[wattB])
                    dma(QP, watt[:, :, 768:784], win_d[:, :, 2816:2832], w=[wattB])
                    dma(QS, cos_t[:], cos_d, w=[cB])
                    dma(QS, sin_t[:], sin_d, w=[cB])
                    dma(QS, g10[:], g10_d, w=[cB])
                    dma(QS, gateb[:], gateb_d, w=[cB])
                    for tb in range(NB):
                        dma(QS, hT_s[tb], hT[:, :, tb * 512:(tb + 1) * 512], r=hTB[tb * 4:tb * 4 + 4], w=[hTsB])
                    op(DVE, lambda: nc.vector.memset(Vaug[:, :, :, 64:65], 1.0), w=[VB])

                    def v5(t):
                        return t[:].rearrange("p (h a b c) -> p h a b c", h=10, a=2, b=2, c=16)

                    pend2 = []
                    for tt in range(NT):
                        i = tt % 3
                        tsl = slice(tt * 128, (tt + 1) * 128)
                        sqb, qg, tcs, tsn, ro, ss10, wkB = sqb_[i], qg_[i], tcs_[i], tsn_[i], ro_[i], ss10_[i], wkB_[i]
                        bq = nbank()
                        for k in range(8):
                            op(PE, lambda k=k, bq=bq: nc.tensor.matmul(ps[bq][:, :], hT[:, k, tsl], watt[:, k, 0:512],
                                                                     start=(k == 0), stop=(k == 7)),
                               r=[hTB[tt], wattB], w=[psB[bq]])
                        bkv = nbank()
                        for k in range(8):
                            op(PE, lambda k=k, bkv=bkv: nc.tensor.matmul(ps[bkv][:, 0:272], hT[:, k, tsl], watt[:, k, 512:784],
                                                                       start=(k == 0), stop=(k == 7)),
                               r=[hTB[tt], wattB], w=[psB[bkv]])
                        op(ACT, lambda: nc.scalar.copy(out=qkf[i][:, 0:512], in_=ps[bq][:, :]), r=[psB[bq]], w=[qkfB[i]])
                        op(ACT, lambda: nc.scalar.copy(out=qkf[i][:, 512:640], in_=ps[bkv][:, 0:128]), r=[psB[bkv]], w=[qkfB[i]])
                        op(DVE, lambda: nc.vector.tensor_copy(Vaug[:, tt, :, 0:64], ps[bkv][:, 128:256].rearrange("p (g d) -> p g d", g=2)),
                           r=[psB[bkv]], w=[VB])
                        op(DVE, lambda: nc.vector.tensor_tensor(out=GT[:, tt, :], in0=ps[bkv][:, 256:272], in1=gateb[:], op=ALU.add),
                           r=[psB[bkv], cB], w=[GTB])
                        for hh in range(10):
                            op(ACT, lambda hh=hh: nc.scalar.activation(out=sqb[:, hh * 64:(hh + 1) * 64], in_=qkf[i][:, hh * 64:(hh + 1) * 64], func=AF.Square,
                                                                       accum_out=ss10[:, hh:hh + 1]), r=[qkfB[i]], w=[ssB_[i]])
                        op(ACT, lambda: nc.scalar.activation(out=ss10[:], in_=ss10[:], func=AF.Ln, scale=1.0 / 64, bias=epsT[:, 0:1]), r=[ssB_[i], cB], w=[ssB_[i]])
                        op(ACT, lambda: nc.scalar.activation(out=ss10[:], in_=ss10[:], func=AF.Exp, scale=-0.5), r=[ssB_[i]], w=[ssB_[i]])
                        op(DVE, lambda: nc.vector.tensor_tensor(out=qg[:], in0=qkf[i][:], in1=g10[:], op=ALU.mult), r=[qkfB[i], cB], w=[wkB])
                        for a in range(2):
                            cosa = cos_t[:, tt, a * 16:(a + 1) * 16].unsqueeze(1).unsqueeze(1).broadcast_to([128, 10, 2, 16])
                            sina = sin_t[:, tt, a * 16:(a + 1) * 16].unsqueeze(1).unsqueeze(1).broadcast_to([128, 10, 2, 16])
                            op(DVE, lambda a=a, cosa=cosa: nc.vector.tensor_tensor(out=v5(tcs)[:, :, a], in0=v5(qg)[:, :, a], in1=cosa, op=ALU.mult), r=[wkB, cB], w=[wkB])
                            op(DVE, lambda a=a, sina=sina: nc.vector.tensor_tensor(out=v5(tsn)[:, :, a], in0=v5(qg)[:, :, a], in1=sina, op=ALU.mult), r=[wkB, cB], w=[wkB])
                        op(DVE, lambda: nc.vector.tensor_tensor(out=v5(ro)[:, :, :, 0, :], in0=v5(tcs)[:, :, :, 0, :], in1=v5(tsn)[:, :, :, 1, :],
                                                                op=ALU.subtract), r=[wkB], w=[wkB])
                        op(DVE, lambda: nc.vector.tensor_tensor(out=v5(ro)[:, :, :, 1, :], in0=v5(tcs)[:, :, :, 1, :], in1=v5(tsn)[:, :, :, 0, :],
                                                                op=ALU.add), r=[wkB], w=[wkB])
                        op(DVE, lambda: nc.vector.tensor_tensor(out=qn[i][:].rearrange("p (h d) -> p h d", h=10),
                                                                in0=ro[:].rearrange("p (h d) -> p h d", h=10),
                                                                in1=ss10[:].unsqueeze(2).broadcast_to([128, 10, 64]), op=ALU.mult),
                           r=[wkB, ssB_[i]], w=[qnB[i]])
                        op(DVE, lambda: nc.vector.tensor_copy(kd[i][:].rearrange("p (g u d) -> p g u d", g=2, u=2),
                                                              qn[i][:, 512:640].rearrange("p (g d) -> p g d", g=2).unsqueeze(2).broadcast_to([128, 2, 2, 64])),
                           r=[qnB[i]], w=[kdB[i]])

                        def stage_b(i=i, tsl=tsl):
                            bt = nbank()
                            for pr in range(4):
                                op(PE, lambda pr=pr, bt=bt: nc.tensor.transpose(psb[bt][:, pr * 128:(pr + 1) * 128], qn[i][:, pr * 128:(pr + 1) * 128], ident[:]),
                                   r=[qnB[i], cB], w=[psB[bt]])
                            for g in range(2):
                                op(PE, lambda g=g, bt=bt: nc.tensor.transpose(psb[bt][:, 512 + g * 128:512 + (g + 1) * 128], kd[i][:, g * 128:(g + 1) * 128], ident[:]),
                                   r=[kdB[i], cB], w=[psB[bt]])
                            op(ACT, lambda: nc.scalar.copy(out=QT[:, :, tsl], in_=psb[bt][:, 0:512].rearrange("p (h t) -> p h t", h=4)),
                               r=[psB[bt]], w=[QTB])
                            op(ACT, lambda: nc.scalar.copy(out=KT[:, :, tsl], in_=psb[bt][:, 512:768].rearrange("p (h t) -> p h t", h=2)),
                               r=[psB[bt]], w=[KTB])

                        pend2.append(stage_b)
                        if len(pend2) > 2:
                            pend2.pop(0)()
                    for fb in pend2:
                        fb()
                    if DEBUG:
                        dma(QS, dbg["QT"], QT[:], r=[QTB])
                        dma(QS, dbg["KT"], KT[:], r=[KTB])
                    cx.barrier()
                hs.close()
                if STOP_AFTER <= 2:
                    return nc, dbg

                with ExitStack() as p3:
                    wa2 = [sb(p3, "wa2%d" % i, [128, 8, 512], BF16) for i in range(8)]
                    wa2B = [Buf() for _ in range(8)]
                    NPT = 3
                    PT = [sb(p3, "PT%d" % i, [128, 1024], BF16) for i in range(NPT)]
                    PTB = [Buf() for _ in range(NPT)]
                    if not PACKPV:
                        rrow = [[sb(p3, "rrow%d%d" % (i, e), [128, 512], F32) for e in range(2)] for i in range(1)] * 2
                        Osb = [[sb(p3, "Osb%d%d" % (i, e), [65, 512], F32) for e in range(2)] for i in range(1)] * 2
                        yst = [[sb(p3, "yst%d%d" % (i, e), [64, 512], BF16) for e in range(2)] for i in range(1)] * 2
                    rrB = [[Buf(), Buf()]] * 2
                    OsB = [[Buf(), Buf()]] * 2
                    ysB = [[Buf(), Buf()]] * 2
                    if PACKPV:
                        acc = [sb(p3, "acc%d" % i, [128, 1024], F32) for i in range(2)]
                        rinv = sb(p3, "rinv", [128, 512], F32)
                        Osb2 = sb(p3, "Osb2", [128, 512], F32)
                        yst2 = [sb(p3, "yst2%d" % i, [128, 512], BF16) for i in range(2)]
                    accB = [Buf(), Buf()]
                    accPB = [Buf(), Buf()]
                    rinvB, Osb2B, yst2B = Buf(), Buf(), [Buf(), Buf()]
                    OB2 = [Buf(excl=True), Buf(excl=True)]
                    NSL = 2 if INTERLEAVE else 3
                    pairB = [Buf(excl=True) for _ in range(NSL)]
                    OB = [[Buf(excl=True), Buf(excl=True)]] * 2
                    LA = 1 if INTERLEAVE else 2
                    slot_ctr = [0]
                    gslot = {}

                    def next_slot():
                        v = slot_ctr[0] % NSL
                        slot_ctr[0] += 1
                        return v

                    def fin_a(o):
                        if PACKPV:
                            op(ACT, lambda: nc.scalar.copy(out=Osb2[:, :], in_=ps[6 + o % 2][:, :]), r=[OB2[o % 2]], w=[Osb2B])
                            return
                        i = o % 2
                        for e in range(2):
                            ob = 6 + e
                            op(ACT, lambda e=e, ob=ob: nc.scalar.copy(out=Osb[i][e][:, :], in_=ps[ob][0:65, :]), r=[OB[i][e]], w=[OsB[i][e]])
                            op(DVE, lambda e=e: nc.vector.reciprocal(rrow[i][e][64:65, :], Osb[i][e][64:65, :]), r=[OsB[i][e]], w=[rrB[i][e]])

                    def bc_s(o, sl):
                        if PACKPV:
                            for e in range(2):
                                op(PE, lambda e=e: nc.tensor.matmul(ps[2 * sl][e * 64:(e + 1) * 64, :], ones_f[:, 0:64], acc[o % 2][:, e * 512:(e + 1) * 512],
                                                                    start=True, stop=True, tile_position=(0, e * 64)),
                                   r=[accB[o % 2], accPB[o % 2], cB], w=[pairB[sl]])
                            return
                        i = o % 2
                        for e in range(2):
                            op(PE, lambda e=e: nc.tensor.matmul(ps[2 * sl + e][0:64, :], ones_f[64:65, 0:64], rrow[i][e][64:65, :], start=True, stop=True),
                               r=[rrB[i][e], cB], w=[pairB[sl]])

                    def bc_rest(pr, qb, o, sl):
                        if PACKPV:
                            op(DVE, lambda: nc.vector.reciprocal(rinv[:, :], ps[2 * sl][:, :]), r=[pairB[sl]], w=[rinvB])
                            op(DVE, lambda: nc.vector.tensor_tensor(out=yst2[o % 2][:, :], in0=Osb2[:, :], in1=rinv[:, :], op=ALU.mult),
                               r=[Osb2B, rinvB], w=[yst2B[o % 2]])
                            dma(QP, mix_s[pr * 128:(pr + 1) * 128, qb * 512:(qb + 1) * 512], yst2[o % 2][:, :], r=[yst2B[o % 2]], w=[mixB])
                            return
                        i = o % 2
                        for e in range(2):
                            h = 2 * pr + e
                            op(DVE, lambda e=e: nc.vector.tensor_tensor(out=yst[i][e][:, :], in0=Osb[i][e][0:64, :], in1=ps[2 * sl + e][0:64, :], op=ALU.mult),
                               r=[OsB[i][e], pairB[sl]], w=[ysB[i][e]])
                            dma(QP, mix_s[h * 64:(h + 1) * 64, qb * 512:(qb + 1) * 512], yst[i][e][:, :], r=[ysB[i][e]], w=[mixB])

                    for wv in range(40):
                        op(PE, lambda wv=wv: nc.tensor.matmul(ps[6][:, :], ident[:], QT[:, wv % 4, (wv % 8) * 512:(wv % 8 + 1) * 512], start=True, stop=True),
                           r=[cB, QTB], w=[OB[0][0]])
                    groups = [(pr, qb, kt) for pr in range(4) for qb in range(NB) for kt in range(NT)]
                    seq = []
                    due = []
                    for gi, (pr, qb, kt) in enumerate(groups):
                        seq.append(("S", pr, qb, kt, gi // NT))
                        due = [(n - 1, it_) for (n, it_) in due]
                        while due and due[0][0] <= 0:
                            seq.append(due.pop(0)[1])
                        if kt == NT - 1:
                            due.append((8, ("BC", pr, qb, 0, gi // NT)))
                    for _ in range(8):
                        seq.append(("NOP", 0, 0, 0, 0))
                    seq.extend(it_ for (_, it_) in due)
                    N = len(seq)
                    real_idx = 0
                    mdone = True
                    if INTERLEAVE:
                        cx.banks = [4, 5]
                        p4s = ExitStack()
                        mgen = mlstm_gen(p4s)
                        mdone = False
                    for it in range(N + LA):
                        if it % 24 == 8 and it // 24 < 8:
                            n_ = 4 + it // 24
                            dma(QP, wa2[n_ - 4][:], wada_d[n_], w=[wa2B[n_ - 4]])
                        if it % 24 == 8 and 10 <= it // 24 < 14:
                            q4 = it // 24 - 10
                            rs = slice(q4 * 704, (q4 + 1) * 704)
                            dma(QP, wg_s[rs, :], wg_d[rs, :], w=[wsB])
                            dma(QP, wu_s[rs, :], wu_d[rs, :], w=[wsB])
                        if INTERLEAVE and not mdone and it % 2 == 0:
                            try:
                                next(mgen)
                            except StopIteration:
                                mdone = True
                        if it < N:
                            kind, pr, qb, kt, o = seq[it]
                            if kind != "NOP":
                                sl = next_slot()
                                gslot[it] = sl
                            if kind == "S":
                                g = pr // 2
                                for e in range(2):
                                    hp = e * 64
                                    op(PE, lambda e=e, hp=hp: nc.tensor.matmul(ps[2 * sl + e][:, :], KT[hp:hp + 64, g, kt * 128:(kt + 1) * 128],
                                                                               QT[hp:hp + 64, pr, qb * 512:(qb + 1) * 512], start=True, stop=True),
                                       r=[KTB, QTB], w=[pairB[sl]])
                            elif kind == "BC":
                                bc_s(o, sl)
                        if it >= LA:
                            j = it - LA
                            kind, pr, qb, kt, o = seq[j]
                            if kind == "NOP":
                                continue
                            sl = gslot.pop(j)
                            if kind == "BC":
                                bc_rest(pr, qb, o, sl)
                                continue
                            g = pr // 2
                            pi = real_idx % NPT
                            real_idx += 1
                            op(ACT, lambda: nc.scalar.activation(out=PT[pi][:], in_=psall[:, sl * 1024:(sl + 1) * 1024], func=AF.Exp, scale=0.125),
                               r=[pairB[sl]], w=[PTB[pi]])
                            for e in range(2):
                                if PACKPV:
                                    ob = 6 + o % 2
                                    op(PE, lambda e=e, ob=ob: nc.tensor.matmul(ps[ob][e * 64:(e + 1) * 64, :], Vaug[:, kt, g, 0:64], PT[pi][:, e * 512:(e + 1) * 512],
                                                                               start=(kt == 0), stop=(kt == NT - 1), tile_position=(0, e * 64)),
                                       r=[VB, PTB[pi]], w=[OB2[o % 2]])
                                    continue
                                ob = 6 + e
                                op(PE, lambda e=e, ob=ob: nc.tensor.matmul(ps[ob][0:65, :], Vaug[:, kt, g, :], PT[pi][:, e * 512:(e + 1) * 512],
                                                                           start=(kt == 0), stop=(kt == NT - 1)),
                                   r=[VB, PTB[pi]], w=[OB[o % 2][e]])
                            if PACKPV:
                                CS = 704
                                if kt == 0:
                                    op(DVE, lambda: nc.vector.tensor_copy(acc[o % 2][:, 0:CS], PT[pi][:, 0:CS]), r=[PTB[pi]], w=[accB[o % 2]])
                                    op(POOL, lambda: nc.gpsimd.tensor_copy(acc[o % 2][:, CS:1024], PT[pi][:, CS:1024]), r=[PTB[pi]], w=[accPB[o % 2]])
                                else:
                                    op(DVE, lambda: nc.vector.tensor_tensor(out=acc[o % 2][:, 0:CS], in0=acc[o % 2][:, 0:CS], in1=PT[pi][:, 0:CS], op=ALU.add),
                                       r=[PTB[pi], accB[o % 2]], w=[accB[o % 2]])
                                    op(POOL, lambda: nc.gpsimd.tensor_tensor(out=acc[o % 2][:, CS:1024], in0=acc[o % 2][:, CS:1024], in1=PT[pi][:, CS:1024], op=ALU.add),
                                       r=[PTB[pi], accPB[o % 2]], w=[accPB[o % 2]])
                            if kt == NT - 1:
                                fin_a(o)
                    if INTERLEAVE:
                        for _ in mgen:
                            pass
                    cx.banks = list(range(8))
                    cx.barrier()
                    fins = [ada_block(n, wa2[n - 4], wa2B[n - 4], defer=True) for n in range(4, 12)]
                    for fn_ in fins:
                        fn_()
                    phase0b()
                    cx.barrier()
                    if INTERLEAVE:
                        p4s.close()
            qs.close()
            if not INTERLEAVE:
                p4s = ExitStack()
                for _ in mlstm_gen(p4s):
                    pass
                cx.barrier()
                p4s.close()
        with ExitStack() as p5:
            if DEBUG:
                dma(QS, dbg["mix"], mix_s, r=[mixB])
            wo = sb(p5, "wo", [128, 8, 1024], BF16)
            wdT = sb(p5, "wdT", [128, NJ, 1024], BF16)
            woB, wdB = Buf(), Buf()
            for q4 in range(4):
                dma(QP, wo[:, q4 * 2:q4 * 2 + 2, :], wout_d[:, q4 * 2:q4 * 2 + 2, :], w=[woB])
            for q4 in range(11):
                dma(QP, wdT[:, q4 * 2:q4 * 2 + 2, :], wd_d[:, q4 * 2:q4 * 2 + 2, :], w=[wdB])
            g1b = sb(p5, "g1b", [128, 1024], F32)
            g2b = sb(p5, "g2b", [128, 1024], F32)
            fngb = sb(p5, "fngb", [128, 1024], F32)
            dgf = [sb(p5, "dgf%d" % i, [128, 128], F32) for i in range(2)]
            dgB = [Buf(), Buf()]
            dma(QS, fngb[:], fng_d, w=[cB])
            cnt = 0
            for (dst, c0) in ((g1b, 16), (g2b, 40)):
                for k in range(8):
                    di = cnt % 2
                    cnt += 1
                    op(DVE, lambda di=di, k=k, c0=c0: nc.vector.tensor_scalar(dgf[di][:], ident_f[:], modT[:, c0 + k:c0 + k + 1], None, op0=ALU.mult),
                       r=[modB, cB], w=[dgB[di]])
                    bk = nbank()
                    op(PE, lambda di=di, bk=bk: nc.tensor.matmul(ps[bk][:, 0:128], ones_f[:], dgf[di][:], start=True, stop=True),
                       r=[dgB[di], cB], w=[psB[bk]])
                    op(ACT, lambda bk=bk, dst=dst, k=k: nc.scalar.copy(out=dst[:, k * 128:(k + 1) * 128], in_=ps[bk][:, 0:128]),
                       r=[psB[bk]], w=[modB])
            NW = 4
            wgt = [sb(p5, "wgt%d" % i, [128, 1024], BF16) for i in range(NW)]
            wut = [sb(p5, "wut%d" % i, [128, 1024], BF16) for i in range(NW)]
            wgB = [Buf() for _ in range(NW)]
            wuB = [Buf() for _ in range(NW)]
            mb = sb(p5, "mb", [128, 8, 512], BF16)
            mbB = Buf()
            xt = [sb(p5, "xt5%d" % i, [128, 1024], F32) for i in range(2)]
            xtB = [Buf(), Buf()]
            x1 = sb(p5, "x1", [128, 2, 4, 1024], F32)
            x1B = [[Buf() for _ in range(4)] for _ in range(2)]
            xn = [sb(p5, "xn5%d" % i, [128, 1024], BF16) for i in range(4)]
            xnB = [Buf() for _ in range(4)]
            junk = sb(p5, "junk5", [128, 1024], BF16)
            junkB = Buf()
            ss = sb(p5, "ss5", [128, NT], F32)
            rstd = sb(p5, "rstd5", [128, NT], F32)
            ss3 = sb(p5, "ss35", [128, NT], F32)
            stB = Buf()
            st3B = Buf()
            h2T = sb(p5, "h2T", [128, 8, 512], BF16)
            h2B = [Buf() for _ in range(4)]
            act = sb(p5, "act", [128, NJ, 512], BF16)
            actB = [Buf() for _ in range(NJ)]
            sg = [sb(p5, "sg%d" % i, [128, 512], F32) for i in range(2)]
            sgB = [Buf(), Buf()]
            x2t = [sb(p5, "x2t%d" % i, [128, 1024], F32) for i in range(2)]
            x2B = [Buf(), Buf()]
            ot = [sb(p5, "ot%d" % i, [128, 1024], F32) for i in range(2)]
            otB = [Buf(), Buf()]
            mixv = mix_s.rearrange("(k p) t -> p k t", p=128)
            wcnt = [0]

            def pro_a(tb, ti):
                tt = tb * 4 + ti
                i = tt % 2
                xb = tb % 2
                dma(QS, xt[i][:], x_d[tt * 128:(tt + 1) * 128, :], w=[xtB[i]])
                for hf_ in range(2):
                    bk = nbank()
                    for k in range(8):
                        op(PE, lambda k=k: nc.tensor.matmul(ps[bk][:, :], mb[:, k, ti * 128:(ti + 1) * 128],
                                                            wo[:, k, hf_ * 512:(hf_ + 1) * 512], start=(k == 0), stop=(k == 7)),
                           r=[mbB, woB], w=[psB[bk]])
                    hs = slice(hf_ * 512, (hf_ + 1) * 512)
                    op(DVE, lambda: nc.vector.tensor_tensor(out=x1[:, xb, ti, hs], in0=ps[bk][:, :], in1=g1b[:, hs], op=ALU.mult),
                       r=[psB[bk], modB], w=[x1B[xb][ti]])
                op(DVE, lambda: nc.vector.tensor_tensor(out=x1[:, xb, ti, :], in0=x1[:, xb, ti, :], in1=xt[i][:], op=ALU.add),
                   r=[x1B[xb][ti], xtB[i]], w=[x1B[xb][ti]])
                return norm_to_T(stB, x1[:, xb, ti, :], x1B[xb][ti], ss, rstd, tt, xn[ti], xnB[ti], junk, junkB, h2T, h2B[ti],
                                 slice(ti * 128, (ti + 1) * 128), scale2, sh2, tt % 2, xn_on_act=True)

            def phase_a(tb):
                for j in range(NJ):
                    wi = wcnt[0] % NW
                    wcnt[0] += 1
                    dma(QS, wgt[wi][:], wg_s[j * 128:(j + 1) * 128, :], r=[wsB], w=[wgB[wi]])
                    dma(QS, wut[wi][:], wu_s[j * 128:(j + 1) * 128, :], r=[wsB], w=[wuB[wi]])
                    bg_ = nbank()
                    for k in range(8):
                        op(PE, lambda k=k: nc.tensor.matmul(ps[bg_][:, :], wgt[wi][:, k * 128:(k + 1) * 128], h2T[:, k, :],
                                                            start=(k == 0), stop=(k == 7)), r=[wgB[wi]] + h2B, w=[psB[bg_]])
                    bu_ = nbank()
                    for k in range(8):
                        op(PE, lambda k=k: nc.tensor.matmul(ps[bu_][:, :], wut[wi][:, k * 128:(k + 1) * 128], h2T[:, k, :],
                                                            start=(k == 0), stop=(k == 7)), r=[wuB[wi]] + h2B, w=[psB[bu_]])
                    si = j % 2
                    op(ACT, lambda: nc.scalar.activation(out=sg[si][:], in_=ps[bg_][:, :], func=AF.Silu), r=[psB[bg_]], w=[sgB[si]])
                    op(DVE, lambda: nc.vector.tensor_tensor(out=act[:, j, :], in0=sg[si][:], in1=ps[bu_][:, :], op=ALU.mult),
                       r=[sgB[si], psB[bu_]], w=[actB[j]])

            def phase_b(tb, ti):
                tt = tb * 4 + ti
                i = tt % 2
                xb = tb % 2
                for hf_ in range(2):
                    bk = nbank()
                    hs = slice(hf_ * 512, (hf_ + 1) * 512)
                    for j in range(NJ):
                        op(PE, lambda j=j: nc.tensor.matmul(ps[bk][:, :], act[:, j, ti * 128:(ti + 1) * 128], wdT[:, j, hs],
                                                            start=(j == 0), stop=(j == NJ - 1)),
                           r=[actB[j], wdB], w=[psB[bk]])
                    op(DVE, lambda: nc.vector.tensor_tensor(out=x2t[i][:, hs], in0=ps[bk][:, :], in1=g2b[:, hs], op=ALU.mult),
                       r=[psB[bk], modB], w=[x2B[i]])
                op(DVE, lambda: nc.vector.tensor_tensor(out=x2t[i][:], in0=x2t[i][:], in1=x1[:, xb, ti, :], op=ALU.add),
                   r=[x2B[i], x1B[xb][ti]], w=[x2B[i]])
                op(ACT, lambda: nc.scalar.activation(out=junk[:], in_=x2t[i][:], func=AF.Square, accum_out=ss3[:, tt:tt + 1]),
                   r=[x2B[i]], w=[junkB, st3B])
                op(ACT, lambda: nc.scalar.activation(out=ss3[:, tt:tt + 1], in_=ss3[:, tt:tt + 1], func=AF.Ln, scale=1.0 / D, bias=epsT[:, 0:1]), r=[st3B, cB], w=[st3B])
                op(ACT, lambda: nc.scalar.activation(out=ss3[:, tt:tt + 1], in_=ss3[:, tt:tt + 1], func=AF.Exp, scale=-0.5), r=[st3B], w=[st3B])
                op(DVE, lambda: nc.vector.scalar_tensor_tensor(out=ot[i][:], in0=x2t[i][:], scalar=ss3[:, tt:tt + 1], in1=fngb[:],
                                                               op0=ALU.mult, op1=ALU.mult), r=[x2B[i], st3B, cB], w=[otB[i]])
                dma(QP, out_d[tt * 128:(tt + 1) * 128, :], ot[i][:], r=[otB[i]])

            dma(QS, mb[:], mixv[:, :, 0:512], r=[mixB], w=[mbB])
            for pb in [pro_a(0, ti) for ti in range(4)]:
                pb()
            for tb in range(NB):
                phase_a(tb)
                if tb + 1 < NB:
                    dma(QS, mb[:], mixv[:, :, (tb + 1) * 512:(tb + 2) * 512], r=[mixB], w=[mbB])
                for ti in range(4):
                    pb = pro_a(tb + 1, ti) if tb + 1 < NB else None
                    phase_b(tb, ti)
                    if pb is not None:
                        pb()
            cx.barrier()
    return nc, dbg


def _prep_shared(inp):
    f = np.float32
    w_ada = inp["w_ada"][0]
    sh = {}
    sh["w_ada"] = np.ascontiguousarray(w_ada.reshape(8, 128, 12, 512).transpose(2, 1, 0, 3))
    sh["b_adaT"] = np.ascontiguousarray(inp["b_ada"][0].reshape(48, 128).T)
    sh["gmixT"] = np.ascontiguousarray(inp["norm_mix_g"][0].reshape(8, 128).T)
    sh["gffnT"] = np.ascontiguousarray(inp["norm_ffn_g"][0].reshape(8, 128).T)
    sh["w_in"] = np.ascontiguousarray(inp["w_in"][0].reshape(8, 128, 2832).transpose(1, 0, 2))
    g10 = np.concatenate([np.tile(inp["q_norm_g"][0], 8), np.tile(inp["k_norm_g"][0], 2)])
    sh["g10"] = np.ascontiguousarray(np.broadcast_to(g10[None, :], (128, 640))).astype(f)
    tok = np.arange(S)
    row = (tok // 64).astype(f)
    col = (tok % 64).astype(f)
    inv_freq = (f(10000.0) ** (-np.arange(0, 32, 2, dtype=f) / f(32))).astype(f)
    ang_r = (row[:, None] * inv_freq[None, :]).astype(f)
    ang_c = (col[:, None] * inv_freq[None, :]).astype(f)
    cos = np.concatenate([np.cos(ang_r), np.cos(ang_c)], axis=1).astype(f)
    sin = np.concatenate([np.sin(ang_r), np.sin(ang_c)], axis=1).astype(f)
    sh["ropecos"] = np.ascontiguousarray(cos.reshape(NT, 128, 32).transpose(1, 0, 2))
    sh["ropesin"] = np.ascontiguousarray(sin.reshape(NT, 128, 32).transpose(1, 0, 2))
    sh["convwT"] = np.ascontiguousarray(inp["conv_w"][0].reshape(5, 8, 128).transpose(2, 1, 0))
    sh["convbT"] = np.ascontiguousarray(inp["conv_b"][0].reshape(8, 128).T)
    sh["gateb"] = np.ascontiguousarray(np.broadcast_to(inp["gate_b"][0][None, :], (128, 16))).astype(f)
    sh["mngT"] = np.ascontiguousarray(inp["mlstm_norm_g"][0].reshape(4, 128).T)
    sh["w_out"] = np.ascontiguousarray(inp["w_out"][0].reshape(8, 128, 1024).transpose(1, 0, 2))
    for nm in ("w_gate", "w_up"):
        w = inp[nm][0].reshape(8, 128, NJ, 128).transpose(2, 1, 0, 3)
        sh[nm] = np.ascontiguousarray(w.reshape(NJ * 128, 1024))
    sh["w_down"] = np.ascontiguousarray(inp["w_down"][0].reshape(NJ, 128, 1024).transpose(1, 0, 2))
    sh["fngb"] = np.ascontiguousarray(np.broadcast_to(inp["final_norm_g"][None, :], (128, 1024))).astype(f)
    sh["ident"] = np.eye(128, dtype=f)
    sh["triu"] = np.triu(np.ones((128, 128), dtype=f))
    sh["tril"] = np.tril(np.ones((128, 128), dtype=f))
    return sh


def kernel(**inputs):
    inp = {k: np.asarray(v) for k, v in inputs.items()}
    B = inp["x"].shape[0]
    nc, _ = build()
    sh = _prep_shared(inp)
    in_maps = []
    for b in range(B):
        m = dict(sh)
        m["x"] = np.ascontiguousarray(inp["x"][b])
        m["cT"] = np.ascontiguousarray(inp["c"][b].reshape(8, 128).T)
        in_maps.append(m)
    res = run_bass_kernel_spmd(nc, in_maps, core_ids=list(range(B)))
    return np.stack([np.asarray(r["out"]) for r in res.results], axis=0).astype(np.float32)
```

```python
import math
from contextlib import ExitStack

import numpy as np
import concourse.bass as bass
import concourse.mybir as mybir
from concourse.bass_utils import run_bass_kernel_spmd

F32 = mybir.dt.float32
BF16 = mybir.dt.bfloat16
AF = mybir.ActivationFunctionType
ALU = mybir.AluOpType
AX = mybir.AxisListType

S = 4096
D = 1024
NT = 32
NB = 8
NJ = 22
EPS = 1e-6
DEBUG = False
STOP_AFTER = 99
INTERLEAVE = False
PACKPV = False


class Sem:
    def __init__(self, h):
        self.h = h
        self.val = 0


class Buf:
    __slots__ = ("w", "r", "excl")

    def __init__(self, excl=False):
        self.w = None
        self.r = {}
        self.excl = excl


class Eng:
    def __init__(self, e, sem, selfsync=True):
        self.e = e
        self.sem = sem
        self.selfsync = selfsync
        self.waited = {}

    def wait_tok(self, tok):
        if tok is None:
            return
        sem, val = tok
        if sem is self.sem and not self.selfsync:
            return
        if self.waited.get(id(sem), 0) >= val:
            return
        self.e.wait_ge(sem.h, val)
        self.waited[id(sem)] = val


class DmaQ:
    def __init__(self, eng, sems):
        self.eng = eng
        self.sems = sems
        self.i = 0


class Ctx:
    def __init__(self, nc, es):
        self.nc = nc
        self.es = es
        self.nsem = 0
        self.all_sems = []
        self.PE = Eng(nc.tensor, self.new_sem("pe"), selfsync=False)
        self.ACT = Eng(nc.scalar, self.new_sem("act"))
        self.DVE = Eng(nc.vector, self.new_sem("dve"))
        self.POOL = Eng(nc.gpsimd, self.new_sem("pool"))
        self.SP = Eng(nc.sync, self.new_sem("sp"))
        self.engs = [self.PE, self.ACT, self.DVE, self.POOL, self.SP]
        self.qsp = DmaQ(self.SP, [self.new_sem("dsp%d" % i) for i in range(40)])
        self.qpool = DmaQ(self.POOL, [self.new_sem("dpl%d" % i) for i in range(40)])
        self.bank_rr = 0

    def new_sem(self, name):
        s = Sem(self.es.enter_context(self.nc.semaphore(name)))
        self.all_sems.append(s)
        return s

    def _deps(self, E, r, w):
        for b in r:
            E.wait_tok(b.w)
        for b in w:
            E.wait_tok(b.w)
            for tok in b.r.values():
                E.wait_tok(tok)

    def _mark(self, tok, r, w):
        for b in r:
            b.r[id(tok[0])] = tok
        for b in w:
            b.w = tok
            b.r = {}

    def op(self, E, fn, r=(), w=()):
        ex = [b for b in r if b.excl]
        if ex:
            r = [b for b in r if not b.excl]
            w = list(w) + ex
        self._deps(E, r, w)
        inst = fn()
        E.sem.val += 1
        inst.then_inc(E.sem.h, 1)
        self._mark((E.sem, E.sem.val), r, w)

    def dma(self, Q, out, in_, r=(), w=()):
        s = Q.sems[Q.i % len(Q.sems)]
        Q.i += 1
        E = Q.eng
        if s.val > 0:
            E.wait_tok((s, s.val))
        self._deps(E, r, w)
        inst = E.e.dma_start(out=out, in_=in_)
        s.val += 16
        inst.then_inc(s.h, 16)
        self._mark((s, s.val), r, w)

    def barrier(self):
        for E in self.engs:
            for s in self.all_sems:
                if s.val > 0:
                    E.wait_tok((s, s.val))


def build():
    nc = bass.Bass("TRN2", target_bir_lowering=False)

    def din(name, shape, dt=F32):
        return nc.dram_tensor(name, list(shape), dt, kind="ExternalInput").ap()

    x_d = din("x", [S, D])
    cT_d = din("cT", [128, 8])
    wada_d = din("w_ada", [12, 128, 8, 512])
    bada_d = din("b_adaT", [128, 48])
    gmix_d = din("gmixT", [128, 8])
    gffn_d = din("gffnT", [128, 8])
    win_d = din("w_in", [128, 8, 2832])
    g10_d = din("g10", [128, 640])
    cos_d = din("ropecos", [128, NT, 32])
    sin_d = din("ropesin", [128, NT, 32])
    convw_d = din("convwT", [128, 8, 5])
    convb_d = din("convbT", [128, 8])
    gateb_d = din("gateb", [128, 16])
    mng_d = din("mngT", [128, 4])
    wout_d = din("w_out", [128, 8, 1024])
    wg_d = din("w_gate", [NJ * 128, 1024])
    wu_d = din("w_up", [NJ * 128, 1024])
    wd_d = din("w_down", [128, NJ, 1024])
    fng_d = din("fngb", [128, 1024])
    ident_d = din("ident", [128, 128])
    triu_d = din("triu", [128, 128])
    tril_d = din("tril", [128, 128])
    out_d = nc.dram_tensor("out", [S, D], F32, kind="ExternalOutput").ap()
    mix_s = nc.dram_tensor("mix_s", [D, S], BF16).ap()
    wg_s = nc.dram_tensor("wg_s", [NJ * 128, 1024], BF16).ap()
    wu_s = nc.dram_tensor("wu_s", [NJ * 128, 1024], BF16).ap()
    hT_s = nc.dram_tensor("hT_s", [NB, 128, 8, 512], BF16).ap()
    dbg = {}
    if DEBUG:
        dbg["hT"] = nc.dram_tensor("dbg_hT", [128, 8, S], BF16, kind="ExternalOutput").ap()
        dbg["QT"] = nc.dram_tensor("dbg_QT", [128, 4, S], BF16, kind="ExternalOutput").ap()
        dbg["KT"] = nc.dram_tensor("dbg_KT", [128, 2, S], BF16, kind="ExternalOutput").ap()
        dbg["mix"] = nc.dram_tensor("dbg_mix", [D, S], BF16, kind="ExternalOutput").ap()
        dbg["modT"] = nc.dram_tensor("dbg_modT", [128, 48], F32, kind="ExternalOutput").ap()

    with ExitStack() as es:
        cx = Ctx(nc, es)
        PE, ACT, DVE, POOL, SP = cx.PE, cx.ACT, cx.DVE, cx.POOL, cx.SP
        op, dma = cx.op, cx.dma
        QS, QP = cx.qsp, cx.qpool

        def sb(stack, name, shape, dt):
            return stack.enter_context(nc.sbuf_tensor("s_" + name, list(shape), dt))

        psall = es.enter_context(nc.psum_tensor("psall", [128, 4096], F32))
        psall_b = psall.bitcast(BF16)
        ps = [psall[:, i * 512:(i + 1) * 512] for i in range(8)]
        psb = [psall_b[:, i * 1024:(i + 1) * 1024] for i in range(8)]
        psB = [Buf(excl=True) for _ in range(8)]

        cx.banks = list(range(8))

        def nbank():
            i = cx.banks[cx.bank_rr % len(cx.banks)]
            cx.bank_rr += 1
            return i

        ident = sb(es, "ident", [128, 128], BF16)
        mask2 = sb(es, "mask2", [128, 2, 128], BF16)
        masku = mask2[:, 0, :]
        maskl = mask2[:, 1, :]
        triu = sb(es, "triu_f", [128, 128], F32)
        tril = sb(es, "tril_f", [128, 128], F32)
        ones_f = sb(es, "ones_f", [128, 128], F32)
        modT = sb(es, "modT", [128, 48], F32)
        scale1 = sb(es, "scale1", [128, 8], F32)
        scale2 = sb(es, "scale2", [128, 8], F32)
        ident_f = sb(es, "ident_f", [128, 128], F32)
        GT = sb(es, "GT", [128, NT, 16], F32)
        lns = sb(es, "lns", [128, 1], F32)
        epsT = sb(es, "epsT", [128, 1], F32)
        cTt = sb(es, "cTt", [128, 8], F32)
        cond = sb(es, "cond", [128, 8], BF16)
        gmixT = sb(es, "gmixT", [128, 8], F32)
        gffnT = sb(es, "gffnT", [128, 8], F32)
        tmp8 = sb(es, "tmp8", [128, 8], F32)
        badaT = sb(es, "badaT", [128, 48], F32)
        mrow = [sb(es, "mrow%d" % i, [1, 512], F32) for i in range(2)]
        cB = Buf()
        modB = Buf()
        GTB = Buf()
        mixB = Buf()
        wsB = Buf()

        dma(QP, ident[:], ident_d, w=[cB])
        dma(QP, mask2[:, 0, :], triu_d, w=[cB])
        dma(QP, mask2[:, 1, :], tril_d, w=[cB])
        dma(QS, triu[:], triu_d, w=[cB])
        dma(QS, tril[:], tril_d, w=[cB])
        dma(QS, ident_f[:], ident_d, w=[cB])
        op(DVE, lambda: nc.vector.memset(ones_f[:], 1.0), w=[cB])
        op(DVE, lambda: nc.vector.memset(lns[:], math.log(128.0 ** -0.5)), w=[cB])
        op(DVE, lambda: nc.vector.memset(epsT[:], EPS), w=[cB])

        with ExitStack() as pm:
            qs = ExitStack()
            QT = sb(qs, "QT", [128, 4, S], BF16)
            KT = sb(qs, "KT", [128, 2, S], BF16)
            Vaug = sb(qs, "Vaug", [128, NT, 2, 65], BF16)
            QTB, KTB, VB = Buf(), Buf(), Buf()
            hs = ExitStack()
            hT = sb(hs, "hT", [128, 8, S], BF16)
            hTsB = Buf()
            hTB = [Buf() for _ in range(NT)]
            p0 = ExitStack()
            mrB = [Buf(), Buf()]
            NWA = 4
            wa = [sb(p0, "wa%d" % i, [128, 8, 512], BF16) for i in range(NWA)]
            waB = [Buf() for _ in range(NWA)]
            dma(QS, cTt[:], cT_d, w=[cB])
            dma(QS, gmixT[:], gmix_d, w=[cB])
            dma(QS, gffnT[:], gffn_d, w=[cB])
            dma(QS, badaT[:], bada_d, w=[cB])
            op(ACT, lambda: nc.scalar.activation(out=cond[:], in_=cTt[:], func=AF.Silu), r=[cB], w=[modB])
            for n in range(NWA):
                dma(QP, wa[n][:], wada_d[n], w=[waB[n]])

            def ada_block(n, wbuf=None, wbufB=None, defer=False):
                if wbuf is None:
                    wbuf, wbufB = wa[n % NWA], waB[n % NWA]
                bk = nbank()
                for k in range(8):
                    op(PE, lambda k=k: nc.tensor.matmul(ps[bk][0:1, :], cond[:, k:k + 1], wbuf[:, k, :], start=(k == 0), stop=(k == 7)),
                       r=[modB, wbufB], w=[psB[bk]])
                def fin():
                    m = n % 2
                    op(DVE, lambda: nc.vector.tensor_copy(mrow[m][0:1, :], ps[bk][0:1, :]), r=[psB[bk]], w=[mrB[m]])
                    bk2 = nbank()
                    for q in range(4):
                        op(PE, lambda q=q: nc.tensor.matmul(ps[bk2][:, q:q + 1], mrow[m][0:1, q * 128:(q + 1) * 128], ones_f[0:1, 0:1], start=True, stop=True),
                           r=[mrB[m], cB], w=[psB[bk2]])
                    op(DVE, lambda: nc.vector.tensor_tensor(out=modT[:, 4 * n:4 * n + 4], in0=ps[bk2][:, 0:4], in1=badaT[:, 4 * n:4 * n + 4], op=ALU.add),
                       r=[psB[bk2], cB], w=[modB])
                if defer:
                    return fin
                fin()

            def mk_scale(dst, c0, g):
                op(DVE, lambda: nc.vector.tensor_scalar(tmp8[:], modT[:, c0:c0 + 8], 1.0, None, op0=ALU.add), r=[modB], w=[modB])
                op(DVE, lambda: nc.vector.tensor_tensor(out=dst[:], in0=tmp8[:], in1=g[:], op=ALU.mult), r=[modB, cB], w=[modB])

            for fn_ in [ada_block(n, defer=True) for n in range(4)]:
                fn_()
            mk_scale(scale1, 8, gmixT)
            sh1, sh2 = modT[:, 0:8], modT[:, 24:32]

            def phase0b():
                mk_scale(scale2, 32, gffnT)
                if DEBUG:
                    dma(QS, dbg["modT"], modT[:], r=[modB])

            def norm_to_T(stack_bufs, xsrc, xsrcB, ssT, rstdT, col, xn, xnB, junk, junkB, dstT, dstB, dst_cols, scaleT, biasT, parity, xn_on_act=False):
                op(ACT, lambda: nc.scalar.activation(out=junk[:], in_=xsrc, func=AF.Square, accum_out=ssT[:, col:col + 1]),
                   r=[xsrcB], w=[junkB, stack_bufs])
                op(ACT, lambda: nc.scalar.activation(out=rstdT[:, col:col + 1], in_=ssT[:, col:col + 1], func=AF.Ln, scale=1.0 / D, bias=epsT[:, 0:1]), r=[stack_bufs, cB], w=[stack_bufs])
                op(ACT, lambda: nc.scalar.activation(out=rstdT[:, col:col + 1], in_=rstdT[:, col:col + 1], func=AF.Exp, scale=-0.5), r=[stack_bufs], w=[stack_bufs])
                if xn_on_act:
                    op(ACT, lambda: nc.scalar.activation(out=xn[:], in_=xsrc, func=AF.Identity, scale=rstdT[:, col:col + 1]),
                       r=[xsrcB, stack_bufs], w=[xnB])
                else:
                    op(DVE, lambda: nc.vector.tensor_scalar(xn[:], xsrc, rstdT[:, col:col + 1], None, op0=ALU.mult),
                       r=[xsrcB, stack_bufs], w=[xnB])
                def part_b():
                    bk = nbank()
                    for k in range(8):
                        op(PE, lambda k=k, bk=bk: nc.tensor.transpose(psb[bk][:, k * 128:(k + 1) * 128], xn[:, k * 128:(k + 1) * 128], ident[:]),
                           r=[xnB, cB], w=[psB[bk]])
                    for k in range(8):
                        if parity == 0:
                            op(ACT, lambda k=k, bk=bk: nc.scalar.activation(out=dstT[:, k, dst_cols], in_=psb[bk][:, k * 128:(k + 1) * 128],
                                                                          func=AF.Identity, scale=scaleT[:, k:k + 1], bias=biasT[:, k:k + 1]),
                               r=[psB[bk], modB], w=[dstB])
                        else:
                            op(DVE, lambda k=k, bk=bk: nc.vector.tensor_scalar(dstT[:, k, dst_cols], psb[bk][:, k * 128:(k + 1) * 128],
                                                                             scaleT[:, k:k + 1], biasT[:, k:k + 1], op0=ALU.mult, op1=ALU.add),
                               r=[psB[bk], modB], w=[dstB])
                return part_b

            with ExitStack() as p1:
                xt = [sb(p1, "xt%d" % i, [128, 1024], F32) for i in range(3)]
                xtB = [Buf() for _ in range(3)]
                xn = [sb(p1, "xn%d" % i, [128, 1024], BF16) for i in range(2)]
                xnB = [Buf(), Buf()]
                junk = sb(p1, "junk", [128, 1024], BF16)
                junkB = Buf()
                ss = sb(p1, "ss", [128, NT], F32)
                rstd = sb(p1, "rstd", [128, NT], F32)
                stB = Buf()
                pend_b = None
                for tt in range(NT):
                    i = tt % 3
                    dma(QS, xt[i][:], x_d[tt * 128:(tt + 1) * 128, :], w=[xtB[i]])
                    pb = norm_to_T(stB, xt[i][:], xtB[i], ss, rstd, tt, xn[tt % 2], xnB[tt % 2], junk, junkB, hT, hTB[tt],
                                   slice(tt * 128, (tt + 1) * 128), scale1, sh1, 1, xn_on_act=True)
                    if pend_b is not None:
                        pend_b()
                    pend_b = pb
                pend_b()
                if DEBUG:
                    dma(QS, dbg["hT"], hT[:], r=hTB)
                cx.barrier()
            p0.close()
            if STOP_AFTER <= 1:
                return nc, dbg

            def mlstm_gen(p4):
                LP = sb(p4, "LP", [128, 2, NT, 4], F32)
                IA = sb(p4, "IA", [128, 2, NT, 4], F32)
                Aa = sb(p4, "Aa", [128, 2, NT, 4], F32)
                EBs = sb(p4, "EBs", [128, 2, NT, 4], F32)
                EG = sb(p4, "EG", [128, 2, NT, 4], F32)
                gB = Buf()
                convw = sb(p4, "convw", [128, 8, 5], F32)
                convb = sb(p4, "convb", [128, 8], F32)
                mng = sb(p4, "mng", [128, 4], F32)
                dma(QS, convw[:], convw_d, w=[cB])
                dma(QS, convb[:], convb_d, w=[cB])
                dma(QS, mng[:], mng_d, w=[cB])
                for d in range(2):
                    fc = 4 + 8 * d
                    op(ACT, lambda d=d, fc=fc: nc.scalar.activation(out=LP[:, d], in_=GT[:, :, fc:fc + 4], func=AF.Exp, scale=-1.0), r=[GTB], w=[gB])
                op(ACT, lambda: nc.scalar.activation(out=LP[:], in_=LP[:], func=AF.Ln, bias=1.0), r=[gB], w=[gB])
                for d in range(2):
                    ic = 8 * d
                    bk = nbank()
                    tri_ = triu if d == 0 else tril
                    op(PE, lambda: nc.tensor.matmul(ps[bk][:, 0:128], tri_[:], LP[:, d].rearrange("p c h -> p (c h)"), start=True, stop=True),
                       r=[gB, cB], w=[psB[bk]])
                    op(DVE, lambda: nc.vector.tensor_tensor(out=IA[:, d], in0=ps[bk][:, 0:128].rearrange("p (c h) -> p c h", h=4),
                                                            in1=GT[:, :, ic:ic + 4], op=ALU.add), r=[psB[bk], GTB], w=[gB])
                    op(ACT, lambda: nc.scalar.activation(out=EBs[:, d], in_=ps[bk][:, 0:128].rearrange("p (c h) -> p c h", h=4),
                                                         func=AF.Exp, scale=-1.0, bias=lns[:, 0:1]), r=[psB[bk], cB], w=[gB])
                bg = nbank()
                op(PE, lambda: nc.tensor.matmul(ps[bg][:, 0:256], ones_f[:], LP[:].rearrange("p d c h -> p (d c h)"), start=True, stop=True),
                   r=[gB, cB], w=[psB[bg]])
                op(ACT, lambda: nc.scalar.activation(out=EG[:].rearrange("p d c h -> p (d c h)"), in_=ps[bg][:, 0:256], func=AF.Exp, scale=-1.0),
                   r=[psB[bg]], w=[gB])
                op(ACT, lambda: nc.scalar.activation(out=Aa[:], in_=IA[:], func=AF.Exp), r=[gB], w=[gB])
                yield

                wm_ = [sb(p4, "wm%d" % i, [128, 8, 4, 128], BF16) for i in range(2)]
                wmB_ = [Buf(), Buf()]
                qkraw_ = [sb(p4, "qkraw%d" % i, [128, 2, NT * 129], BF16) for i in range(2)]
                qkpB_ = [[Buf(), Buf()], [Buf(), Buf()]]
                QKm = sb(p4, "QKm", [128, 2, S], BF16)
                QKB = [Buf(), Buf()]
                sigo_ = [sb(p4, "sigo%d" % i, [128, S], BF16) for i in range(2)]
                sigB_ = [Buf(), Buf()]
                Ktok = sb(p4, "Ktok", [128, NT, 128], BF16)
                KtB = Buf()
                Vm_ = [sb(p4, "Vm%d" % i, [128, NT, 129], BF16) for i in range(2)]
                VmB_ = [Buf(), Buf()]
                Cst = sb(p4, "Cst", [128, 2, NT, 129], BF16)
                CsB = [Buf(), Buf()]
                Pch = sb(p4, "Pch", [128, 2, 2, 129], F32)
                PcB = [[Buf(), Buf()], [Buf(), Buf()]]
                Dg = sb(p4, "Dg", [128, 2, 5, 128], BF16)
                DgB = Buf()
                Sm2 = [sb(p4, "Sm2%d" % i, [128, 2, 128], BF16) for i in range(2)]
                Sm = [[Sm2[i][:, d, :] for i in range(2)] for d in range(2)]
                SmB_ = [Buf(), Buf()]
                SmB = [[SmB_[0], SmB_[1]], [SmB_[0], SmB_[1]]]
                FG = 16
                Hraw = sb(p4, "Hraw", [128, FG, 2, 129], F32)
                HrB = Buf()
                hf = sb(p4, "hf", [128, FG, 128], F32)
                hb = sb(p4, "hb", [128, FG, 128], F32)
                rr = sb(p4, "rr", [128, FG, 2], F32)
                ssh = sb(p4, "ssh", [128, FG], F32)
                hn = sb(p4, "hn", [128, FG, 128], BF16)
                fB = Buf()
                hnB = Buf()
                y4 = [sb(p4, "y4%d" % i, [128, FG * 128], BF16) for i in range(2)]
                y4B = [Buf(), Buf()]
                hblk = [sb(p4, "hblk%d" % i, [128, 8, 512], BF16) for i in range(2)]
                hblkB = [Buf(), Buf()]
                hcnt = [0]
                for i_ in range(2):
                    op(DVE, lambda i_=i_: nc.vector.memset(Vm_[i_][:, :, 128:129], 1.0), w=[VmB_[i_]])
                op(DVE, lambda: nc.vector.memset(Cst[:, 0, 0, :], 0.0), w=[CsB[0]])
                op(DVE, lambda: nc.vector.memset(Cst[:, 1, NT - 1, :], 0.0), w=[CsB[1]])

                def proj_units(j):
                    sl_ = j % 2
                    wm, wmB = wm_[sl_], wmB_[sl_]
                    qkpre, qkpB = qkraw_[sl_][:, :, 0:S + 4], qkpB_[sl_]
                    sigo, sigB = sigo_[sl_], sigB_[sl_]
                    Vm, VmB = Vm_[sl_], VmB_[sl_]
                    cols = (768 + j * 128, 1280 + j * 128, 1792 + j * 128, 2304 + j * 128)
                    for wi, c0 in enumerate(cols):
                        dma(QP, wm[:, :, wi, :], win_d[:, :, c0:c0 + 128], w=[wmB])
                    op(DVE, lambda: nc.vector.memset(qkpre[:, :, 0:2], 0.0), w=qkpB)
                    op(DVE, lambda: nc.vector.memset(qkpre[:, :, S + 2:S + 4], 0.0), w=qkpB)
                    for tb in range(NB):
                        hb_ = hblk[hcnt[0] % 2]
                        hbB = hblkB[hcnt[0] % 2]
                        hcnt[0] += 1
                        dma(QS, hb_[:], hT_s[tb], r=[hTsB], w=[hbB])
                        for w2 in range(3):
                            wi = (0, 1, 3)[w2]
                            bk = nbank()
                            for k in range(8):
                                op(PE, lambda k=k: nc.tensor.matmul(ps[bk][:, :], wm[:, k, wi, :], hb_[:, k, :], start=(k == 0), stop=(k == 7)),
                                   r=[wmB, hbB], w=[psB[bk]])
                            if w2 < 2:
                                op(ACT, lambda: nc.scalar.copy(out=qkpre[:, w2, 2 + tb * 512:2 + (tb + 1) * 512], in_=ps[bk][:, :]),
                                   r=[psB[bk]], w=[qkpB[w2]])
                            else:
                                op(ACT, lambda: nc.scalar.activation(out=sigo[:, tb * 512:(tb + 1) * 512], in_=ps[bk][:, :], func=AF.Sigmoid),
                                   r=[psB[bk]], w=[sigB])
                            yield
                        bk = nbank()
                        for ti in range(4):
                            for k in range(8):
                                op(PE, lambda k=k, ti=ti: nc.tensor.matmul(ps[bk][:, ti * 128:(ti + 1) * 128], hb_[:, k, ti * 128:(ti + 1) * 128],
                                                                           wm[:, k, 2, :], start=(k == 0), stop=(k == 7)),
                                   r=[wmB, hbB], w=[psB[bk]])
                        op(DVE, lambda: nc.vector.tensor_copy(Vm[:, tb * 4:tb * 4 + 4, 0:128], ps[bk][:, :].rearrange("p (t d) -> p t d", t=4)),
                           r=[psB[bk]], w=[VmB])
                        yield

                fill = proj_units(0)
                for _ in fill:
                    yield
                for j in range(4):
                    sl_ = j % 2
                    qkraw = qkraw_[sl_]
                    qkpre, qkpB = qkraw[:, :, 0:S + 4], qkpB_[sl_]
                    sigo, sigB = sigo_[sl_], sigB_[sl_]
                    Vm, VmB = Vm_[sl_], VmB_[sl_]
                    Vt = qkraw[:].rearrange("p d (c v) -> p d c v", v=129)
                    VtB = qkpB
                    for w2 in range(2):
                        ch = j + 4 * w2
                        for k5 in range(5):
                            op(DVE, lambda w2=w2, ch=ch, k5=k5: nc.vector.tensor_scalar(Dg[:, w2, k5, :], ident[:], convw[:, ch, k5:k5 + 1], None, op0=ALU.mult),
                               r=[cB], w=[DgB])
                    for w2 in range(2):
                        ch = j + 4 * w2
                        for tb in range(NB):
                            bk = nbank()
                            for k5 in range(5):
                                op(PE, lambda k5=k5, bk=bk, w2=w2, tb=tb: nc.tensor.matmul(ps[bk][:, :], Dg[:, w2, k5, :],
                                                                                          qkpre[:, w2, tb * 512 + k5:tb * 512 + k5 + 512],
                                                                                          start=(k5 == 0), stop=(k5 == 4)),
                                   r=[DgB, qkpB[w2]], w=[psB[bk]])
                            op(ACT, lambda bk=bk, w2=w2, tb=tb, ch=ch: nc.scalar.activation(out=QKm[:, w2, tb * 512:(tb + 1) * 512], in_=ps[bk][:, :],
                                                                                           func=AF.Silu, bias=convb[:, ch:ch + 1]),
                               r=[psB[bk], cB], w=[QKB[w2]])
                            yield
                    for c8 in range(NT // 8):
                        bk = nbank()
                        for ci in range(8):
                            c = c8 * 8 + ci
                            op(PE, lambda bk=bk, ci=ci, c=c: nc.tensor.transpose(psb[bk][:, ci * 128:(ci + 1) * 128], QKm[:, 1, c * 128:(c + 1) * 128], ident[:]),
                               r=[QKB[1], cB], w=[psB[bk]])
                        op(DVE, lambda bk=bk, c8=c8: nc.vector.tensor_copy(Ktok[:, c8 * 8:c8 * 8 + 8, :], psb[bk][:, :].rearrange("p (c d) -> p c d", c=8)),
                           r=[psB[bk]], w=[KtB])
                        yield
                    for d in range(2):
                        op(DVE, lambda d=d: nc.vector.tensor_tensor(out=Vt[:, d], in0=Vm[:], in1=Aa[:, d, :, j].unsqueeze(2).broadcast_to([128, NT, 129]), op=ALU.mult),
                           r=[VmB, gB], w=[VtB[d]])
                    fill = proj_units(j + 1) if j < 3 else iter(())
                    for _ in range(5):
                        next(fill, None)
                    for stp in range(NT - 1):
                        yield
                        next(fill, None)
                        for d in range(2):
                            c = stp if d == 0 else NT - 1 - stp
                            cn = c + 1 if d == 0 else c - 1
                            bk = nbank()
                            op(PE, lambda bk=bk, c=c, d=d: nc.tensor.matmul(ps[bk][:, 0:129], Ktok[:, c, :], Vt[:, d, c, :], start=True, stop=True),
                               r=[KtB, VtB[d]], w=[psB[bk]])
                            cur, prv = stp % 2, (stp + 1) % 2
                            if stp == 0:
                                op(DVE, lambda bk=bk, d=d, cur=cur: nc.vector.tensor_copy(Pch[:, d, cur, :], ps[bk][:, 0:129]), r=[psB[bk]], w=[PcB[d][cur]])
                            else:
                                cp = c - 1 if d == 0 else c + 1
                                op(DVE, lambda bk=bk, d=d, cur=cur, prv=prv, cp=cp: nc.vector.scalar_tensor_tensor(
                                    out=Pch[:, d, cur, :], in0=Pch[:, d, prv, :], scalar=EG[:, d, cp, j:j + 1], in1=ps[bk][:, 0:129],
                                    op0=ALU.mult, op1=ALU.add), r=[PcB[d][prv], psB[bk], gB], w=[PcB[d][cur]])
                            op(ACT, lambda d=d, cur=cur, c=c, cn=cn: nc.scalar.activation(out=Cst[:, d, cn, :], in_=Pch[:, d, cur, :], func=AF.Identity,
                                                                                         scale=EG[:, d, c, j:j + 1]),
                               r=[PcB[d][cur], gB], w=[CsB[d]])
                    def emit_s(c):
                        bs_ = nbank()
                        cs_ = slice(c * 128, (c + 1) * 128)
                        op(PE, lambda: nc.tensor.matmul(ps[bs_][:, 0:128], QKm[:, 1, cs_], QKm[:, 0, cs_], start=True, stop=True),
                           r=[QKB[0], QKB[1]], w=[psB[bs_]])
                        return bs_

                    look = len(cx.banks) == 8
                    bs_next = emit_s(0) if look else None
                    for c in range(NT):
                        yield
                        next(fill, None)
                        csl = slice(c * 128, (c + 1) * 128)
                        i = c % 2
                        if look:
                            bs = bs_next
                            if c + 1 < NT:
                                bs_next = emit_s(c + 1)
                        else:
                            bs = emit_s(c)
                        op(DVE, lambda bs=bs: nc.vector.tensor_tensor(out=Sm2[i][:], in0=ps[bs][:, 0:128].unsqueeze(1).broadcast_to([128, 2, 128]),
                                                                      in1=mask2[:], op=ALU.mult),
                           r=[psB[bs], cB], w=[SmB_[i]])
                        for d in range(2):
                            bn = nbank()
                            op(PE, lambda d=d, bn=bn: nc.tensor.matmul(ps[bn][:, 0:129], Sm[d][i], Vt[:, d, c, :], start=True, stop=False),
                               r=[SmB[d][i], VtB[d]], w=[psB[bn]])
                            op(PE, lambda d=d, bn=bn: nc.tensor.matmul(ps[bn][:, 0:129], QKm[:, 0, csl], Cst[:, d, c, :], start=False, stop=True),
                               r=[QKB[0], CsB[d]], w=[psB[bn]])
                            op(ACT, lambda d=d, bn=bn: nc.scalar.copy(out=Hraw[:, c % FG, d, :], in_=ps[bn][:, 0:129]), r=[psB[bn]], w=[HrB])
                        if c % FG == FG - 1:
                            c0 = c - (FG - 1)
                            ebv = EBs[:, :, c0:c0 + FG, j].rearrange("p d c -> p c d")
                            op(ACT, lambda: nc.scalar.activation(out=rr[:], in_=Hraw[:, :, :, 128], func=AF.Abs), r=[HrB], w=[fB])
                            op(DVE, lambda ebv=ebv: nc.vector.tensor_tensor(out=rr[:], in0=rr[:], in1=ebv, op=ALU.mult), r=[fB, gB], w=[fB])
                            op(DVE, lambda: nc.vector.tensor_scalar(rr[:], rr[:], 1.0, None, op0=ALU.max), r=[fB], w=[fB])
                            op(DVE, lambda: nc.vector.reciprocal(rr[:], rr[:]), r=[fB], w=[fB])
                            op(DVE, lambda ebv=ebv: nc.vector.tensor_tensor(out=rr[:], in0=rr[:], in1=ebv, op=ALU.mult), r=[fB, gB], w=[fB])
                            op(DVE, lambda: nc.vector.tensor_tensor(out=hf[:], in0=Hraw[:, :, 0, 0:128], in1=rr[:, :, 0:1].broadcast_to([128, FG, 128]), op=ALU.mult),
                               r=[HrB, fB], w=[fB])
                            op(DVE, lambda: nc.vector.tensor_tensor(out=hb[:], in0=Hraw[:, :, 1, 0:128], in1=rr[:, :, 1:2].broadcast_to([128, FG, 128]), op=ALU.mult),
                               r=[HrB, fB], w=[fB])
                            op(DVE, lambda: nc.vector.tensor_tensor(out=hf[:], in0=hf[:], in1=hb[:], op=ALU.add), r=[fB], w=[fB])
                            op(DVE, lambda: nc.vector.tensor_tensor(out=hb[:], in0=hf[:], in1=hf[:], op=ALU.mult), r=[fB], w=[fB])
                            op(DVE, lambda: nc.vector.tensor_reduce(out=ssh[:], in_=hb[:], axis=AX.X, op=ALU.add), r=[fB], w=[fB])
                            op(ACT, lambda: nc.scalar.activation(out=ssh[:], in_=ssh[:], func=AF.Ln, scale=1.0 / 128, bias=epsT[:, 0:1]), r=[fB, cB], w=[fB])
                            op(ACT, lambda: nc.scalar.activation(out=ssh[:], in_=ssh[:], func=AF.Exp, scale=-0.5), r=[fB], w=[fB])
                            op(DVE, lambda: nc.vector.tensor_tensor(out=hn[:], in0=hf[:], in1=ssh[:].unsqueeze(2).broadcast_to([128, FG, 128]), op=ALU.mult),
                               r=[fB], w=[hnB])
                            yi = (c // FG) % 2
                            for c8 in range(FG // 8):
                                bt = nbank()
                                for ci in range(8):
                                    op(PE, lambda bt=bt, ci=ci, c8=c8: nc.tensor.transpose(psb[bt][:, ci * 128:(ci + 1) * 128], hn[:, c8 * 8 + ci, :], ident[:]),
                                       r=[hnB, cB], w=[psB[bt]])
                                op(DVE, lambda bt=bt, c8=c8: nc.vector.scalar_tensor_tensor(out=y4[yi][:, c8 * 1024:(c8 + 1) * 1024], in0=psb[bt][:, :],
                                                                                           scalar=mng[:, j:j + 1], in1=sigo[:, (c0 + c8 * 8) * 128:(c0 + c8 * 8 + 8) * 128],
                                                                                           op0=ALU.mult, op1=ALU.mult),
                                   r=[psB[bt], cB, sigB], w=[y4B[yi]])
                            dma(QP, mix_s[512 + j * 128:512 + (j + 1) * 128, c0 * 128:(c0 + FG) * 128], y4[yi][:, :], r=[y4B[yi]], w=[mixB])
                    for _ in fill:
                        yield
                yield

            with ExitStack() as pa:
                with ExitStack() as p2:
                    watt = sb(p2, "watt", [128, 8, 784], BF16)
                    wattB = Buf()
                    cos_t = sb(p2, "cos_t", [128, NT, 32], F32)
                    sin_t = sb(p2, "sin_t", [128, NT, 32], F32)
                    g10 = sb(p2, "g10", [128, 640], F32)
                    gateb = sb(p2, "gateb", [128, 16], F32)
                    qkf = [sb(p2, "qkf%d" % i, [128, 640], F32) for i in range(3)]
                    qkfB = [Buf(), Buf(), Buf()]
                    sqb_ = [sb(p2, "sqb%d" % i, [128, 640], F32) for i in range(3)]
                    qg_ = [sb(p2, "qg%d" % i, [128, 640], F32) for i in range(3)]
                    tcs_ = [sb(p2, "tcs%d" % i, [128, 640], F32) for i in range(3)]
                    tsn_ = [sb(p2, "tsn%d" % i, [128, 640], F32) for i in range(3)]
                    ro_ = [sb(p2, "ro%d" % i, [128, 640], F32) for i in range(3)]
                    ss10_ = [sb(p2, "ss10%d" % i, [128, 10], F32) for i in range(3)]
                    wkB_ = [Buf(), Buf(), Buf()]
                    ssB_ = [Buf(), Buf(), Buf()]
                    qn = [sb(p2, "qn%d" % i, [128, 640], BF16) for i in range(3)]
                    qnB = [Buf(), Buf(), Buf()]
                    kd = [sb(p2, "kd%d" % i, [128, 256], BF16) for i in range(3)]
                    kdB = [Buf(), Buf(), Buf()]
                    dma(QP, watt[:, :, 0:768], win_d[:, :, 0:768], w=[wattB])
                    dma(QP, watt[:, :, 768:784], win_d[:, :, 2816:2832], w=[wattB])
                    dma(QS, cos_t[:], cos_d, w=[cB])
                    dma(QS, sin_t[:], sin_d, w=[cB])
                    dma(QS, g10[:], g10_d, w=[cB])
                    dma(QS, gateb[:], gateb_d, w=[cB])
                    for tb in range(NB):
                        dma(QS, hT_s[tb], hT[:, :, tb * 512:(tb + 1) * 512], r=hTB[tb * 4:tb * 4 + 4], w=[hTsB])
                    op(DVE, lambda: nc.vector.memset(Vaug[:, :, :, 64:65], 1.0), w=[VB])

                    def v5(t):
                        return t[:].rearrange("p (h a b c) -> p h a b c", h=10, a=2, b=2, c=16)

                    pend2 = []
                    for tt in range(NT):
                        i = tt % 3
                        tsl = slice(tt * 128, (tt + 1) * 128)
                        sqb, qg, tcs, tsn, ro, ss10, wkB = sqb_[i], qg_[i], tcs_[i], tsn_[i], ro_[i], ss10_[i], wkB_[i]
                        bq = nbank()
                        for k in range(8):
                            op(PE, lambda k=k, bq=bq: nc.tensor.matmul(ps[bq][:, :], hT[:, k, tsl], watt[:, k, 0:512],
                                                                     start=(k == 0), stop=(k == 7)),
                               r=[hTB[tt], wattB], w=[psB[bq]])
                        bkv = nbank()
                        for k in range(8):
                            op(PE, lambda k=k, bkv=bkv: nc.tensor.matmul(ps[bkv][:, 0:272], hT[:, k, tsl], watt[:, k, 512:784],
                                                                       start=(k == 0), stop=(k == 7)),
                               r=[hTB[tt], wattB], w=[psB[bkv]])
                        op(ACT, lambda: nc.scalar.copy(out=qkf[i][:, 0:512], in_=ps[bq][:, :]), r=[psB[bq]], w=[qkfB[i]])
                        op(ACT, lambda: nc.scalar.copy(out=qkf[i][:, 512:640], in_=ps[bkv][:, 0:128]), r=[psB[bkv]], w=[qkfB[i]])
                        op(DVE, lambda: nc.vector.tensor_copy(Vaug[:, tt, :, 0:64], ps[bkv][:, 128:256].rearrange("p (g d) -> p g d", g=2)),
                           r=[psB[bkv]], w=[VB])
                        op(DVE, lambda: nc.vector.tensor_tensor(out=GT[:, tt, :], in0=ps[bkv][:, 256:272], in1=gateb[:], op=ALU.add),
                           r=[psB[bkv], cB], w=[GTB])
                        for hh in range(10):
                            op(ACT, lambda hh=hh: nc.scalar.activation(out=sqb[:, hh * 64:(hh + 1) * 64], in_=qkf[i][:, hh * 64:(hh + 1) * 64], func=AF.Square,
                                                                       accum_out=ss10[:, hh:hh + 1]), r=[qkfB[i]], w=[ssB_[i]])
                        op(ACT, lambda: nc.scalar.activation(out=ss10[:], in_=ss10[:], func=AF.Ln, scale=1.0 / 64, bias=epsT[:, 0:1]), r=[ssB_[i], cB], w=[ssB_[i]])
                        op(ACT, lambda: nc.scalar.activation(out=ss10[:], in_=ss10[:], func=AF.Exp, scale=-0.5), r=[ssB_[i]], w=[ssB_[i]])
                        op(DVE, lambda: nc.vector.tensor_tensor(out=qg[:], in0=qkf[i][:], in1=g10[:], op=ALU.mult), r=[qkfB[i], cB], w=[wkB])
                        for a in range(2):
                            cosa = cos_t[:, tt, a * 16:(a + 1) * 16].unsqueeze(1).unsqueeze(1).broadcast_to([128, 10, 2, 16])
                            sina = sin_t[:, tt, a * 16:(a + 1) * 16].unsqueeze(1).unsqueeze(1).broadcast_to([128, 10, 2, 16])
                            op(DVE, lambda a=a, cosa=cosa: nc.vector.tensor_tensor(out=v5(tcs)[:, :, a], in0=v5(qg)[:, :, a], in1=cosa, op=ALU.mult), r=[wkB, cB], w=[wkB])
                            op(DVE, lambda a=a, sina=sina: nc.vector.tensor_tensor(out=v5(tsn)[:, :, a], in0=v5(qg)[:, :, a], in1=sina, op=ALU.mult), r=[wkB, cB], w=[wkB])
                        op(DVE, lambda: nc.vector.tensor_tensor(out=v5(ro)[:, :, :, 0, :], in0=v5(tcs)[:, :, :, 0, :], in1=v5(tsn)[:, :, :, 1, :],
                                                                op=ALU.subtract), r=[wkB], w=[wkB])
                        op(DVE, lambda: nc.vector.tensor_tensor(out=v5(ro)[:, :, :, 1, :], in0=v5(tcs)[:, :, :, 1, :], in1=v5(tsn)[:, :, :, 0, :],
                                                                op=ALU.add), r=[wkB], w=[wkB])
                        op(DVE, lambda: nc.vector.tensor_tensor(out=qn[i][:].rearrange("p (h d) -> p h d", h=10),
                                                                in0=ro[:].rearrange("p (h d) -> p h d", h=10),
                                                                in1=ss10[:].unsqueeze(2).broadcast_to([128, 10, 64]), op=ALU.mult),
                           r=[wkB, ssB_[i]], w=[qnB[i]])
                        op(DVE, lambda: nc.vector.tensor_copy(kd[i][:].rearrange("p (g u d) -> p g u d", g=2, u=2),
                                                              qn[i][:, 512:640].rearrange("p (g d) -> p g d", g=2).unsqueeze(2).broadcast_to([128, 2, 2, 64])),
                           r=[qnB[i]], w=[kdB[i]])

                        def stage_b(i=i, tsl=tsl):
                            bt = nbank()
                            for pr in range(4):
                                op(PE, lambda pr=pr, bt=bt: nc.tensor.transpose(psb[bt][:, pr * 128:(pr + 1) * 128], qn[i][:, pr * 128:(pr + 1) * 128], ident[:]),
                                   r=[qnB[i], cB], w=[psB[bt]])
                            for g in range(2):
                                op(PE, lambda g=g, bt=bt: nc.tensor.transpose(psb[bt][:, 512 + g * 128:512 + (g + 1) * 128], kd[i][:, g * 128:(g + 1) * 128], ident[:]),
                                   r=[kdB[i], cB], w=[psB[bt]])
                            op(ACT, lambda: nc.scalar.copy(out=QT[:, :, tsl], in_=psb[bt][:, 0:512].rearrange("p (h t) -> p h t", h=4)),
                               r=[psB[bt]], w=[QTB])
                            op(ACT, lambda: nc.scalar.copy(out=KT[:, :, tsl], in_=psb[bt][:, 512:768].rearrange("p (h t) -> p h t", h=2)),
                               r=[psB[bt]], w=[KTB])

                        pend2.append(stage_b)
                        if len(pend2) > 2:
                            pend2.pop(0)()
                    for fb in pend2:
                        fb()
                    if DEBUG:
                        dma(QS, dbg["QT"], QT[:], r=[QTB])
                        dma(QS, dbg["KT"], KT[:], r=[KTB])
                    cx.barrier()
                hs.close()
                if STOP_AFTER <= 2:
                    return nc, dbg

                with ExitStack() as p3:
                    wa2 = [sb(p3, "wa2%d" % i, [128, 8, 512], BF16) for i in range(8)]
                    wa2B = [Buf() for _ in range(8)]
                    NPT = 3
                    PT = [sb(p3, "PT%d" % i, [128, 1024], BF16) for i in range(NPT)]
                    PTB = [Buf() for _ in range(NPT)]
                    if not PACKPV:
                        rrow = [[sb(p3, "rrow%d%d" % (i, e), [128, 512], F32) for e in range(2)] for i in range(1)] * 2
                        Osb = [[sb(p3, "Osb%d%d" % (i, e), [65, 512], F32) for e in range(2)] for i in range(1)] * 2
                        yst = [[sb(p3, "yst%d%d" % (i, e), [64, 512], BF16) for e in range(2)] for i in range(1)] * 2
                    rrB = [[Buf(), Buf()]] * 2
                    OsB = [[Buf(), Buf()]] * 2
                    ysB = [[Buf(), Buf()]] * 2
                    if PACKPV:
                        acc = [sb(p3, "acc%d" % i, [128, 1024], F32) for i in range(2)]
                        rinv = sb(p3, "rinv", [128, 512], F32)
                        Osb2 = sb(p3, "Osb2", [128, 512], F32)
                        yst2 = [sb(p3, "yst2%d" % i, [128, 512], BF16) for i in range(2)]
                    accB = [Buf(), Buf()]
                    accPB = [Buf(), Buf()]
                    rinvB, Osb2B, yst2B = Buf(), Buf(), [Buf(), Buf()]
                    OB2 = [Buf(excl=True), Buf(excl=True)]
                    NSL = 2 if INTERLEAVE else 3
                    pairB = [Buf(excl=True) for _ in range(NSL)]
                    OB = [[Buf(excl=True), Buf(excl=True)]] * 2
                    LA = 1 if INTERLEAVE else 2
                    slot_ctr = [0]
                    gslot = {}

                    def next_slot():
                        v = slot_ctr[0] % NSL
                        slot_ctr[0] += 1
                        return v

                    def fin_a(o):
                        if PACKPV:
                            op(ACT, lambda: nc.scalar.copy(out=Osb2[:, :], in_=ps[6 + o % 2][:, :]), r=[OB2[o % 2]], w=[Osb2B])
                            return
                        i = o % 2
                        for e in range(2):
                            ob = 6 + e
                            op(ACT, lambda e=e, ob=ob: nc.scalar.copy(out=Osb[i][e][:, :], in_=ps[ob][0:65, :]), r=[OB[i][e]], w=[OsB[i][e]])
                            op(DVE, lambda e=e: nc.vector.reciprocal(rrow[i][e][64:65, :], Osb[i][e][64:65, :]), r=[OsB[i][e]], w=[rrB[i][e]])

                    def bc_s(o, sl):
                        if PACKPV:
                            for e in range(2):
                                op(PE, lambda e=e: nc.tensor.matmul(ps[2 * sl][e * 64:(e + 1) * 64, :], ones_f[:, 0:64], acc[o % 2][:, e * 512:(e + 1) * 512],
                                                                    start=True, stop=True, tile_position=(0, e * 64)),
                                   r=[accB[o % 2], accPB[o % 2], cB], w=[pairB[sl]])
                            return
                        i = o % 2
                        for e in range(2):
                            op(PE, lambda e=e: nc.tensor.matmul(ps[2 * sl + e][0:64, :], ones_f[64:65, 0:64], rrow[i][e][64:65, :], start=True, stop=True),
                               r=[rrB[i][e], cB], w=[pairB[sl]])

                    def bc_rest(pr, qb, o, sl):
                        if PACKPV:
                            op(DVE, lambda: nc.vector.reciprocal(rinv[:, :], ps[2 * sl][:, :]), r=[pairB[sl]], w=[rinvB])
                            op(DVE, lambda: nc.vector.tensor_tensor(out=yst2[o % 2][:, :], in0=Osb2[:, :], in1=rinv[:, :], op=ALU.mult),
                               r=[Osb2B, rinvB], w=[yst2B[o % 2]])
                            dma(QP, mix_s[pr * 128:(pr + 1) * 128, qb * 512:(qb + 1) * 512], yst2[o % 2][:, :], r=[yst2B[o % 2]], w=[mixB])
                            return
                        i = o % 2
                        for e in range(2):
                            h = 2 * pr + e
                            op(DVE, lambda e=e: nc.vector.tensor_tensor(out=yst[i][e][:, :], in0=Osb[i][e][0:64, :], in1=ps[2 * sl + e][0:64, :], op=ALU.mult),
                               r=[OsB[i][e], pairB[sl]], w=[ysB[i][e]])
                            dma(QP, mix_s[h * 64:(h + 1) * 64, qb * 512:(qb + 1) * 512], yst[i][e][:, :], r=[ysB[i][e]], w=[mixB])

                    for wv in range(40):
                        op(PE, lambda wv=wv: nc.tensor.matmul(ps[6][:, :], ident[:], QT[:, wv % 4, (wv % 8) * 512:(wv % 8 + 1) * 512], start=True, stop=True),
                           r=[cB, QTB], w=[OB[0][0]])
                    groups = [(pr, qb, kt) for pr in range(4) for qb in range(NB) for kt in range(NT)]
                    seq = []
                    due = []
                    for gi, (pr, qb, kt) in enumerate(groups):
                        seq.append(("S", pr, qb, kt, gi // NT))
                        due = [(n - 1, it_) for (n, it_) in due]
                        while due and due[0][0] <= 0:
                            seq.append(due.pop(0)[1])
                        if kt == NT - 1:
                            due.append((8, ("BC", pr, qb, 0, gi // NT)))
                    for _ in range(8):
                        seq.append(("NOP", 0, 0, 0, 0))
                    seq.extend(it_ for (_, it_) in due)
                    N = len(seq)
                    real_idx = 0
                    mdone = True
                    if INTERLEAVE:
                        cx.banks = [4, 5]
                        p4s = ExitStack()
                        mgen = mlstm_gen(p4s)
                        mdone = False
                    for it in range(N + LA):
                        if it % 24 == 8 and it // 24 < 8:
                            n_ = 4 + it // 24
                            dma(QP, wa2[n_ - 4][:], wada_d[n_], w=[wa2B[n_ - 4]])
                        if it % 24 == 8 and 10 <= it // 24 < 14:
                            q4 = it // 24 - 10
                            rs = slice(q4 * 704, (q4 + 1) * 704)
                            dma(QP, wg_s[rs, :], wg_d[rs, :], w=[wsB])
                            dma(QP, wu_s[rs, :], wu_d[rs, :], w=[wsB])
                        if INTERLEAVE and not mdone and it % 2 == 0:
                            try:
                                next(mgen)
                            except StopIteration:
                                mdone = True
                        if it < N:
                            kind, pr, qb, kt, o = seq[it]
                            if kind != "NOP":
                                sl = next_slot()
                                gslot[it] = sl
                            if kind == "S":
                                g = pr // 2
                                for e in range(2):
                                    hp = e * 64
                                    op(PE, lambda e=e, hp=hp: nc.tensor.matmul(ps[2 * sl + e][:, :], KT[hp:hp + 64, g, kt * 128:(kt + 1) * 128],
                                                                               QT[hp:hp + 64, pr, qb * 512:(qb + 1) * 512], start=True, stop=True),
                                       r=[KTB, QTB], w=[pairB[sl]])
                            elif kind == "BC":
                                bc_s(o, sl)
                        if it >= LA:
                            j = it - LA
                            kind, pr, qb, kt, o = seq[j]
                            if kind == "NOP":
                                continue
                            sl = gslot.pop(j)
                            if kind == "BC":
                                bc_rest(pr, qb, o, sl)
                                continue
                            g = pr // 2
                            pi = real_idx % NPT
                            real_idx += 1
                            op(ACT, lambda: nc.scalar.activation(out=PT[pi][:], in_=psall[:, sl * 1024:(sl + 1) * 1024], func=AF.Exp, scale=0.125),
                               r=[pairB[sl]], w=[PTB[pi]])
                            for e in range(2):
                                if PACKPV:
                                    ob = 6 + o % 2
                                    op(PE, lambda e=e, ob=ob: nc.tensor.matmul(ps[ob][e * 64:(e + 1) * 64, :], Vaug[:, kt, g, 0:64], PT[pi][:, e * 512:(e + 1) * 512],
                                                                               start=(kt == 0), stop=(kt == NT - 1), tile_position=(0, e * 64)),
                                       r=[VB, PTB[pi]], w=[OB2[o % 2]])
                                    continue
                                ob = 6 + e
                                op(PE, lambda e=e, ob=ob: nc.tensor.matmul(ps[ob][0:65, :], Vaug[:, kt, g, :], PT[pi][:, e * 512:(e + 1) * 512],
                                                                           start=(kt == 0), stop=(kt == NT - 1)),
                                   r=[VB, PTB[pi]], w=[OB[o % 2][e]])
                            if PACKPV:
                                CS = 704
                                if kt == 0:
                                    op(DVE, lambda: nc.vector.tensor_copy(acc[o % 2][:, 0:CS], PT[pi][:, 0:CS]), r=[PTB[pi]], w=[accB[o % 2]])
                                    op(POOL, lambda: nc.gpsimd.tensor_copy(acc[o % 2][:, CS:1024], PT[pi][:, CS:1024]), r=[PTB[pi]], w=[accPB[o % 2]])
                                else:
                                    op(DVE, lambda: nc.vector.tensor_tensor(out=acc[o % 2][:, 0:CS], in0=acc[o % 2][:, 0:CS], in1=PT[pi][:, 0:CS], op=ALU.add),
                                       r=[PTB[pi], accB[o % 2]], w=[accB[o % 2]])
                                    op(POOL, lambda: nc.gpsimd.tensor_tensor(out=acc[o % 2][:, CS:1024], in0=acc[o % 2][:, CS:1024], in1=PT[pi][:, CS:1024], op=ALU.add),
                                       r=[PTB[pi], accPB[o % 2]], w=[accPB[o % 2]])
                            if kt == NT - 1:
                                fin_a(o)
                    if INTERLEAVE:
                        for _ in mgen:
                            pass
                    cx.banks = list(range(8))
                    cx.barrier()
                    fins = [ada_block(n, wa2[n - 4], wa2B[n - 4], defer=True) for n in range(4, 12)]
                    for fn_ in fins:
                        fn_()
                    phase0b()
                    cx.barrier()
                    if INTERLEAVE:
                        p4s.close()
            qs.close()
            if not INTERLEAVE:
                p4s = ExitStack()
                for _ in mlstm_gen(p4s):
                    pass
                cx.barrier()
                p4s.close()
        with ExitStack() as p5:
            if DEBUG:
                dma(QS, dbg["mix"], mix_s, r=[mixB])
            wo = sb(p5, "wo", [128, 8, 1024], BF16)
            wdT = sb(p5, "wdT", [128, NJ, 1024], BF16)
            woB, wdB = Buf(), Buf()
            for q4 in range(4):
                dma(QP, wo[:, q4 * 2:q4 * 2 + 2, :], wout_d[:, q4 * 2:q4 * 2 + 2, :], w=[woB])
            for q4 in range(11):
                dma(QP, wdT[:, q4 * 2:q4 * 2 + 2, :], wd_d[:, q4 * 2:q4 * 2 + 2, :], w=[wdB])
            g1b = sb(p5, "g1b", [128, 1024], F32)
            g2b = sb(p5, "g2b", [128, 1024], F32)
            fngb = sb(p5, "fngb", [128, 1024], F32)
            dgf = [sb(p5, "dgf%d" % i, [128, 128], F32) for i in range(2)]
            dgB = [Buf(), Buf()]
            dma(QS, fngb[:], fng_d, w=[cB])
            cnt = 0
            for (dst, c0) in ((g1b, 16), (g2b, 40)):
                for k in range(8):
                    di = cnt % 2
                    cnt += 1
                    op(DVE, lambda di=di, k=k, c0=c0: nc.vector.tensor_scalar(dgf[di][:], ident_f[:], modT[:, c0 + k:c0 + k + 1], None, op0=ALU.mult),
                       r=[modB, cB], w=[dgB[di]])
                    bk = nbank()
                    op(PE, lambda di=di, bk=bk: nc.tensor.matmul(ps[bk][:, 0:128], ones_f[:], dgf[di][:], start=True, stop=True),
                       r=[dgB[di], cB], w=[psB[bk]])
                    op(ACT, lambda bk=bk, dst=dst, k=k: nc.scalar.copy(out=dst[:, k * 128:(k + 1) * 128], in_=ps[bk][:, 0:128]),
                       r=[psB[bk]], w=[modB])
            NW = 4
            wgt = [sb(p5, "wgt%d" % i, [128, 1024], BF16) for i in range(NW)]
            wut = [sb(p5, "wut%d" % i, [128, 1024], BF16) for i in range(NW)]
            wgB = [Buf() for _ in range(NW)]
            wuB = [Buf() for _ in range(NW)]
            mb = sb(p5, "mb", [128, 8, 512], BF16)
            mbB = Buf()
            xt = [sb(p5, "xt5%d" % i, [128, 1024], F32) for i in range(2)]
            xtB = [Buf(), Buf()]
            x1 = sb(p5, "x1", [128, 2, 4, 1024], F32)
            x1B = [[Buf() for _ in range(4)] for _ in range(2)]
            xn = [sb(p5, "xn5%d" % i, [128, 1024], BF16) for i in range(4)]
            xnB = [Buf() for _ in range(4)]
            junk = sb(p5, "junk5", [128, 1024], BF16)
            junkB = Buf()
            ss = sb(p5, "ss5", [128, NT], F32)
            rstd = sb(p5, "rstd5", [128, NT], F32)
            ss3 = sb(p5, "ss35", [128, NT], F32)
            stB = Buf()
            st3B = Buf()
            h2T = sb(p5, "h2T", [128, 8, 512], BF16)
            h2B = [Buf() for _ in range(4)]
            act = sb(p5, "act", [128, NJ, 512], BF16)
            actB = [Buf() for _ in range(NJ)]
            sg = [sb(p5, "sg%d" % i, [128, 512], F32) for i in range(2)]
            sgB = [Buf(), Buf()]
            x2t = [sb(p5, "x2t%d" % i, [128, 1024], F32) for i in range(2)]
            x2B = [Buf(), Buf()]
            ot = [sb(p5, "ot%d" % i, [128, 1024], F32) for i in range(2)]
            otB = [Buf(), Buf()]
            mixv = mix_s.rearrange("(k p) t -> p k t", p=128)
            wcnt = [0]

            def pro_a(tb, ti):
                tt = tb * 4 + ti
                i = tt % 2
                xb = tb % 2
                dma(QS, xt[i][:], x_d[tt * 128:(tt + 1) * 128, :], w=[xtB[i]])
                for hf_ in range(2):
                    bk = nbank()
                    for k in range(8):
                        op(PE, lambda k=k: nc.tensor.matmul(ps[bk][:, :], mb[:, k, ti * 128:(ti + 1) * 128],
                                                            wo[:, k, hf_ * 512:(hf_ + 1) * 512], start=(k == 0), stop=(k == 7)),
                           r=[mbB, woB], w=[psB[bk]])
                    hs = slice(hf_ * 512, (hf_ + 1) * 512)
                    op(DVE, lambda: nc.vector.tensor_tensor(out=x1[:, xb, ti, hs], in0=ps[bk][:, :], in1=g1b[:, hs], op=ALU.mult),
                       r=[psB[bk], modB], w=[x1B[xb][ti]])
                op(DVE, lambda: nc.vector.tensor_tensor(out=x1[:, xb, ti, :], in0=x1[:, xb, ti, :], in1=xt[i][:], op=ALU.add),
                   r=[x1B[xb][ti], xtB[i]], w=[x1B[xb][ti]])
                return norm_to_T(stB, x1[:, xb, ti, :], x1B[xb][ti], ss, rstd, tt, xn[ti], xnB[ti], junk, junkB, h2T, h2B[ti],
                                 slice(ti * 128, (ti + 1) * 128), scale2, sh2, tt % 2, xn_on_act=True)

            def phase_a(tb):
                for j in range(NJ):
                    wi = wcnt[0] % NW
                    wcnt[0] += 1
                    dma(QS, wgt[wi][:], wg_s[j * 128:(j + 1) * 128, :], r=[wsB], w=[wgB[wi]])
                    dma(QS, wut[wi][:], wu_s[j * 128:(j + 1) * 128, :], r=[wsB], w=[wuB[wi]])
                    bg_ = nbank()
                    for k in range(8):
                        op(PE, lambda k=k: nc.tensor.matmul(ps[bg_][:, :], wgt[wi][:, k * 128:(k + 1) * 128], h2T[:, k, :],
                                                            start=(k == 0), stop=(k == 7)), r=[wgB[wi]] + h2B, w=[psB[bg_]])
                    bu_ = nbank()
                    for k in range(8):
                        op(PE, lambda k=k: nc.tensor.matmul(ps[bu_][:, :], wut[wi][:, k * 128:(k + 1) * 128], h2T[:, k, :],
                                                            start=(k == 0), stop=(k == 7)), r=[wuB[wi]] + h2B, w=[psB[bu_]])
                    si = j % 2
                    op(ACT, lambda: nc.scalar.activation(out=sg[si][:], in_=ps[bg_][:, :], func=AF.Silu), r=[psB[bg_]], w=[sgB[si]])
                    op(DVE, lambda: nc.vector.tensor_tensor(out=act[:, j, :], in0=sg[si][:], in1=ps[bu_][:, :], op=ALU.mult),
                       r=[sgB[si], psB[bu_]], w=[actB[j]])

            def phase_b(tb, ti):
                tt = tb * 4 + ti
                i = tt % 2
                xb = tb % 2
                for hf_ in range(2):
                    bk = nbank()
                    hs = slice(hf_ * 512, (hf_ + 1) * 512)
                    for j in range(NJ):
                        op(PE, lambda j=j: nc.tensor.matmul(ps[bk][:, :], act[:, j, ti * 128:(ti + 1) * 128], wdT[:, j, hs],
                                                            start=(j == 0), stop=(j == NJ - 1)),
                           r=[actB[j], wdB], w=[psB[bk]])
                    op(DVE, lambda: nc.vector.tensor_tensor(out=x2t[i][:, hs], in0=ps[bk][:, :], in1=g2b[:, hs], op=ALU.mult),
                       r=[psB[bk], modB], w=[x2B[i]])
                op(DVE, lambda: nc.vector.tensor_tensor(out=x2t[i][:], in0=x2t[i][:], in1=x1[:, xb, ti, :], op=ALU.add),
                   r=[x2B[i], x1B[xb][ti]], w=[x2B[i]])
                op(ACT, lambda: nc.scalar.activation(out=junk[:], in_=x2t[i][:], func=AF.Square, accum_out=ss3[:, tt:tt + 1]),
                   r=[x2B[i]], w=[junkB, st3B])
                op(ACT, lambda: nc.scalar.activation(out=ss3[:, tt:tt + 1], in_=ss3[:, tt:tt + 1], func=AF.Ln, scale=1.0 / D, bias=epsT[:, 0:1]), r=[st3B, cB], w=[st3B])
                op(ACT, lambda: nc.scalar.activation(out=ss3[:, tt:tt + 1], in_=ss3[:, tt:tt + 1], func=AF.Exp, scale=-0.5), r=[st3B], w=[st3B])
                op(DVE, lambda: nc.vector.scalar_tensor_tensor(out=ot[i][:], in0=x2t[i][:], scalar=ss3[:, tt:tt + 1], in1=fngb[:],
                                                               op0=ALU.mult, op1=ALU.mult), r=[x2B[i], st3B, cB], w=[otB[i]])
                dma(QP, out_d[tt * 128:(tt + 1) * 128, :], ot[i][:], r=[otB[i]])

            dma(QS, mb[:], mixv[:, :, 0:512], r=[mixB], w=[mbB])
            for pb in [pro_a(0, ti) for ti in range(4)]:
                pb()
            for tb in range(NB):
                phase_a(tb)
                if tb + 1 < NB:
                    dma(QS, mb[:], mixv[:, :, (tb + 1) * 512:(tb + 2) * 512], r=[mixB], w=[mbB])
                for ti in range(4):
                    pb = pro_a(tb + 1, ti) if tb + 1 < NB else None
                    phase_b(tb, ti)
                    if pb is not None:
                        pb()
            cx.barrier()
    return nc, dbg


def _prep_shared(inp):
    f = np.float32
    w_ada = inp["w_ada"][0]
    sh = {}
    sh["w_ada"] = np.ascontiguousarray(w_ada.reshape(8, 128, 12, 512).transpose(2, 1, 0, 3))
    sh["b_adaT"] = np.ascontiguousarray(inp["b_ada"][0].reshape(48, 128).T)
    sh["gmixT"] = np.ascontiguousarray(inp["norm_mix_g"][0].reshape(8, 128).T)
    sh["gffnT"] = np.ascontiguousarray(inp["norm_ffn_g"][0].reshape(8, 128).T)
    sh["w_in"] = np.ascontiguousarray(inp["w_in"][0].reshape(8, 128, 2832).transpose(1, 0, 2))
    g10 = np.concatenate([np.tile(inp["q_norm_g"][0], 8), np.tile(inp["k_norm_g"][0], 2)])
    sh["g10"] = np.ascontiguousarray(np.broadcast_to(g10[None, :], (128, 640))).astype(f)
    tok = np.arange(S)
    row = (tok // 64).astype(f)
    col = (tok % 64).astype(f)
    inv_freq = (f(10000.0) ** (-np.arange(0, 32, 2, dtype=f) / f(32))).astype(f)
    ang_r = (row[:, None] * inv_freq[None, :]).astype(f)
    ang_c = (col[:, None] * inv_freq[None, :]).astype(f)
    cos = np.concatenate([np.cos(ang_r), np.cos(ang_c)], axis=1).astype(f)
    sin = np.concatenate([np.sin(ang_r), np.sin(ang_c)], axis=1).astype(f)
    sh["ropecos"] = np.ascontiguousarray(cos.reshape(NT, 128, 32).transpose(1, 0, 2))
    sh["ropesin"] = np.ascontiguousarray(sin.reshape(NT, 128, 32).transpose(1, 0, 2))
    sh["convwT"] = np.ascontiguousarray(inp["conv_w"][0].reshape(5, 8, 128).transpose(2, 1, 0))
    sh["convbT"] = np.ascontiguousarray(inp["conv_b"][0].reshape(8, 128).T)
    sh["gateb"] = np.ascontiguousarray(np.broadcast_to(inp["gate_b"][0][None, :], (128, 16))).astype(f)
    sh["mngT"] = np.ascontiguousarray(inp["mlstm_norm_g"][0].reshape(4, 128).T)
    sh["w_out"] = np.ascontiguousarray(inp["w_out"][0].reshape(8, 128, 1024).transpose(1, 0, 2))
    for nm in ("w_gate", "w_up"):
        w = inp[nm][0].reshape(8, 128, NJ, 128).transpose(2, 1, 0, 3)
        sh[nm] = np.ascontiguousarray(w.reshape(NJ * 128, 1024))
    sh["w_down"] = np.ascontiguousarray(inp["w_down"][0].reshape(NJ, 128, 1024).transpose(1, 0, 2))
    sh["fngb"] = np.ascontiguousarray(np.broadcast_to(inp["final_norm_g"][None, :], (128, 1024))).astype(f)
    sh["ident"] = np.eye(128, dtype=f)
    sh["triu"] = np.triu(np.ones((128, 128), dtype=f))
    sh["tril"] = np.tril(np.ones((128, 128), dtype=f))
    return sh


def kernel(**inputs):
    inp = {k: np.asarray(v) for k, v in inputs.items()}
    B = inp["x"].shape[0]
    nc, _ = build()
    sh = _prep_shared(inp)
    in_maps = []
    for b in range(B):
        m = dict(sh)
        m["x"] = np.ascontiguousarray(inp["x"][b])
        m["cT"] = np.ascontiguousarray(inp["c"][b].reshape(8, 128).T)
        in_maps.append(m)
    res = run_bass_kernel_spmd(nc, in_maps, core_ids=list(range(B)))
    return np.stack([np.asarray(r["out"]) for r in res.results], axis=0).astype(np.float32)
```
